# Optimizing a Trainium2 kernel written in Bass

```python
import jax
import jax.numpy as jnp
from jax import lax
import numpy as np

D_MODEL = 2048
BATCH = 2
SEQ = 4096
DEPTH = 1
DEC_BATCH = 32
DEC_SEQ = 4
PAST_LEN = 8192
PAGE_SIZE = 128

HEAD_DIM = 128
NSA_HEADS = 8
NSA_GROUPS = 2
NSA_HPG = NSA_HEADS // NSA_GROUPS
CMP_LEN = 32
CMP_STRIDE = 16
CMP_RATIO = CMP_LEN // CMP_STRIDE
SEL_BLOCK = 64
N_SEL = 16
WINDOW = 512
SB_HEADS = 4
MEM_HEADS = 4
MEM_TOKENS = 256
D_FF = 5632
CONV_W = 3
Q_BLOCK = 128
ALPHA = (2.0 * DEPTH) ** 0.25
BETA = (8.0 * DEPTH) ** -0.25

NSA_Q = NSA_HEADS * HEAD_DIM
KVG = NSA_GROUPS * HEAD_DIM
SB_W = SB_HEADS * HEAD_DIM
MEM_W = MEM_HEADS * HEAD_DIM
KV_CH = 4 * KVG + 2 * SB_W
WIN_CH = 2 * KVG
MEM_KV_CH = 2 * MEM_W
OFF_KV = NSA_Q
OFF_WIN = OFF_KV + KV_CH
OFF_QSB = OFF_WIN + WIN_CH
OFF_QMEM = OFF_QSB + SB_W
OFF_GNSA = OFF_QMEM + MEM_W
OFF_GMERGE = OFF_GNSA + 3 * NSA_HEADS
PROJ_W = OFF_GMERGE + 3 * D_MODEL

kernel_name = 'nsa_stickbreak_memory_deepnorm_decoder_step'


def _layer_norm(x, g, b, eps=1e-5):
    xf = x.astype(jnp.float32)
    mu = jnp.mean(xf, axis=-1, keepdims=True)
    var = jnp.mean(jnp.square(xf - mu), axis=-1, keepdims=True)
    return ((xf - mu) * lax.rsqrt(var + eps) * g + b).astype(x.dtype)


def _masked_softmax(s, mask):
    s = jnp.where(mask, s, -1e30)
    m = jnp.max(s, axis=-1, keepdims=True)
    e = jnp.where(mask, jnp.exp(s - m), 0.0)
    return e / jnp.maximum(jnp.sum(e, axis=-1, keepdims=True), 1e-30)


def _alibi_slopes(n):
    return jnp.exp2(-8.0 * jnp.arange(1, n + 1, dtype=jnp.float32) / n)


def _query_blocks(t):
    qb = Q_BLOCK if t % Q_BLOCK == 0 else t
    return t // qb, qb


def _to_blocks(a, axis, nb):
    s = a.shape
    a = a.reshape(s[:axis] + (nb, s[axis] // nb) + s[axis + 1:])
    return jnp.moveaxis(a, axis, 0)


def _from_blocks(o):
    o = jnp.moveaxis(o, 0, 1)
    return o.reshape((o.shape[0], o.shape[1] * o.shape[2]) + o.shape[3:])


def _compress(k, w1, pe, w2):
    B, L, G, D = k.shape
    n_sub = L // CMP_STRIDE
    n_cmp = n_sub - CMP_RATIO + 1
    kb = k[:, :n_sub * CMP_STRIDE].reshape(B, n_sub, CMP_STRIDE, G, D)
    w1b = w1.reshape(CMP_RATIO, CMP_STRIDE, D, D)
    peb = pe.reshape(CMP_RATIO, CMP_STRIDE, D)
    hid = jnp.einsum('bnsgd,sde->bnge', kb[:, 0:n_cmp] + peb[0][:, None, :], w1b[0])
    for r in range(1, CMP_RATIO):
        hid = hid + jnp.einsum('bnsgd,sde->bnge', kb[:, r:r + n_cmp] + peb[r][:, None, :], w1b[r])
    return jax.nn.gelu(hid) @ w2


def _nsa(q, kv_all, win_all, gates, q_pos, win_pos, p):
    B, T = q.shape[:2]
    L = kv_all.shape[1]
    G, D = NSA_GROUPS, HEAD_DIM
    scale = D ** -0.5
    slopes = _alibi_slopes(NSA_HEADS).reshape(G, NSA_HPG)
    kc = kv_all[..., 0 * KVG:1 * KVG].reshape(B, L, G, D)
    vc = kv_all[..., 1 * KVG:2 * KVG].reshape(B, L, G, D)
    ks = kv_all[..., 2 * KVG:3 * KVG].reshape(B, L, G, D)
    vs = kv_all[..., 3 * KVG:4 * KVG].reshape(B, L, G, D)

    k_cmp = _compress(kc, p['w_cmp_k1'], p['pe_cmp_k'], p['w_cmp_k2'])
    v_cmp = _compress(vc, p['w_cmp_v1'], p['pe_cmp_v'], p['w_cmp_v2'])
    n_cmp = k_cmp.shape[1]
    cmp_start = jnp.arange(n_cmp) * CMP_STRIDE
    cmp_end = cmp_start + CMP_LEN - 1
    dist_c = q_pos[:, None] - cmp_end[None, :]
    s_c = (jnp.einsum('btghd,bngd->bghtn', q, k_cmp).astype(jnp.float32) * scale
           - slopes[:, :, None, None] * dist_c.astype(jnp.float32))
    p_cmp = _masked_softmax(s_c, dist_c >= 0)
    o_cmp = jnp.einsum('bghtn,bngd->btghd', p_cmp.astype(v_cmp.dtype), v_cmp)

    n_blk = -(-L // SEL_BLOCK)
    n_top = min(N_SEL, n_blk)
    blk = jnp.arange(n_blk)
    overlap = ((cmp_start[:, None] < (blk[None, :] + 1) * SEL_BLOCK)
               & (cmp_start[:, None] + CMP_LEN > blk[None, :] * SEL_BLOCK))
    imp = jnp.einsum('bghtn,nj->bgtj', p_cmp, overlap.astype(jnp.float32))
    cur = q_pos // SEL_BLOCK
    forced = (blk[None, :] == 0) | (blk[None, :] == cur[:, None]) | (blk[None, :] == cur[:, None] - 1)
    visible = blk[None, :] * SEL_BLOCK <= q_pos[:, None]
    score = jnp.where(forced, 1e9, jnp.where(visible, imp, -1.0))
    top_val, top_idx = lax.top_k(score, n_top)
    top_ok = top_val >= 0

    nb, qb = _query_blocks(T)
    bi = jnp.arange(B)[:, None, None]
    gi = jnp.arange(G)[None, :, None]
    offs = jnp.arange(SEL_BLOCK)
    n_keys = n_top * SEL_BLOCK

    def sel_block(args):
        q_b, idx_b, ok_b, pos_b = args
        tok = (idx_b[..., None] * SEL_BLOCK + offs).reshape(B, G, qb, n_keys)
        ok = jnp.repeat(ok_b, SEL_BLOCK, axis=-1) & (tok <= pos_b[:, None])
        tok_c = jnp.minimum(tok, L - 1).reshape(B, G, qb * n_keys)
        k_g = ks[bi, tok_c, gi].reshape(B, G, qb, n_keys, D)
        v_g = vs[bi, tok_c, gi].reshape(B, G, qb, n_keys, D)
        dist = (pos_b[:, None] - tok).astype(jnp.float32)
        s = (jnp.einsum('bqghd,bgqkd->bghqk', q_b, k_g).astype(jnp.float32) * scale
             - slopes[None, :, :, None, None] * dist[:, :, None])
        pr = _masked_softmax(s, ok[:, :, None])
        return jnp.einsum('bghqk,bgqkd->bqghd', pr.astype(v_g.dtype), v_g)

    o_slc = _from_blocks(lax.map(sel_block, (_to_blocks(q, 1, nb), _to_blocks(top_idx, 2, nb),
                                             _to_blocks(top_ok, 2, nb), q_pos.reshape(nb, qb))))

    pad = win_all.shape[1] - T
    kw = win_all[..., :KVG].reshape(B, pad + T, G, D)
    vw = win_all[..., KVG:].reshape(B, pad + T, G, D)

    def win_block(args):
        q_b, pos_b, b = args
        k_b = lax.dynamic_slice_in_dim(kw, b * qb, pad + qb, axis=1)
        v_b = lax.dynamic_slice_in_dim(vw, b * qb, pad + qb, axis=1)
        kp = lax.dynamic_slice_in_dim(win_pos, b * qb, pad + qb)
        dist = pos_b[:, None] - kp[None, :]
        ok = (kp[None, :] >= 0) & (dist >= 0) & (dist < WINDOW)
        s = (jnp.einsum('bqghd,bkgd->bghqk', q_b, k_b).astype(jnp.float32) * scale
             - slopes[:, :, None, None] * dist.astype(jnp.float32))
        pr = _masked_softmax(s, ok)
        return jnp.einsum('bghqk,bkgd->bqghd', pr.astype(v_b.dtype), v_b)

    o_win = _from_blocks(lax.map(win_block, (_to_blocks(q, 1, nb), q_pos.reshape(nb, qb), jnp.arange(nb))))

    g = gates.reshape(B, T, 3, G, NSA_HPG, 1)
    o = g[:, :, 0] * o_cmp + g[:, :, 1] * o_slc + g[:, :, 2] * o_win
    return o.reshape(B, T, NSA_Q)


def _stick_breaking(q, k, v, q_pos):
    B, T, H, D = q.shape
    L = k.shape[1]
    k_pos = jnp.arange(L)
    nb, qb = _query_blocks(T)
    scale = D ** -0.5

    def sb_block(args):
        q_b, pos_b = args
        z = jnp.einsum('bqhd,bkhd->bhqk', q_b, k).astype(jnp.float32) * scale
        before = k_pos[None, :] < pos_b[:, None]
        neg_log_keep = jnp.where(before, jax.nn.softplus(z), 0.0)
        between = lax.cumsum(neg_log_keep, axis=3, reverse=True) - neg_log_keep
        a = jnp.where(before, jnp.exp(jax.nn.log_sigmoid(z) - between), 0.0)
        return jnp.einsum('bhqk,bkhd->bqhd', a.astype(v.dtype), v)

    return _from_blocks(lax.map(sb_block, (_to_blocks(q, 1, nb), q_pos.reshape(nb, qb))))


def _memory_attention(q, mem_kv):
    B, T, H, D = q.shape
    M = mem_kv.shape[1]
    km = mem_kv[..., :MEM_W].reshape(B, M, H, D)
    vm = mem_kv[..., MEM_W:].reshape(B, M, H, D)
    s = jnp.einsum('bthd,bmhd->bhtm', q, km).astype(jnp.float32) * (D ** -0.5)
    pr = jax.nn.softmax(s, axis=-1)
    return jnp.einsum('bhtm,bmhd->bthd', pr.astype(vm.dtype), vm).reshape(B, T, MEM_W)


def _mixer_block(x, mem_kv, past_kv, win_buf, pos0, p):
    B, T, _ = x.shape
    h = x @ p['w_in'] + p['b_in']
    q_nsa = h[..., :OFF_KV].reshape(B, T, NSA_GROUPS, NSA_HPG, HEAD_DIM)
    kv_new = h[..., OFF_KV:OFF_WIN]
    win_new = h[..., OFF_WIN:OFF_QSB]
    q_sb = h[..., OFF_QSB:OFF_QMEM].reshape(B, T, SB_HEADS, HEAD_DIM)
    q_mem = h[..., OFF_QMEM:OFF_GNSA].reshape(B, T, MEM_HEADS, HEAD_DIM)
    g_nsa = jax.nn.sigmoid(h[..., OFF_GNSA:OFF_GMERGE]).reshape(B, T, 3, NSA_HEADS)
    g_merge = jax.nn.sigmoid(h[..., OFF_GMERGE:]).reshape(B, T, 3, D_MODEL)

    kv_all = kv_new if past_kv is None else jnp.concatenate([past_kv, kv_new], axis=1)
    win_all = jnp.concatenate([win_buf, win_new], axis=1)
    pad = win_buf.shape[1]
    q_pos = pos0 + jnp.arange(T)
    win_pos = pos0 - pad + jnp.arange(pad + T)

    o_nsa = _nsa(q_nsa, kv_all, win_all, g_nsa, q_pos, win_pos, p)
    L = kv_all.shape[1]
    k_sb = kv_all[..., 4 * KVG:4 * KVG + SB_W].reshape(B, L, SB_HEADS, HEAD_DIM)
    v_sb = kv_all[..., 4 * KVG + SB_W:].reshape(B, L, SB_HEADS, HEAD_DIM)
    o_sb = _stick_breaking(q_sb, k_sb, v_sb, q_pos).reshape(B, T, SB_W)
    o_mem = _memory_attention(q_mem, mem_kv)

    merged = (g_merge[:, :, 0] * (o_nsa @ p['w_br_nsa'])
              + g_merge[:, :, 1] * (o_sb @ p['w_br_sb'])
              + g_merge[:, :, 2] * (o_mem @ p['w_br_mem']))
    return merged @ p['w_o'], kv_new, win_all


def _conv_ffn(x, conv_buf, p):
    T = x.shape[1]
    u = x @ p['w_up'] + p['b_up']
    u_all = jnp.concatenate([conv_buf, u], axis=1)
    c = p['b_conv'] + p['w_conv'][0] * u_all[:, 0:T]
    for i in range(1, CONV_W):
        c = c + p['w_conv'][i] * u_all[:, i:i + T]
    a, g = jnp.split(c, 2, axis=-1)
    out = (a * jax.nn.gelu(g)) @ p['w_down'] + p['b_down']
    return out, u_all[:, -(CONV_W - 1):]


def _layer(x, mem_kv, past_kv, win_buf, conv_buf, pos0, p):
    mix, kv_new, win_all = _mixer_block(x, mem_kv, past_kv, win_buf, pos0, p)
    x1 = _layer_norm(ALPHA * x + mix, p['ln1_g'], p['ln1_b'])
    ff, conv_new = _conv_ffn(x1, conv_buf, p)
    y = _layer_norm(ALPHA * x1 + ff, p['ln2_g'], p['ln2_b'])
    return y, kv_new, win_all, conv_new


def setup_inputs(seed: int = 0) -> dict:
    key = jax.random.key(seed)
    keys = iter(jax.random.split(key, 48))

    def nrm(shape, scale=1.0):
        return scale * jax.random.normal(next(keys), shape, jnp.float32)

    n_pages = PAST_LEN // PAGE_SIZE
    n_phys = (DEC_BATCH * n_pages * 5) // 4
    w_buf = min(WINDOW, PAST_LEN)
    Dp = DEPTH
    D = HEAD_DIM
    return {
        'x_prompt': nrm((BATCH, SEQ, D_MODEL)),
        'x_sample': nrm((DEC_BATCH, DEC_SEQ, D_MODEL)),
        'mem_prompt': nrm((BATCH, MEM_TOKENS, D_MODEL)),
        'cache_kv_pages': nrm((Dp, n_phys, PAGE_SIZE, KV_CH)),
        'page_table': jax.random.permutation(next(keys), n_phys)[:DEC_BATCH * n_pages]
                      .reshape(DEC_BATCH, n_pages).astype(jnp.int32),
        'cache_win_kv': nrm((Dp, DEC_BATCH, w_buf, WIN_CH)),
        'cache_mem_kv': nrm((Dp, DEC_BATCH, MEM_TOKENS, MEM_KV_CH)),
        'state_ffn_conv': nrm((Dp, DEC_BATCH, CONV_W - 1, 2 * D_FF)),
        'w_in': nrm((Dp, D_MODEL, PROJ_W), D_MODEL ** -0.5),
        'b_in': nrm((Dp, PROJ_W), 0.01),
        'w_cmp_k1': nrm((Dp, CMP_LEN, D, D), (CMP_LEN * D) ** -0.5),
        'pe_cmp_k': nrm((Dp, CMP_LEN, D), 0.5),
        'w_cmp_k2': nrm((Dp, D, D), D ** -0.5),
        'w_cmp_v1': nrm((Dp, CMP_LEN, D, D), (CMP_LEN * D) ** -0.5),
        'pe_cmp_v': nrm((Dp, CMP_LEN, D), 0.5),
        'w_cmp_v2': nrm((Dp, D, D), D ** -0.5),
        'w_br_nsa': nrm((Dp, NSA_Q, D_MODEL), BETA * NSA_Q ** -0.5),
        'w_br_sb': nrm((Dp, SB_W, D_MODEL), BETA * SB_W ** -0.5),
        'w_br_mem': nrm((Dp, MEM_W, D_MODEL), BETA * MEM_W ** -0.5),
        'w_o': nrm((Dp, D_MODEL, D_MODEL), BETA * D_MODEL ** -0.5),
        'w_mem_kv': nrm((Dp, D_MODEL, MEM_KV_CH), D_MODEL ** -0.5),
        'b_mem_kv': nrm((Dp, MEM_KV_CH), 0.01),
        'ln1_g': 1.0 + nrm((Dp, D_MODEL), 0.01),
        'ln1_b': nrm((Dp, D_MODEL), 0.01),
        'w_up': nrm((Dp, D_MODEL, 2 * D_FF), D_MODEL ** -0.5),
        'b_up': nrm((Dp, 2 * D_FF), 0.01),
        'w_conv': nrm((Dp, CONV_W, 2 * D_FF), CONV_W ** -0.5),
        'b_conv': nrm((Dp, 2 * D_FF), 0.01),
        'w_down': nrm((Dp, D_FF, D_MODEL), BETA * D_FF ** -0.5),
        'b_down': nrm((Dp, D_MODEL), 0.01),
        'ln2_g': 1.0 + nrm((Dp, D_MODEL), 0.01),
        'ln2_b': nrm((Dp, D_MODEL), 0.01),
    }


def reference(x_prompt, x_sample, mem_prompt, cache_kv_pages, page_table, cache_win_kv, cache_mem_kv,
              state_ffn_conv, w_in, b_in, w_cmp_k1, pe_cmp_k, w_cmp_k2, w_cmp_v1, pe_cmp_v, w_cmp_v2,
              w_br_nsa, w_br_sb, w_br_mem, w_o, w_mem_kv, b_mem_kv, ln1_g, ln1_b, w_up, b_up, w_conv,
              b_conv, w_down, b_down, ln2_g, ln2_b):
    n_batch, n_seq = x_prompt.shape[0], x_prompt.shape[1]
    n_dec, n_pages = page_table.shape
    past_len = n_pages * cache_kv_pages.shape[2]
    w_buf = cache_win_kv.shape[2]
    hp, hs = x_prompt, x_sample
    kv_p, win_p, mem_p, conv_p, kv_s, win_s, conv_s = [], [], [], [], [], [], []
    for l in range(DEPTH):
        p = dict(w_in=w_in[l], b_in=b_in[l], w_cmp_k1=w_cmp_k1[l], pe_cmp_k=pe_cmp_k[l],
                 w_cmp_k2=w_cmp_k2[l], w_cmp_v1=w_cmp_v1[l], pe_cmp_v=pe_cmp_v[l], w_cmp_v2=w_cmp_v2[l],
                 w_br_nsa=w_br_nsa[l], w_br_sb=w_br_sb[l], w_br_mem=w_br_mem[l], w_o=w_o[l],
                 ln1_g=ln1_g[l], ln1_b=ln1_b[l], w_up=w_up[l], b_up=b_up[l], w_conv=w_conv[l],
                 b_conv=b_conv[l], w_down=w_down[l], b_down=b_down[l], ln2_g=ln2_g[l], ln2_b=ln2_b[l])
        mem_kv = mem_prompt @ w_mem_kv[l] + b_mem_kv[l]
        win0 = jnp.zeros((n_batch, WINDOW, WIN_CH), hp.dtype)
        conv0 = jnp.zeros((n_batch, CONV_W - 1, 2 * D_FF), hp.dtype)
        hp, kv_new, win_all, conv_new = _layer(hp, mem_kv, None, win0, conv0, 0, p)
        kv_p.append(kv_new)
        win_p.append(win_all[:, -min(WINDOW, n_seq):])
        mem_p.append(mem_kv)
        conv_p.append(conv_new)
        past = cache_kv_pages[l][page_table].reshape(n_dec, past_len, KV_CH)
        hs, kv_new, win_all, conv_new = _layer(hs, cache_mem_kv[l], past, cache_win_kv[l],
                                               state_ffn_conv[l], past_len, p)
        kv_s.append(kv_new)
        win_s.append(win_all[:, -w_buf:])
        conv_s.append(conv_new)
    return (hp, hs, jnp.stack(kv_p), jnp.stack(win_p), jnp.stack(mem_p), jnp.stack(conv_p),
            jnp.stack(kv_s), jnp.stack(win_s), jnp.stack(conv_s))
```

```python
import contextlib
import numpy as np
import concourse.bass as bass
import concourse.mybir as mybir
from concourse.bass_utils import run_bass_kernel_spmd

F32 = mybir.dt.float32
BF16 = mybir.dt.bfloat16
I32 = mybir.dt.int32
AF = mybir.ActivationFunctionType
ALU = mybir.AluOpType

D = 2048
KC = 16
SEQ = 4096
NQ = 1044
TILES = [(0, 512), (512, 512), (1024, 20)]
PROJ_W = 10776
D_FF = 5632
ALPHA = 2.0 ** 0.25
SCALE = 128 ** -0.5
NEG = -30000.0
SB_BASE = 16512

CHUNKS = []
for i in range(8):
    CHUNKS.append((i * 128, 128, "q", i))
for i in range(20):
    CHUNKS.append((1024 + i * 128, 128, "kv", i))
for i in range(4):
    CHUNKS.append((3584 + i * 128, 128, "q", 8 + i))
for i in range(4):
    CHUNKS.append((4096 + i * 128, 128, "q", 12 + i))
CHUNKS.append((4608, 24, "gn", 0))
for i in range(48):
    CHUNKS.append((4632 + i * 128, 128, "gm", i))
NCH = len(CHUNKS)

KFM_COLS = [1024 + i * 128 for i in range(6)] + [1024 + 1024 + i * 128 for i in range(4)]
VTM = [(1024 + 768, 256, 0), (1024 + 1536, 512, 256)]


class Sched:
    COMPUTE = ("pe", "act", "dve", "pool")

    def __init__(self, nc, n_dma_sems=10):
        self.nc = nc
        self.eng = {"pe": nc.tensor, "act": nc.scalar, "dve": nc.vector, "pool": nc.gpsimd, "sp": nc.sync}
        self.sem = {e: nc.alloc_semaphore("s_" + e) for e in self.COMPUTE}
        self.cnt = {e: 0 for e in self.COMPUTE}
        self.qeng = {"sp": "sp", "actq": "act", "poolq": "pool"}
        self.dsems = {q: [nc.alloc_semaphore(f"d_{q}{i}") for i in range(n_dma_sems)] for q in self.qeng}
        self.dval = {q: [0] * n_dma_sems for q in self.qeng}
        self.dnext = {q: 0 for q in self.qeng}
        self.seen = {}
        self.last_w = {}
        self.readers = {}
        self.n_waits = 0
        self.n_ops = 0

    def _wait(self, issuer, tok):
        if tok is None:
            return
        if tok[0] == "c":
            _, e, idx = tok
            if e == issuer and e == "pe":
                return
            key = (issuer, e)
            if self.seen.get(key, 0) >= idx:
                return
            self.seen[key] = idx
            self.eng[issuer].wait_ge(self.sem[e], idx)
        else:
            _, q, i, val = tok
            key = (issuer, q, i)
            if self.seen.get(key, 0) >= val:
                return
            self.seen[key] = val
            self.eng[issuer].wait_ge(self.dsems[q][i], val)
        self.n_waits += 1

    def _deps(self, issuer, reads, writes):
        for r in reads:
            self._wait(issuer, self.last_w.get(r))
        for w in writes:
            self._wait(issuer, self.last_w.get(w))
            for t in self.readers.get(w, ()):
                self._wait(issuer, t)

    def _record(self, tok, reads, writes):
        for r in reads:
            self.readers.setdefault(r, []).append(tok)
        for w in writes:
            self.last_w[w] = tok
            self.readers[w] = []

    def op(self, e, fn, reads=(), writes=()):
        self._deps(e, reads, writes)
        ins = fn(self.eng[e])
        self.cnt[e] += 1
        ins.then_inc(self.sem[e], 1)
        tok = ("c", e, self.cnt[e])
        self._record(tok, reads, writes)
        self.n_ops += 1
        return tok

    def dma(self, q, out, in_, reads=(), writes=(), **kw):
        issuer = self.qeng[q]
        i = self.dnext[q]
        self.dnext[q] = (i + 1) % len(self.dsems[q])
        if self.dval[q][i] > 0:
            self._wait(issuer, ("d", q, i, self.dval[q][i]))
        self._deps(issuer, reads, writes)
        ins = self.eng[issuer].dma_start(out=out, in_=in_, **kw)
        self.dval[q][i] += 16
        ins.then_inc(self.dsems[q][i], 16)
        tok = ("d", q, i, self.dval[q][i])
        self._record(tok, reads, writes)
        self.n_ops += 1
        return tok

    def idma(self, out, in_, idx_ap, reads=(), writes=(), element_offset=0):
        q, issuer = "poolq", "pool"
        i = self.dnext[q]
        self.dnext[q] = (i + 1) % len(self.dsems[q])
        if self.dval[q][i] > 0:
            self._wait(issuer, ("d", q, i, self.dval[q][i]))
        self._deps(issuer, reads, writes)
        ins = self.nc.gpsimd.indirect_dma_start(out=out, out_offset=None, in_=in_,
                                                in_offset=bass.IndirectOffsetOnAxis(ap=idx_ap, axis=0),
                                                element_offset=element_offset)
        self.dval[q][i] += 16
        ins.then_inc(self.dsems[q][i], 16)
        tok = ("d", q, i, self.dval[q][i])
        self._record(tok, reads, writes)
        self.n_ops += 1
        return tok

    def barrier(self):
        for issuer in ("pe", "act", "dve", "pool", "sp"):
            for e in self.COMPUTE:
                if e != issuer and self.cnt[e] > 0:
                    self._wait(issuer, ("c", e, self.cnt[e]))
            for q in self.qeng:
                for i, v in enumerate(self.dval[q]):
                    if v > 0:
                        self._wait(issuer, ("d", q, i, v))
        self.last_w = {}
        self.readers = {}

    def finish(self):
        for q, issuer in self.qeng.items():
            for i, v in enumerate(self.dval[q]):
                if v > 0:
                    self._wait(issuer, ("d", q, i, v))


class Rot:
    def __init__(self, items):
        self.items = items
        self.i = 0

    def next(self):
        it = self.items[self.i]
        self.i = (self.i + 1) % len(self.items)
        return it


class Bump:
    _n = [0]

    def __init__(self, nc, ranges):
        self.nc = nc
        self.ranges = [list(r) for r in ranges]

    def alloc(self, shape, dt):
        nbytes = int(np.prod(shape[1:])) * (2 if dt == BF16 else 4)
        nbytes = (nbytes + 63) // 64 * 64
        for r in self.ranges:
            if r[1] - r[0] >= nbytes:
                off = r[0]
                r[0] += nbytes
                Bump._n[0] += 1
                return self.nc.alloc_sbuf_tensor_at(f"bmp{Bump._n[0]}", list(shape), dt, offset=SB_BASE + off).ap()
        raise RuntimeError(f"Bump: out of SBUF for {shape} {dt} ranges={self.ranges}")

    def rot(self, n, shape, dt, tag):
        Bump._n[0] += 1
        k = Bump._n[0]
        return Rot([(self.alloc(shape, dt), f"{tag}{k}_{i}") for i in range(n)])


SB_TOP = 206 * 1024
EPS = 1e-5
GELU_C = 1.5957691216057308


def build_program(debug=False, phases=("p1", "p2", "p3", "p4"), P3_BRANCHES=("mem", "sb", "nsa"), npages=2560, job_names=None):
    nc = bass.Bass("TRN2", target_bir_lowering=False)
    S = Sched(nc)

    def din(name, shape, dt=F32):
        return nc.dram_tensor(name, list(shape), dt, kind="ExternalInput").ap()

    def dout(name, shape, dt=F32):
        return nc.dram_tensor(name, list(shape), dt, kind="ExternalOutput").ap()

    def dscr(name, shape, dt):
        return nc.dram_tensor(name, list(shape), dt, kind=("ExternalOutput" if debug else "Internal")).ap()

    xq = din("xq", [D, NQ])
    xs = din("xs", [D, SEQ])
    xw = din("xw", [D, 2560])
    memT = din("memT", [D, 256])
    w_in = din("w_in", [D, PROJ_W])
    b_in_fm = din("b_in_fm", [128, NCH])
    b_v_bc = din("b_v_bc", [128, 768])
    b_wv_bc = din("b_wv_bc", [128, 256])
    w_mem = din("w_mem", [D, 1024])
    b_mem_fm = din("b_mem_fm", [128, 8])
    b_mem_bc = din("b_mem_bc", [128, 1024])
    w_br = din("w_br", [D, D])
    w_o = din("w_o", [D, D])
    w_up = din("w_up", [D, 2 * D_FF])
    w_down = din("w_down", [D_FF, D])
    vec_fm = din("vec_fm", [128, 5, 16])
    bup_fm = din("bup_fm", [128, 88])
    bconv_fm = din("bconv_fm", [128, 88])
    wconv_fm = din("wconv_fm", [128, 3, 88])
    state_fm = din("state_fm", [128, 88, 4, 2])
    hvalid = din("hvalid", [128, 4])

    kvq = dout("kvq", [20, 128, NQ])
    memkv_o = dout("memkv_o", [256, 1024])
    yq = dout("yq", [16, 128, NQ])
    convp_o = dout("convp_o", [128, 88, 2])
    convs_o = dout("convs_o", [128, 88, 4, 2])

    qT_d = dscr("qT_d", [16, 128, NQ], BF16)
    gn_d = dscr("gn_d", [24, NQ], F32)
    gm_d = dscr("gm_d", [48, 128, NQ], F32)
    kT_d = dscr("kT_d", [10, 128, SEQ], BF16)
    v_d = dscr("v_d", [SEQ, 768], BF16)
    wkT_d = dscr("wkT_d", [2, 128, 2560], BF16)
    wv_d = dscr("wv_d", [2560, 256], BF16)
    mkT_d = dscr("mkT_d", [4, 128, 256], BF16)
    mv_d = dscr("mv_d", [256, 512], BF16)
    oT_d = dscr("oT_d", [16, 128, NQ], BF16)
    x1_d = dscr("x1_d", [16, 128, NQ], F32)

    PSA = nc.alloc_psum_tensor("psall", [128, 8, 512], F32).ap()
    PS = [PSA[:, i, :] for i in range(8)]
    psrot = Rot([(PS[i], f"ps{i}") for i in range(8)])
    psrot4 = Rot([(PS[i], f"ps{i}") for i in range(4)])

    xs_v = xs.rearrange("(kc p) n -> p kc n", p=128)
    xw_v = xw.rearrange("(kc p) n -> p kc n", p=128)
    xq_v = xq.rearrange("(kc p) n -> p kc n", p=128)
    memT_v = memT.rearrange("(kc p) n -> p kc n", p=128)
    w_in_v = w_in.rearrange("(kc p) n -> p kc n", p=128)
    w_mem_v = w_mem.rearrange("(kc p) n -> p kc n", p=128)
    w_br_v = w_br.rearrange("(kc p) n -> p kc n", p=128)
    w_o_v = w_o.rearrange("(kc p) n -> p kc n", p=128)
    w_up_v = w_up.rearrange("(kc p) n -> p kc n", p=128)
    w_down_v = w_down.rearrange("(kc p) n -> p kc n", p=128)

    def mm_fm(ps, pn, wt, wn, wcols, xt, xres, xcols, nk=KC, kc0=0):
        (c0, m), (t0, n) = wcols, xcols
        for k in range(nk):
            kc = kc0 + k
            S.op("pe", lambda e, kc=kc, k=k: e.matmul(
                ps[:m, :n], lhsT=wt[:, kc, c0:c0 + m], rhs=xt[:, kc, t0:t0 + n], start=(k == 0), stop=(k == nk - 1)),
                reads=[wn, xres(kc) if callable(xres) else xres], writes=[pn])

    def phase1():
        B = Bump(nc, [(0, SB_TOP)])
        bfm = B.alloc([128, NCH], F32)
        bvbc = B.alloc([128, 768], F32)
        bwvbc = B.alloc([128, 256], F32)
        bmfm = B.alloc([128, 8], F32)
        bmbc = B.alloc([128, 1024], F32)
        S.dma("sp", bfm, b_in_fm, writes=["bfm"])
        S.dma("sp", bvbc, b_v_bc, writes=["bvbc"])
        S.dma("sp", bwvbc, b_wv_bc, writes=["bwvbc"])
        S.dma("sp", bmfm, b_mem_fm, writes=["bmfm"])
        S.dma("sp", bmbc, b_mem_bc, writes=["bmbc"])
        wkv = B.alloc([128, KC, 2560], BF16)
        wstg = B.rot(2, [128, 2560], F32, "p1wstg")
        for kc in range(KC):
            sg, sgn = wstg.next()
            S.dma("sp", sg, w_in_v[:, kc, 1024:3584], writes=[sgn])
            if kc % 2:
                S.op("dve", lambda e: e.tensor_copy(out=wkv[:, kc, :], in_=sg), reads=[sgn], writes=[f"wkv{kc}"])
            else:
                S.op("act", lambda e: e.activation(out=wkv[:, kc, :], in_=sg, func=AF.Identity), reads=[sgn], writes=[f"wkv{kc}"])
        xbufs = B.rot(2, [128, KC, 512], BF16, "p1x")
        xstg = B.rot(2, [128, 8, 512], F32, "p1xstg")
        kst = B.rot(3, [128, 512], BF16, "p1ks")
        vst = B.rot(3, [128, 768], BF16, "p1vs")

        def kv_tile(src_v, t, fm_list, tm_list, wres):
            xb, xn = xbufs.next()
            for hf in range(2):
                sg, sgn = xstg.next()
                S.dma("sp", sg, src_v[:, 8 * hf:8 * hf + 8, t * 512:(t + 1) * 512], writes=[sgn])
                S.op("dve", lambda e: e.tensor_copy(out=xb[:, 8 * hf:8 * hf + 8, :], in_=sg), reads=[sgn], writes=[xn + "ab"[hf]])
            for (c0, bcol, bsrc, dst) in fm_list:
                ps, pn = psrot.next()
                for kc in range(KC):
                    S.op("pe", lambda e, ps=ps, kc=kc, c0=c0: e.matmul(
                        ps, lhsT=wkv[:, kc, c0:c0 + 128], rhs=xb[:, kc, :], start=(kc == 0), stop=(kc == KC - 1)),
                        reads=[xn + "ab"[kc // 8], wres(kc)], writes=[pn])
                st, sn = kst.next()
                S.op("act", lambda e, st=st, ps=ps, bcol=bcol, bsrc=bsrc: e.activation(
                    out=st, in_=ps, func=AF.Identity, bias=bsrc[:, bcol:bcol + 1], scale=1.0),
                    reads=[pn, "bfm"], writes=[sn])
                S.dma("poolq", dst, st, reads=[sn])
            for tt in range(4):
                st, sn = vst.next()
                tot = 0
                for (c0, ncols, bap, voff, bres) in tm_list:
                    ps, pn = psrot.next()
                    for kc in range(KC):
                        S.op("pe", lambda e, ps=ps, kc=kc, c0=c0, ncols=ncols, tt=tt: e.matmul(
                            ps[:, :ncols], lhsT=xb[:, kc, tt * 128:(tt + 1) * 128], rhs=wkv[:, kc, c0:c0 + ncols],
                            start=(kc == 0), stop=(kc == KC - 1)),
                            reads=[xn + "ab"[kc // 8], wres(kc)], writes=[pn])
                    S.op("dve", lambda e, st=st, ps=ps, ncols=ncols, voff=voff, bap=bap: e.tensor_tensor(
                        out=st[:, voff:voff + ncols], in0=ps[:, :ncols], in1=bap[:, voff:voff + ncols], op=ALU.add),
                        reads=[pn, bres], writes=[sn])
                    tot = max(tot, voff + ncols)
                yield tt, st, sn, tot

        for t in range(SEQ // 512):
            fm = [(col0 - 1024, 8 + (col0 - 1024) // 128, bfm, kT_d[ci, :, t * 512:(t + 1) * 512])
                  for ci, col0 in enumerate(KFM_COLS)]
            tm = [(col0 - 1024, ncols, bvbc, voff, "bvbc") for (col0, ncols, voff) in VTM]
            for tt, st, sn, tot in kv_tile(xs_v, t, fm, tm, lambda kc: f"wkv{kc}"):
                r0 = t * 512 + tt * 128
                S.dma("poolq", v_d[r0:r0 + 128, :], st[:, :tot], reads=[sn])
        for t in range(5):
            fm = [(2048 + g * 128, 8 + 16 + g, bfm, wkT_d[g, :, t * 512:(t + 1) * 512]) for g in range(2)]
            tm = [(2048 + 256, 256, bwvbc, 0, "bwvbc")]
            for tt, st, sn, tot in kv_tile(xw_v, t, fm, tm, lambda kc: f"wkv{kc}"):
                r0 = t * 512 + tt * 128
                S.dma("poolq", wv_d[r0:r0 + 128, :], st[:, :256], reads=[sn])
        S.barrier()
        B2 = Bump(nc, [(0, SB_TOP)])
        bmfm2 = B2.alloc([128, 8], F32)
        bmbc2 = B2.alloc([128, 1024], F32)
        S.dma("sp", bmfm2, b_mem_fm, writes=["bmfm"])
        S.dma("sp", bmbc2, b_mem_bc, writes=["bmbc"])
        wm = B2.alloc([128, KC, 1024], BF16)
        S.dma("poolq", wm, w_mem_v, writes=["wm"])
        mx = B2.alloc([128, KC, 256], BF16)
        S.dma("poolq", mx, memT_v, writes=["mx"])
        mst = B2.rot(2, [128, 256], BF16, "mst")
        for hh in range(4):
            ps, pn = psrot.next()
            mm_fm(ps, pn, wm, "wm", (hh * 128, 128), mx, "mx", (0, 256))
            st, sn = mst.next()
            S.op("act", lambda e, st=st, ps=ps, hh=hh: e.activation(
                out=st, in_=ps[:, :256], func=AF.Identity, bias=bmfm2[:, hh:hh + 1], scale=1.0),
                reads=[pn, "bmfm"], writes=[sn])
            S.dma("sp", mkT_d[hh], st, reads=[sn])
        m32 = B2.rot(2, [128, 1024], F32, "m32")
        m16 = B2.rot(2, [128, 512], BF16, "m16")
        for tt in range(2):
            o32, on = m32.next()
            o16, o16n = m16.next()
            for half in range(2):
                ps, pn = psrot.next()
                for kc in range(KC):
                    S.op("pe", lambda e, ps=ps, kc=kc, tt=tt, half=half: e.matmul(
                        ps, lhsT=mx[:, kc, tt * 128:(tt + 1) * 128], rhs=wm[:, kc, half * 512:(half + 1) * 512],
                        start=(kc == 0), stop=(kc == KC - 1)), reads=["mx", "wm"], writes=[pn])
                S.op("dve", lambda e, ps=ps, o32=o32, half=half: e.tensor_tensor(
                    out=o32[:, half * 512:(half + 1) * 512], in0=ps, in1=bmbc2[:, half * 512:(half + 1) * 512], op=ALU.add),
                    reads=[pn, "bmbc"], writes=[on])
            S.op("dve", lambda e, o16=o16, o32=o32: e.tensor_copy(out=o16, in_=o32[:, 512:1024]),
                 reads=[on], writes=[o16n])
            S.dma("sp", memkv_o[tt * 128:(tt + 1) * 128, :], o32, reads=[on])
            S.dma("sp", mv_d[tt * 128:(tt + 1) * 128, :], o16, reads=[o16n])
        S.barrier()

    def phase2():
        B = Bump(nc, [(0, SB_TOP)])
        bfm = B.alloc([128, NCH], F32)
        S.dma("sp", bfm, b_in_fm, writes=["bfm"])
        xqb = B.alloc([128, KC, NQ], BF16)
        xstg = B.rot(3, [128, NQ], F32, "p2xstg")
        for kc in range(KC):
            sg, sgn = xstg.next()
            S.dma("sp", sg, xq_v[:, kc, :], writes=[sgn])
            S.op("dve", lambda e: e.tensor_copy(out=xqb[:, kc, :], in_=sg), reads=[sgn], writes=[f"xq{kc}"])
        wb = B.rot(2, [128, KC, 512], BF16, "p2w")
        wstg = B.rot(2, [128, KC, 512], F32, "p2wstg")
        st32 = B.rot(3, [128, NQ], F32, "p2s32")
        st16 = B.rot(3, [128, NQ], BF16, "p2s16")
        groups = []
        i = 0
        while i < NCH:
            g = [i]
            while (len(g) < 4 and i + len(g) < NCH and CHUNKS[i + len(g)][1] == 128 and CHUNKS[g[-1]][1] == 128
                   and CHUNKS[i + len(g)][0] == CHUNKS[g[-1]][0] + 128):
                g.append(i + len(g))
            groups.append(g)
            i += len(g)
        for g in groups:
            wt, wn = wb.next()
            gcol0 = CHUNKS[g[0]][0]
            gcols = sum(CHUNKS[c][1] for c in g)
            sg, sgn = wstg.next()
            S.dma("sp", sg[:, :, :gcols], w_in_v[:, :, gcol0:gcol0 + gcols], writes=[sgn])
            S.op("dve", lambda e: e.tensor_copy(out=wt[:, :, :gcols], in_=sg[:, :, :gcols]), reads=[sgn], writes=[wn])
            for c in g:
                col0, ncols, kind, idx = CHUNKS[c]
                lc = col0 - gcol0
                pss = [psrot.next() for _ in TILES]
                for ti, (t0, tn) in enumerate(TILES):
                    ps, pn = pss[ti]
                    mm_fm(ps, pn, wt, wn, (lc, ncols), xqb, lambda kc: f"xq{kc}", (t0, tn))
                st, sn = st16.next() if kind == "q" else st32.next()
                func = AF.Sigmoid if kind in ("gn", "gm") else AF.Identity
                for ti, (t0, tn) in enumerate(TILES):
                    ps, pn = pss[ti]
                    S.op("act", lambda e, st=st, ps=ps, ncols=ncols, t0=t0, tn=tn, c=c, func=func: e.activation(
                        out=st[:ncols, t0:t0 + tn], in_=ps[:ncols, :tn], func=func, bias=bfm[:ncols, c:c + 1], scale=1.0),
                        reads=[pn, "bfm"], writes=[sn])
                if kind == "q":
                    S.dma("poolq", qT_d[idx], st, reads=[sn])
                elif kind == "kv":
                    S.dma("poolq", kvq[idx], st, reads=[sn])
                elif kind == "gn":
                    S.dma("poolq", gn_d, st[:24, :], reads=[sn])
                else:
                    S.dma("poolq", gm_d[idx], st, reads=[sn])
        S.barrier()

    def phase3_zero():
        B = Bump(nc, [(0, SB_TOP)])
        z = B.alloc([128, NQ], BF16)
        S.op("dve", lambda e: e.memset(z, 0.0), writes=["z"])
        for i in range(16):
            S.dma("sp", oT_d[i], z, reads=["z"])
        S.barrier()

    SZ16 = 16 * NQ * 2
    SZ32 = 16 * NQ * 4
    SZACT = 44 * NQ * 2

    def fixed(name, off, shape, dt):
        return nc.alloc_sbuf_tensor_at(name, list(shape), dt, offset=SB_BASE + off).ap()

    def layer_norm_fm(B, r, gcol, bcol, vec, ones32, out_bf=None, tag="ln"):
        sq = B.rot(3, [128, 512], F32, tag + "sq")
        mean_r = B.rot(2, [128, 512], F32, tag + "mean")
        rstd_r = B.rot(2, [128, 512], F32, tag + "rstd")
        tmp_r = B.rot(3, [128, 512], F32, tag + "tmp")
        for ti, (t0, tn) in enumerate(TILES):
            ps1, pn1 = psrot.next()
            ps2, pn2 = psrot.next()
            for dc in range(16):
                s, sn = sq.next()
                S.op("act", lambda e, s=s, dc=dc: e.activation(out=s[:, :tn], in_=r[:, dc, t0:t0 + tn], func=AF.Square),
                     reads=[f"r{dc}"], writes=[sn])
                S.op("pe", lambda e, dc=dc: e.matmul(ps1[:, :tn], lhsT=ones32, rhs=r[:, dc, t0:t0 + tn],
                                                      start=(dc == 0), stop=(dc == 15)),
                     reads=[f"r{dc}", "ones32"], writes=[pn1])
                S.op("pe", lambda e, dc=dc, s=s: e.matmul(ps2[:, :tn], lhsT=ones32, rhs=s[:, :tn],
                                                           start=(dc == 0), stop=(dc == 15)),
                     reads=[sn, "ones32"], writes=[pn2])
            mean, mn = mean_r.next()
            rstd, rn = rstd_r.next()
            S.op("act", lambda e: e.activation(out=mean[:, :tn], in_=ps1[:, :tn], func=AF.Identity), reads=[pn1], writes=[mn])
            S.op("act", lambda e: e.activation(out=rstd[:, :tn], in_=ps1[:, :tn], func=AF.Square), reads=[pn1], writes=[rn])
            S.op("dve", lambda e: e.tensor_tensor(out=rstd[:, :tn], in0=ps2[:, :tn], in1=rstd[:, :tn], op=ALU.subtract),
                 reads=[pn2, rn], writes=[rn])
            S.op("dve", lambda e: e.tensor_scalar(out=rstd[:, :tn], in0=rstd[:, :tn], scalar1=0.0, scalar2=EPS,
                                                  op0=ALU.max, op1=ALU.add), reads=[rn], writes=[rn])
            S.op("act", lambda e: e.activation(out=rstd[:, :tn], in_=rstd[:, :tn], func=AF.Sqrt), reads=[rn], writes=[rn])
            S.op("dve", lambda e: e.reciprocal(out=rstd[:, :tn], in_=rstd[:, :tn]), reads=[rn], writes=[rn])
            for dc in range(16):
                t, tnm = tmp_r.next()
                S.op("dve", lambda e, t=t, dc=dc: e.tensor_tensor(out=t[:, :tn], in0=r[:, dc, t0:t0 + tn], in1=mean[:, :tn],
                                                                   op=ALU.subtract), reads=[f"r{dc}", mn], writes=[tnm])
                S.op("pool", lambda e, t=t: e.tensor_tensor(out=t[:, :tn], in0=t[:, :tn], in1=rstd[:, :tn], op=ALU.mult),
                     reads=[tnm, rn], writes=[tnm])
                S.op("act", lambda e, t=t, dc=dc: e.activation(
                    out=r[:, dc, t0:t0 + tn], in_=t[:, :tn], func=AF.Identity,
                    scale=vec[:, gcol, dc:dc + 1], bias=vec[:, bcol, dc:dc + 1]),
                    reads=[tnm, "vec"], writes=[f"r{dc}"])
                if out_bf is not None:
                    S.op("pool", lambda e, dc=dc: e.tensor_copy(out=out_bf[:, dc, t0:t0 + tn], in_=r[:, dc, t0:t0 + tn]),
                         reads=[f"r{dc}"], writes=[f"xb{dc}"])

    def phase4():
        oT = fixed("p4_oT", 0, [128, 16, NQ], BF16)
        merged = fixed("p4_merged", SZ16, [128, 16, NQ], BF16)
        B = Bump(nc, [(2 * SZ16, SB_TOP)])
        for i in range(16):
            S.dma("sp", oT[:, i, :], oT_d[i], writes=[f"o{i}"])
        wr = B.rot(2, [128, KC, 512], BF16, "p4w")
        wsg = B.rot(2, [128, KC, 512], F32, "p4wsg")
        gr = B.rot(2, [128, 3, NQ], F32, "p4g")
        tr = B.rot(6, [128, 512], F32, "p4t")
        gm_v = gm_d.rearrange("(br dc) p n -> p br dc n", br=3)
        for dg in range(4):
            wt, wn = wr.next()
            sg, sgn = wsg.next()
            S.dma("sp", sg, w_br_v[:, :, dg * 512:(dg + 1) * 512], writes=[sgn])
            S.op("dve", lambda e: e.tensor_copy(out=wt, in_=sg), reads=[sgn], writes=[wn])
            for dl in range(4):
                dc = dg * 4 + dl
                gt, gn = gr.next()
                S.dma("sp", gt, gm_v[:, :, dc, :], writes=[gn])
                for ti, (t0, tn) in enumerate(TILES):
                    tmps = []
                    for br, (k0, nk) in enumerate([(0, 8), (8, 4), (12, 4)]):
                        ps, pn = psrot.next()
                        mm_fm(ps, pn, wt, wn, (dl * 128, 128), oT, lambda kc: f"o{kc}", (t0, tn), nk=nk, kc0=k0)
                        t, tnm = tr.next()
                        S.op("dve", lambda e, t=t, ps=ps, br=br: e.tensor_tensor(
                            out=t[:, :tn], in0=ps[:, :tn], in1=gt[:, br, t0:t0 + tn], op=ALU.mult),
                            reads=[pn, gn], writes=[tnm])
                        tmps.append((t, tnm))
                    (ta, na), (tb, nb), (tc, ncn) = tmps
                    S.op("pool", lambda e, ta=ta, tb=tb: e.tensor_tensor(out=ta[:, :tn], in0=ta[:, :tn], in1=tb[:, :tn], op=ALU.add),
                         reads=[na, nb], writes=[na])
                    S.op("pool", lambda e, ta=ta, tc=tc, dc=dc: e.tensor_tensor(
                        out=merged[:, dc, t0:t0 + tn], in0=ta[:, :tn], in1=tc[:, :tn], op=ALU.add),
                        reads=[na, ncn], writes=[f"m{dc}"])
        S.barrier()
        r1 = fixed("p4_r1", 2 * SZ16, [128, 16, NQ], F32)
        B = Bump(nc, [(0, SZ16), (2 * SZ16 + SZ32, SB_TOP)])
        wr = B.rot(2, [128, KC, 512], BF16, "p4wo")
        xr = B.rot(2, [128, NQ], F32, "p4x")
        wsg = B.rot(2, [128, KC, 512], F32, "p4wosg")
        for dg in range(4):
            wt, wn = wr.next()
            sg, sgn = wsg.next()
            S.dma("sp", sg, w_o_v[:, :, dg * 512:(dg + 1) * 512], writes=[sgn])
            S.op("act", lambda e: e.activation(out=wt, in_=sg, func=AF.Identity), reads=[sgn], writes=[wn])
            for dl in range(4):
                dc = dg * 4 + dl
                xt, xn = xr.next()
                S.dma("sp", xt, xq_v[:, dc, :], writes=[xn])
                for ti, (t0, tn) in enumerate(TILES):
                    ps, pn = psrot.next()
                    mm_fm(ps, pn, wt, wn, (dl * 128, 128), merged, lambda kc: f"m{kc}", (t0, tn))
                    S.op("dve", lambda e, ps=ps, xt=xt, dc=dc: e.scalar_tensor_tensor(
                        out=r1[:, dc, t0:t0 + tn], in0=xt[:, t0:t0 + tn], scalar=ALPHA, in1=ps[:, :tn],
                        op0=ALU.mult, op1=ALU.add), reads=[pn, xn], writes=[f"r{dc}"])
        S.barrier()
        x1b = fixed("p4_x1b", 0, [128, 16, NQ], BF16)
        B = Bump(nc, [(SZ16, 2 * SZ16), (2 * SZ16 + SZ32, SB_TOP)])
        vec = B.alloc([128, 5, 16], F32)
        S.dma("sp", vec, vec_fm, writes=["vec"])
        ones32 = B.alloc([128, 128], F32)
        S.op("dve", lambda e: e.memset(ones32, 1.0 / D), writes=["ones32"])
        layer_norm_fm(B, r1, 0, 1, vec, ones32, out_bf=x1b, tag="ln1")
        for dc in range(16):
            S.dma("poolq", x1_d[dc], r1[:, dc, :], reads=[f"r{dc}"])
        S.barrier()
        act = fixed("p4_act", SZ16, [128, 44, NQ], BF16)
        B = Bump(nc, [(SZ16 + SZACT, SB_TOP)])
        bup = B.alloc([128, 88], F32)
        bcv = B.alloc([128, 88], F32)
        wcv = B.alloc([128, 3, 88], F32)
        stt = B.alloc([128, 88, 4, 2], F32)
        hv = B.alloc([128, 4], F32)
        S.dma("sp", bup, bup_fm, writes=["bup"])
        S.dma("sp", bcv, bconv_fm, writes=["bcv"])
        S.dma("sp", wcv, wconv_fm, writes=["wcv"])
        S.dma("sp", stt, state_fm, writes=["stt"])
        S.dma("sp", hv, hvalid, writes=["hv"])
        convp = B.alloc([128, 88, 2], F32)
        convs = B.alloc([128, 88, 4, 2], F32)
        S.op("pool", lambda e: e.memset(act[:, :, 1024:1028], 0.0), writes=["act_halo"])
        wa_r = B.rot(2, [128, KC, 128], BF16, "wa")
        wg_r = B.rot(2, [128, KC, 128], BF16, "wg")
        wsg = B.rot(3, [128, KC, 128], F32, "wupsg")
        ub_r = B.rot(2, [128, 2, 514], F32, "ub")
        us_r = B.rot(4, [128, 4, 6], F32, "us")
        uh_r = B.rot(4, [128, 4], F32, "uh")
        cv_r = B.rot(4, [128, NQ], F32, "cv")
        t1_r = B.rot(1, [128, NQ], F32, "t1")
        t2_r = B.rot(1, [128, NQ], F32, "t2")

        def up_chunk(uc, wt, wn, lc):
            pss = [psrot.next() for _ in TILES]
            for ti, (t0, tn) in enumerate(TILES):
                ps, pn = pss[ti]
                mm_fm(ps, pn, wt, wn, (lc, 128), x1b, lambda kc: f"xb{kc}", (t0, tn))
            ub, ubn = ub_r.next()
            us, usn = us_r.next()
            uh, uhn = uh_r.next()
            bias = bup[:, uc:uc + 1]
            ps2, pn2 = pss[2]
            S.op("act", lambda e: e.activation(out=uh, in_=ps2[:, 0:4], func=AF.Identity, bias=bias, scale=1.0),
                 reads=[pn2, "bup"], writes=[uhn])
            S.op("act", lambda e: e.activation(out=us[:, :, 2:6], in_=ps2[:, 4:20].rearrange("p (s t) -> p s t", s=4),
                                               func=AF.Identity, bias=bias, scale=1.0),
                 reads=[pn2, "bup"], writes=[usn + "u"])
            S.op("pool", lambda e: e.tensor_copy(out=us[:, :, 0:2], in_=stt[:, uc, :, :]), reads=["stt"], writes=[usn + "s"])
            S.op("dve", lambda e: e.tensor_tensor(out=ub[:, :, 0:2], in0=uh.rearrange("p (s t) -> p s t", s=2),
                                                  in1=hv.rearrange("p (s t) -> p s t", s=2), op=ALU.mult),
                 reads=[uhn, "hv"], writes=[ubn + "h"])
            for s in range(2):
                ps, pn = pss[s]
                S.op("act", lambda e, s=s, ps=ps: e.activation(out=ub[:, s, 2:514], in_=ps, func=AF.Identity, bias=bias, scale=1.0),
                     reads=[pn, "bup"], writes=[ubn + f"u{s}"])
            S.op("pool", lambda e: e.tensor_copy(out=convp[:, uc, :], in_=ub[:, 1, 512:514]), reads=[ubn + "u1"], writes=["convp"])
            S.op("pool", lambda e: e.tensor_copy(out=convs[:, uc, :, :], in_=us[:, :, 4:6]), reads=[usn + "u"], writes=["convs"])
            cv, cvn = cv_r.next()
            w0, w1, w2 = (wcv[:, k, uc:uc + 1] for k in range(3))
            ub_reads = [ubn + "h", ubn + "u0", ubn + "u1"]
            cvp = cv[:, 0:1024].rearrange("p (s t) -> p s t", s=2)
            S.op("dve", lambda e: e.tensor_scalar(out=cvp, in0=ub[:, :, 0:512], scalar1=w0, scalar2=bcv[:, uc:uc + 1],
                                                  op0=ALU.mult, op1=ALU.add), reads=ub_reads + ["wcv", "bcv"], writes=[cvn])
            S.op("dve", lambda e: e.scalar_tensor_tensor(out=cvp, in0=ub[:, :, 1:513], scalar=w1, in1=cvp,
                                                         op0=ALU.mult, op1=ALU.add), reads=ub_reads + [cvn, "wcv"], writes=[cvn])
            S.op("dve", lambda e: e.scalar_tensor_tensor(out=cvp, in0=ub[:, :, 2:514], scalar=w2, in1=cvp,
                                                         op0=ALU.mult, op1=ALU.add), reads=ub_reads + [cvn, "wcv"], writes=[cvn])
            cvs = cv[:, 1028:1044].rearrange("p (s t) -> p s t", s=4)
            us_reads = [usn + "u", usn + "s"]
            S.op("dve", lambda e: e.tensor_scalar(out=cvs, in0=us[:, :, 0:4], scalar1=w0, scalar2=bcv[:, uc:uc + 1],
                                                  op0=ALU.mult, op1=ALU.add), reads=us_reads + ["wcv", "bcv"], writes=[cvn + "s"])
            S.op("dve", lambda e: e.scalar_tensor_tensor(out=cvs, in0=us[:, :, 1:5], scalar=w1, in1=cvs,
                                                         op0=ALU.mult, op1=ALU.add), reads=us_reads + [cvn + "s", "wcv"], writes=[cvn + "s"])
            S.op("dve", lambda e: e.scalar_tensor_tensor(out=cvs, in0=us[:, :, 2:6], scalar=w2, in1=cvs,
                                                         op0=ALU.mult, op1=ALU.add), reads=us_reads + [cvn + "s", "wcv"], writes=[cvn + "s"])
            return cv, cvn

        SEGS = [(0, 1024), (1028, 16)]
        for pg in range(22):
            for i in range(2):
                ia = pg * 2 + i
                wa, wan = wa_r.next()
                wg, wgn = wg_r.next()
                sg, sgn = wsg.next()
                S.dma("sp", sg, w_up_v[:, :, ia * 128:(ia + 1) * 128], writes=[sgn])
                S.op("act", lambda e: e.activation(out=wa, in_=sg, func=AF.Identity), reads=[sgn], writes=[wan])
                sg2, sgn2 = wsg.next()
                S.dma("sp", sg2, w_up_v[:, :, D_FF + ia * 128:D_FF + (ia + 1) * 128], writes=[sgn2])
                S.op("act", lambda e: e.activation(out=wg, in_=sg2, func=AF.Identity), reads=[sgn2], writes=[wgn])
                ca, can = up_chunk(ia, wa, wan, 0)
                cg, cgn = up_chunk(44 + ia, wg, wgn, 0)
                t1, t1n = t1_r.next()
                t2, t2n = t2_r.next()
                for (c0, cn) in SEGS:
                    sl = slice(c0, c0 + cn)
                    gres = cgn if c0 == 0 else cgn + "s"
                    ares = can if c0 == 0 else can + "s"
                    S.op("act", lambda e, sl=sl: e.activation(out=t1[:, sl], in_=cg[:, sl], func=AF.Square), reads=[gres], writes=[t1n])
                    S.op("dve", lambda e, sl=sl: e.tensor_scalar(out=t1[:, sl], in0=t1[:, sl], scalar1=0.044715, scalar2=1.0,
                                                                 op0=ALU.mult, op1=ALU.add), reads=[t1n], writes=[t1n])
                    S.op("pool", lambda e, sl=sl: e.tensor_tensor(out=t1[:, sl], in0=t1[:, sl], in1=cg[:, sl], op=ALU.mult),
                         reads=[t1n, gres], writes=[t1n])
                    S.op("act", lambda e, sl=sl: e.activation(out=t1[:, sl], in_=t1[:, sl], func=AF.Sigmoid, scale=GELU_C),
                         reads=[t1n], writes=[t1n])
                    S.op("pool", lambda e, sl=sl: e.tensor_tensor(out=t2[:, sl], in0=ca[:, sl], in1=cg[:, sl], op=ALU.mult),
                         reads=[ares, gres], writes=[t2n])
                    S.op("dve", lambda e, sl=sl, ia=ia: e.tensor_tensor(out=act[:, ia, sl], in0=t1[:, sl], in1=t2[:, sl], op=ALU.mult),
                         reads=[t1n, t2n], writes=[f"a{ia}"])
        S.dma("poolq", convp_o, convp, reads=["convp"])
        S.dma("poolq", convs_o, convs, reads=["convs"])
        S.barrier()
        r2 = fixed("p4_r2", SZ16 + SZACT, [128, 16, NQ], F32)
        B = Bump(nc, [(0, SZ16), (SZ16 + SZACT + SZ32, SB_TOP)])
        vec = B.alloc([128, 5, 16], F32)
        S.dma("sp", vec, vec_fm, writes=["vec"])
        ones32 = B.alloc([128, 128], F32)
        S.op("dve", lambda e: e.memset(ones32, 1.0 / D), writes=["ones32"])
        wd_r = B.rot(2, [128, 44, 128], BF16, "wd")
        wdsg = B.rot(1, [128, 22, 128], F32, "wdsg")
        x1_r = B.rot(2, [128, NQ], F32, "x1c")
        tm_r = B.rot(2, [128, 512], F32, "dtm")
        for dc in range(16):
            wt, wn = wd_r.next()
            for hf in range(2):
                sg, sgn = wdsg.next()
                S.dma("sp", sg, w_down_v[:, 22 * hf:22 * hf + 22, dc * 128:(dc + 1) * 128], writes=[sgn])
                S.op("dve", lambda e: e.tensor_copy(out=wt[:, 22 * hf:22 * hf + 22, :], in_=sg), reads=[sgn], writes=[wn + "ab"[hf]])
            xt, xn = x1_r.next()
            S.dma("sp", xt, x1_d[dc], writes=[xn])
            for ti, (t0, tn) in enumerate(TILES):
                ps, pn = psrot.next()
                for kc in range(44):
                    S.op("pe", lambda e, kc=kc, ps=ps: e.matmul(ps[:, :tn], lhsT=wt[:, kc, :], rhs=act[:, kc, t0:t0 + tn],
                                                               start=(kc == 0), stop=(kc == 43)),
                         reads=[wn + "ab"[kc // 22], f"a{kc}"] + (["act_halo"] if ti == 2 else []), writes=[pn])
                tm, tmn = tm_r.next()
                S.op("act", lambda e, ps=ps, tm=tm, dc=dc: e.activation(out=tm[:, :tn], in_=ps[:, :tn], func=AF.Identity,
                                                                         bias=vec[:, 4, dc:dc + 1], scale=1.0),
                     reads=[pn, "vec"], writes=[tmn])
                S.op("dve", lambda e, tm=tm, xt=xt, dc=dc: e.scalar_tensor_tensor(
                    out=r2[:, dc, t0:t0 + tn], in0=xt[:, t0:t0 + tn], scalar=ALPHA, in1=tm[:, :tn],
                    op0=ALU.mult, op1=ALU.add), reads=[tmn, xn], writes=[f"r{dc}"])
        S.barrier()
        B = Bump(nc, [(0, SZ16), (SZ16 + SZACT + SZ32, SB_TOP)])
        vec = B.alloc([128, 5, 16], F32)
        S.dma("sp", vec, vec_fm, writes=["vec"])
        ones32 = B.alloc([128, 128], F32)
        S.op("dve", lambda e: e.memset(ones32, 1.0 / D), writes=["ones32"])
        layer_norm_fm(B, r2, 2, 3, vec, ones32, out_bf=None, tag="ln2")
        for dc in range(16):
            S.dma("poolq", yq[dc], r2[:, dc, :], reads=[f"r{dc}"])
        S.barrier()

    SLOPES = [2.0 ** (-(h + 1)) for h in range(8)]
    KCH = [0, 128, 256, 384, 512, 640, 1024, 1152, 1280, 1408]
    KV_OF_CI = [0, 1, 2, 3, 4, 5, 8, 9, 10, 11]
    V_CHUNKS = [6, 7, 12, 13, 14, 15]
    LS = 8320

    pool_in = din("pool", [npages * 128, 2048])
    pt_in = din("pt", [1, 256], I32)
    rowid_in = din("rowid", [128, 1])
    cwin_in = din("cwin", [4, 512, 512])
    cmem_in = din("cmem", [4, 256, 1024])
    ident_in = din("ident", [128, 128])
    uinc_in = din("uinc", [128, 128])
    qpos_bc_in = din("qpos_bc", [128, NQ])
    refs_in = din("refs", [128, 14])
    qposT_in = din("qposT", [128, 13])
    curT_in = din("curT", [128, 13])
    kpos_seq_in = din("kpos_seq", [128, 32])
    kpos_smp_in = din("kpos_smp", [128, 65])
    wpos_in = din("wpos", [128, 20])
    wpos_smp_in = din("wpos_smp", [128, 5])
    cend_in = din("cend", [128, 4])
    cend_seq_in = din("cend_seq", [128, 2])
    blk_in = din("blk", [128, 2, 129])
    ov_in = din("ov", [128, 4, 129])
    w1_in = din("w1c", [2, 128, 32, 128])
    w2_in = din("w2c", [2, 128, 128])
    pe_in = din("pec", [2, 128, 32])

    wins_o = dout("wins_o", [4, 512, 512])

    kTs_d = dscr("kTs_d", [4, 10, 128, LS], BF16)
    vs_d = dscr("vs_d", [4, LS, 768], BF16)
    wkTs_d = dscr("wkTs_d", [4, 2, 128, 640], BF16)
    wvs_d = dscr("wvs_d", [4, 640, 256], BF16)
    mkTs_d = dscr("mkTs_d", [4, 4, 128, 256], BF16)
    mvs_d = dscr("mvs_d", [4, 256, 512], BF16)
    selT_d = dscr("selT_d", [7, 2, 129, 512], BF16)

    def phase3_prepare():
        B = Bump(nc, [(0, SB_TOP)])
        ident = B.alloc([128, 128], F32)
        S.dma("sp", ident, ident_in, writes=["ident"])
        pti = B.alloc([128, 256], I32)
        ptf = B.alloc([128, 256], F32)
        rid = B.alloc([128, 1], F32)
        idx = B.alloc([128, 256], I32)
        S.dma("sp", pti, pt_in.partition_broadcast(128), writes=["pti"])
        S.dma("sp", rid, rowid_in, writes=["rid"])
        S.op("dve", lambda e: e.tensor_copy(out=ptf, in_=pti), reads=["pti"], writes=["ptf"])
        S.op("dve", lambda e: e.tensor_scalar(out=ptf, in0=ptf, scalar1=128.0, scalar2=rid[:, 0:1], op0=ALU.mult, op1=ALU.add),
             reads=["ptf", "rid"], writes=["ptf"])
        S.op("dve", lambda e: e.tensor_copy(out=idx, in_=ptf), reads=["ptf"], writes=["idx"])
        zt = B.alloc([128, 10, 124], BF16)
        S.op("pool", lambda e: e.memset(zt, 0.0), writes=["zt"])
        kvn = B.alloc([128, 20, 16], F32)
        kvnb = B.alloc([128, 20, 16], BF16)
        S.dma("sp", kvn, kvq.rearrange("c p n -> p c n")[:, :, 1028:1044], writes=["kvn"])
        S.op("dve", lambda e: e.tensor_copy(out=kvnb, in_=kvn), reads=["kvn"], writes=["kvnb"])
        tok32 = B.alloc([16, 10, 128], F32)
        tokb = B.alloc([16, 10, 128], BF16)
        tl = V_CHUNKS + [16, 17, 18, 19]
        for g0 in (0, 4, 8):
            ps, pn = psrot.next()
            n = min(4, 10 - g0)
            for i in range(n):
                S.op("pe", lambda e, i=i: e.transpose(out=ps[:16, i * 128:(i + 1) * 128], in_=kvn[:, tl[g0 + i], :], identity=ident),
                     reads=["kvn", "ident"], writes=[pn])
            S.op("act", lambda e: e.activation(out=tok32[:, g0:g0 + n, :].rearrange("p a b -> p (a b)"), in_=ps[:16, :n * 128], func=AF.Identity),
                 reads=[pn], writes=["tok32"])
        S.op("dve", lambda e: e.tensor_copy(out=tokb, in_=tok32), reads=["tok32"], writes=["tokb"])
        pg = B.rot(4, [128, 2048], F32, "pg")
        kst = B.rot(2, [128, 10, 128], BF16, "kst")
        vst = B.rot(2, [128, 768], BF16, "vst")
        pgq = {}

        def issue_gather(s_, p_):
            t_, tn_ = pg.next()
            S.idma(t_, pool_in, idx[:, s_ * 64 + p_:s_ * 64 + p_ + 1], reads=["idx"], writes=[tn_])
            pgq[(s_, p_)] = (t_, tn_)

        for s in range(4):
            issue_gather(s, 0)
            issue_gather(s, 1)
            for p in range(64):
                if p + 2 < 64:
                    issue_gather(s, p + 2)
                t, tn = pgq.pop((s, p))
                ks, ksn = kst.next()
                for (g0, g1) in ((0, 4), (4, 8), (8, 10)):
                    ps, pn = psrot.next()
                    for ci in range(g0, g1):
                        S.op("pe", lambda e, ci=ci: e.transpose(out=ps[:, (ci - g0) * 128:(ci - g0 + 1) * 128],
                                                                 in_=t[:, KCH[ci]:KCH[ci] + 128], identity=ident),
                             reads=[tn, "ident"], writes=[pn])
                    eng = "act" if g0 == 0 else "dve"
                    if eng == "act":
                        S.op("act", lambda e: e.activation(out=ks[:, g0:g1, :].rearrange("p a b -> p (a b)"),
                                                           in_=ps[:, :(g1 - g0) * 128], func=AF.Identity), reads=[pn], writes=[ksn])
                    else:
                        S.op("dve", lambda e: e.tensor_copy(out=ks[:, g0:g1, :].rearrange("p a b -> p (a b)"),
                                                            in_=ps[:, :(g1 - g0) * 128]), reads=[pn], writes=[ksn])
                S.dma("sp", kTs_d[s].rearrange("c p n -> p c n")[:, :, p * 128:(p + 1) * 128], ks, reads=[ksn])
                vs, vsn = vst.next()
                S.op("dve", lambda e: e.tensor_copy(out=vs[:, 0:256], in_=t[:, 768:1024]), reads=[tn], writes=[vsn])
                S.op("act", lambda e: e.activation(out=vs[:, 256:768], in_=t[:, 1536:2048], func=AF.Identity), reads=[tn], writes=[vsn])
                S.dma("sp", vs_d[s, p * 128:(p + 1) * 128, :], vs, reads=[vsn])
            kview = kTs_d[s].rearrange("c p n -> p c n")
            for ci in range(10):
                S.dma("sp", kTs_d[s, ci, :, 8192:8196], kvnb[:, KV_OF_CI[ci], 4 * s:4 * s + 4], reads=["kvnb"])
            S.dma("sp", kview[:, :, 8196:8320], zt, reads=["zt"])
            S.dma("sp", vs_d[s, 8192:8196, :], tokb[4 * s:4 * s + 4, 0:6, :].rearrange("p a b -> p (a b)"), reads=["tokb"])
            S.dma("sp", vs_d[s, 8196:8320, :], zt[:124, 0:7, :].rearrange("p a b -> p (a b)")[:, 0:768], reads=["zt"])
            for tt in range(4):
                t, tn = pg.next()
                S.dma("sp", t[:, 0:512], cwin_in[s, tt * 128:(tt + 1) * 128, :], writes=[tn])
                ks, ksn = kst.next()
                ps, pn = psrot.next()
                for g in range(2):
                    S.op("pe", lambda e, g=g: e.transpose(out=ps[:, g * 128:(g + 1) * 128], in_=t[:, g * 128:(g + 1) * 128], identity=ident),
                         reads=[tn, "ident"], writes=[pn])
                S.op("act", lambda e: e.activation(out=ks[:, 0:2, :].rearrange("p a b -> p (a b)"), in_=ps[:, :256], func=AF.Identity),
                     reads=[pn], writes=[ksn])
                S.dma("sp", wkTs_d[s].rearrange("c p n -> p c n")[:, :, tt * 128:(tt + 1) * 128], ks[:, 0:2, :], reads=[ksn])
                vs, vsn = vst.next()
                S.op("pool", lambda e: e.tensor_copy(out=vs[:, 0:256], in_=t[:, 256:512]), reads=[tn], writes=[vsn])
                S.dma("sp", wvs_d[s, tt * 128:(tt + 1) * 128, :], vs[:, 0:256], reads=[vsn])
            for g in range(2):
                S.dma("sp", wkTs_d[s, g, :, 512:516], kvnb[:, 16 + g, 4 * s:4 * s + 4], reads=["kvnb"])
            S.dma("sp", wkTs_d[s].rearrange("c p n -> p c n")[:, :, 516:640], zt[:, 0:2, :], reads=["zt"])
            S.dma("sp", wvs_d[s, 512:516, :], tokb[4 * s:4 * s + 4, 8:10, :].rearrange("p a b -> p (a b)"), reads=["tokb"])
            S.dma("sp", wvs_d[s, 516:640, :], zt[:124, 0:3, :].rearrange("p a b -> p (a b)")[:, 0:256], reads=["zt"])
            S.dma("sp", wins_o[s, 0:508, :], cwin_in[s, 4:512, :])
            S.dma("sp", wins_o[s, 508:512, :], tok32[4 * s:4 * s + 4, 6:10, :].rearrange("p a b -> p (a b)"), reads=["tok32"])
            for tt in range(2):
                t, tn = pg.next()
                S.dma("sp", t[:, 0:1024], cmem_in[s, tt * 128:(tt + 1) * 128, :], writes=[tn])
                ks, ksn = kst.next()
                ps, pn = psrot.next()
                for h in range(4):
                    S.op("pe", lambda e, h=h: e.transpose(out=ps[:, h * 128:(h + 1) * 128], in_=t[:, h * 128:(h + 1) * 128], identity=ident),
                         reads=[tn, "ident"], writes=[pn])
                S.op("act", lambda e: e.activation(out=ks[:, 0:4, :].rearrange("p a b -> p (a b)"), in_=ps, func=AF.Identity),
                     reads=[pn], writes=[ksn])
                S.dma("sp", mkTs_d[s].rearrange("c p n -> p c n")[:, :, tt * 128:(tt + 1) * 128], ks[:, 0:4, :], reads=[ksn])
                vs, vsn = vst.next()
                S.op("pool", lambda e: e.tensor_copy(out=vs[:, 0:512], in_=t[:, 512:1024]), reads=[tn], writes=[vsn])
                S.dma("sp", mvs_d[s, tt * 128:(tt + 1) * 128, :], vs[:, 0:512], reads=[vsn])
        S.barrier()

    def mkjobs():
        jobs = []
        seq = dict(kT=kT_d, v=v_d, mkT=mkT_d, mv=mv_d, wkT=wkT_d, wv=wv_d, kpos="seq", wposk="seq", n_cmp=255, n_blk=64, L=4096)
        jobs.append(dict(seq, name="J0", ji=0, c0=0, n=512, asubs=[(i * 128, 128, i) for i in range(4)],
                         sblks=[(i * 128, 128, i) for i in range(4)], nkt=16, win=[(0, 512, 0, 10)]))
        jobs.append(dict(seq, name="J1", ji=1, c0=512, n=512, asubs=[(i * 128, 128, 4 + i) for i in range(4)],
                         sblks=[(i * 128, 128, 4 + i) for i in range(4)], nkt=32, win=[(0, 512, 10, 10)]))
        jobs.append(dict(seq, name="H", ji=2, c0=1024, n=4, asubs=[(0, 2, 8), (2, 2, 9)], sblks=[(0, 4, 8)], nkt=32,
                         win=[(0, 2, 0, 10), (2, 2, 10, 10)]))
        for s in range(4):
            jobs.append(dict(kT=kTs_d[s], v=vs_d[s], mkT=mkTs_d[s], mv=mvs_d[s], wkT=wkTs_d[s], wv=wvs_d[s], kpos="smp", wposk="smp",
                             n_cmp=511, n_blk=129, L=8192, name=f"S{s}", ji=3 + s, c0=1028 + 4 * s, n=4,
                             asubs=[(0, 4, 10 + s)], sblks=[(0, 4, 9 + s)], nkt=65, win=[(0, 4, 0, 5)]))
        return jobs

    def phase3_attention():
        B = Bump(nc, [(0, SB_TOP)])
        ident = B.alloc([128, 128], F32)
        uinc32 = B.alloc([128, 128], F32)
        uinc = B.alloc([128, 128], BF16)
        ones_bf = B.alloc([128, 128], BF16)
        ones32 = B.alloc([128, 128], F32)
        S.dma("sp", ident, ident_in, writes=["ident"])
        S.dma("sp", uinc32, uinc_in, writes=["uinc32"])
        S.op("dve", lambda e: e.tensor_copy(out=uinc, in_=uinc32), reads=["uinc32"], writes=["uinc"])
        S.op("dve", lambda e: e.memset(ones_bf, 1.0), writes=["ones_bf"])
        S.op("dve", lambda e: e.memset(ones32, 1.0), writes=["ones32"])
        tabs = {}
        for nm, ap, shp in (("qpos_bc", qpos_bc_in, [128, NQ]), ("refs", refs_in, [128, 14]), ("qposT", qposT_in, [128, 13]),
                            ("curT", curT_in, [128, 13]), ("kpos_seq", kpos_seq_in, [128, 32]), ("kpos_smp", kpos_smp_in, [128, 65]),
                            ("wpos_seq", wpos_in, [128, 20]), ("wpos_smp", wpos_smp_in, [128, 5]), ("cend", cend_in, [128, 4]), ("cend_seq", cend_seq_in, [128, 2]),
                            ("blk", blk_in, [128, 2, 129]), ("ov", ov_in, [128, 4, 129])):
            t = B.alloc(shp, F32)
            S.dma("sp", t, ap, writes=[nm])
            tabs[nm] = t
        qpos_bc = tabs["qpos_bc"]
        Qsb = B.alloc([128, 16, NQ], BF16)
        for i in range(16):
            S.dma("sp", Qsb[:, i, :], qT_d[i], writes=[f"Q{i}"])
        gn = B.alloc([24, NQ], F32)
        S.dma("sp", gn, gn_d, writes=["gn"])
        sel24 = B.alloc([24, 24, 128], F32)
        S.op("pool", lambda e: e.memset(sel24, 0.0), writes=["sel24"])
        for c in range(24):
            pass
        S.op("dve", lambda e: e.tensor_tensor(out=sel24, in0=sel24, in1=ident[:24, 0:24].unsqueeze(2).to_broadcast([24, 24, 128]), op=ALU.add),
             reads=["sel24", "ident"], writes=["sel24"])
        oacc = B.alloc([128, 8, NQ], F32)
        halfones = B.alloc([8, 2, 128], F32)
        S.op("pool", lambda e: e.memset(halfones, 0.0), writes=["halfones"])
        S.op("pool", lambda e: e.memset(halfones[:, 0, 0:64], 1.0), reads=["halfones"], writes=["halfones"])
        S.op("pool", lambda e: e.memset(halfones[:, 1, 64:128], 1.0), reads=["halfones"], writes=["halfones"])
        w1 = B.alloc([128, 2, 32, 128], BF16)
        w2 = B.alloc([128, 2, 128], BF16)
        pe = B.alloc([128, 2, 32], BF16)
        for i in range(2):
            S.dma("poolq", w1[:, i], w1_in[i], writes=[f"w1_{i}"])
            S.dma("poolq", w2[:, i], w2_in[i], writes=[f"w2_{i}"])
            S.dma("poolq", pe[:, i], pe_in[i], writes=[f"pe_{i}"])
        cmp_cache = {}
        for g_ in range(2):
            cmp_cache[g_] = dict(kT=B.alloc([128, 256], BF16), v=B.alloc([128, 2, 128], BF16), done=False)
        mark = [list(r) for r in B.ranges]

        def kpos_tab(job):
            return tabs["kpos_seq"] if job["kpos"] == "seq" else tabs["kpos_smp"]

        def bias_table(Bj, job, kp, nkt, heads, tag):
            out = {}
            for (a0, an, ai) in job["asubs"]:
                for h in heads:
                    t = Bj.alloc([128, nkt], F32)
                    rn = f"bt_{tag}_{ai}_{h}"
                    S.op("dve", lambda e, t=t, ai=ai, h=h: e.tensor_scalar(out=t, in0=kp[:, :nkt], scalar1=tabs["refs"][:, ai:ai + 1],
                                                                         scalar2=SLOPES[h], op0=ALU.subtract, op1=ALU.mult),
                         reads=["refs", "kpos_seq", "kpos_smp", "wpos_seq", "wpos_smp", "cend", "cend_seq"], writes=[rn])
                    S.op("dve", lambda e, t=t: e.tensor_scalar(out=t, in0=t, scalar1=0.0, scalar2=None, op0=ALU.min),
                         reads=[rn], writes=[rn])
                    out[(ai, h)] = (t, rn)
            return out

        def softmax_head(Bj, job, cols, qchunk, kT_sb, kres, v_fn, vres, nkt, bias_fn, mask_fn, pools, head_slope_idx=None):
            c0, n = cols
            (Ob, On), (Db, Dn) = pools["acc"].next()
            LA = 3
            stl = {}

            def emit_S(k_):
                ps_, pn_ = psrot4.next()
                S.op("pe", lambda e: e.matmul(ps_[:, :n], lhsT=kT_sb[:, k_ * 128:(k_ + 1) * 128], rhs=Qsb[:, qchunk, c0:c0 + n],
                                              start=True, stop=True), reads=[kres, f"Q{qchunk}"], writes=[pn_])
                stl[k_] = (ps_, pn_)

            for k_ in range(min(LA, nkt)):
                emit_S(k_)
            for kt in range(nkt):
                if kt + LA < nkt:
                    emit_S(kt + LA)
                ps, pn = stl.pop(kt)
                pf, pfn = pools["pf"].next()
                asl = job["asubs"]
                if bias_fn is not None and len(asl) == 4 and head_slope_idx is not None:
                    gs = 1 if head_slope_idx == 0 else (2 if head_slope_idx == 1 else 4)
                    asl = [(asl[i][0], sum(x[1] for x in asl[i:i + gs]), asl[i + gs - 1][2]) for i in range(0, 4, gs)]
                for (a0, an, ai) in asl:
                    lo, hi = max(a0, c0 - job["c0"]), min(a0 + an, c0 - job["c0"] + n)
                    if lo >= hi:
                        continue
                    l0 = lo - (c0 - job["c0"])
                    if bias_fn is None:
                        S.op("act", lambda e: e.activation(out=pf[:, l0:l0 + hi - lo], in_=ps[:, l0:l0 + hi - lo], func=AF.Exp, scale=SCALE),
                             reads=[pn], writes=[pfn])
                    else:
                        bt, brn = bias_fn(ai)
                        S.op("act", lambda e: e.activation(out=pf[:, l0:l0 + hi - lo], in_=ps[:, l0:l0 + hi - lo], func=AF.Exp, scale=SCALE,
                                                           bias=bt[:, kt:kt + 1]), reads=[pn, brn], writes=[pfn])
                if mask_fn is not None:
                    mk, mkn = mask_fn(kt)
                    pm, pmn = pools["pm"].next()
                    S.op("pool" if (kt % 4 == 3) else "dve", lambda e: e.tensor_tensor(out=pm[:, :n], in0=pf[:, :n], in1=mk, op=ALU.mult),
                         reads=[pfn, mkn], writes=[pmn])
                else:
                    pm, pmn = pf, pfn
                vt = v_fn(kt)
                S.op("pe", lambda e: e.matmul(Ob[:, :n], lhsT=vt, rhs=pm[:, :n], start=(kt == 0), stop=(kt == nkt - 1)),
                     reads=[vres, pmn], writes=[On])
                S.op("pe", lambda e: e.matmul(Db[:, :n], lhsT=ones_bf, rhs=pm[:, :n], start=(kt == 0), stop=(kt == nkt - 1)),
                     reads=["ones_bf", pmn], writes=[Dn])
            return (Ob, On), (Db, Dn)

        def normalize(Bj, cols, O, Dn, pools):
            (Ob, On), (Db, Dnn) = O, Dn
            c0, n = cols
            rd, rdn = pools["rd"].next()
            S.op("dve", lambda e: e.tensor_scalar(out=rd[:, :n], in0=Db[:, :n], scalar1=1e-30, scalar2=None, op0=ALU.max),
                 reads=[Dnn], writes=[rdn])
            S.op("dve", lambda e: e.reciprocal(out=rd[:, :n], in_=rd[:, :n]), reads=[rdn], writes=[rdn])
            t, tn_ = pools["nt"].next()
            S.op("dve", lambda e: e.tensor_tensor(out=t[:, :n], in0=Ob[:, :n], in1=rd[:, :n], op=ALU.mult), reads=[On, rdn], writes=[tn_])
            return t, tn_, rd, rdn

        def gated_acc(cols, h, br, t, tn_, first, toff=0):
            c0, n = cols
            t = t[:, toff:]
            ps, pn = psrot4.next()
            c = br * 8 + h
            S.op("pe", lambda e: e.matmul(ps[:, :n], lhsT=sel24[:, c, :], rhs=gn[:, c0:c0 + n], start=True, stop=True),
                 reads=["sel24", "gn"], writes=[pn])
            if first:
                S.op("dve", lambda e: e.tensor_tensor(out=oacc[:, h, c0:c0 + n], in0=t[:, :n], in1=ps[:, :n], op=ALU.mult),
                     reads=[tn_, pn], writes=[f"oacc{h}"])
            else:
                S.op("dve", lambda e: e.tensor_tensor(out=t[:, :n], in0=t[:, :n], in1=ps[:, :n], op=ALU.mult), reads=[tn_, pn], writes=[tn_])
                S.op("pool", lambda e: e.tensor_tensor(out=oacc[:, h, c0:c0 + n], in0=oacc[:, h, c0:c0 + n], in1=t[:, :n], op=ALU.add),
                     reads=[tn_, f"oacc{h}"], writes=[f"oacc{h}"])

        def small_group(Bs, job, cols, g, br, kT_sb, kres, v_fn, vres, nkt, kp_ap, mask_all, mres):
            c0, n = cols
            W = 4 * n
            pools_ = std_pools(Bs, W)
            BT = Bs.alloc([128, nkt, 4, n], F32)
            for hi in range(4):
                for q in range(n):
                    S.op("dve", lambda e: e.tensor_scalar(out=BT[:, :, hi, q], in0=kp_ap, scalar1=qpos_bc[:, c0 + q:c0 + q + 1], scalar2=SLOPES[4 * g + hi] / SCALE,
                                                          op0=ALU.subtract, op1=ALU.mult),
                         reads=["qpos_bc", "kpos_seq", "kpos_smp", "wpos_seq", "wpos_smp"], writes=["BT"])
            BT2 = BT.rearrange("p k h q -> p (k h q)")
            S.op("dve", lambda e: e.tensor_scalar(out=BT2, in0=BT2, scalar1=0.0, scalar2=None, op0=ALU.min), reads=["BT"], writes=["BT"])
            (Ob, On), (Db, Dn) = pools_["acc"].next()
            argr = Bs.rot(3, [128, W], F32, "sarg")
            pr = Bs.rot(3, [128, W], BF16, "spp")
            pmr = Bs.rot(3, [128, W], BF16, "spm")
            LA = 2
            stl = {}

            def emit_S(k_):
                ps_, pn_ = psrot4.next()
                for hi in range(4):
                    S.op("pe", lambda e: e.matmul(ps_[:, hi * n:(hi + 1) * n], lhsT=kT_sb[:, k_ * 128:(k_ + 1) * 128], rhs=Qsb[:, 4 * g + hi, c0:c0 + n],
                                                  start=True, stop=True), reads=[kres, f"Q{4 * g + hi}"], writes=[pn_])
                stl[k_] = (ps_, pn_)

            for k_ in range(min(LA, nkt)):
                emit_S(k_)
            for kt in range(nkt):
                if kt + LA < nkt:
                    emit_S(kt + LA)
                ps, pn = stl.pop(kt)
                ar, arn = argr.next()
                S.op("dve", lambda e: e.tensor_tensor(out=ar, in0=ps[:, :W], in1=BT[:, kt].rearrange("p h q -> p (h q)"), op=ALU.add),
                     reads=[pn, "BT"], writes=[arn])
                p_, ppn = pr.next()
                S.op("act", lambda e: e.activation(out=p_, in_=ar, func=AF.Exp, scale=SCALE), reads=[arn], writes=[ppn])
                if mask_all is not None:
                    pm, pmn = pmr.next()
                    S.op("dve", lambda e: e.tensor_tensor(out=pm.rearrange("p (h q) -> p h q", h=4), in0=p_.rearrange("p (h q) -> p h q", h=4),
                                                           in1=mask_all[:, kt, :n].unsqueeze(1).to_broadcast([128, 4, n]), op=ALU.mult),
                         reads=[ppn, mres], writes=[pmn])
                else:
                    pm, pmn = p_, ppn
                vt = v_fn(kt)
                S.op("pe", lambda e: e.matmul(Ob[:, :W], lhsT=vt, rhs=pm, start=(kt == 0), stop=(kt == nkt - 1)), reads=[vres, pmn], writes=[On])
                S.op("pe", lambda e: e.matmul(Db[:, :W], lhsT=ones_bf, rhs=pm, start=(kt == 0), stop=(kt == nkt - 1)), reads=["ones_bf", pmn], writes=[Dn])
            t, tn_, _, _ = normalize(Bs, (c0, W), (Ob, On), (Db, Dn), pools_)
            for hi in range(4):
                gated_acc(cols, 4 * g + hi, br, t, tn_, first=False, toff=hi * n)

        def std_pools(Bj, n):
            nn = max(n, 8)
            return dict(
                acc=Rot([((PSA[:, 4, :], "ps4"), (PSA[:, 5, :], "ps5")), ((PSA[:, 6, :], "ps6"), (PSA[:, 7, :], "ps7"))]),
                pf=Bj.rot(3, [128, nn], BF16, "pf"), pm=Bj.rot(3, [128, nn], BF16, "pm"),
                rd=Bj.rot(2, [128, nn], F32, "rd"), nt=Bj.rot(3, [128, nn], F32, "nt"))

        def mem_branch(job):
            S.barrier()
            Bj = Bump(nc, [list(r) for r in mark])
            n = job["n"]
            cols = (job["c0"], n)
            pools = std_pools(Bj, n)
            kT = Bj.alloc([128, 4, 256], BF16)
            vv = Bj.alloc([128, 2, 512], BF16)
            S.dma("sp", kT, job["mkT"].rearrange("c p n -> p c n"), writes=["mkT"])
            S.dma("sp", vv, job["mv"].rearrange("(t p) n -> p t n", p=128), writes=["mv"])
            ost = Bj.rot(2, [128, max(n, 8)], BF16, "ost")
            for h in range(4):
                O, Dn = softmax_head(Bj, job, cols, 12 + h, kT[:, h, :], "mkT", lambda kt: vv[:, kt, h * 128:(h + 1) * 128], "mv", 2,
                                     None, None, pools)
                t, tn_, _, _ = normalize(Bj, cols, O, Dn, pools)
                o, on = ost.next()
                S.op("act", lambda e: e.activation(out=o[:, :n], in_=t[:, :n], func=AF.Identity), reads=[tn_], writes=[on])
                S.dma("sp", oT_d[12 + h, :, cols[0]:cols[0] + n], o[:, :n], reads=[on])

        psrot6 = Rot([(PS[i], f"ps{i}") for i in range(6)])

        def sb_steps(n, nkt, nch, load_fn, z_fn, pv_fn, mask_fn, Bj):
            nn = max(n, 8)
            f32r = {k: Bj.rot(2 * nch, [128, nn], F32, k) for k in ("e", "sp", "zs", "arg", "a")}
            b16r = {k: Bj.rot(2 * nch, [128, nn], BF16, k) for k in ("spm", "am")}
            mr = Bj.rot(3, [128, nn], BF16, "m")
            carry = [Bj.alloc([128, nn], F32) for _ in range(nch)]
            for ch in range(nch):
                S.op("pool", lambda e, ch=ch: e.memset(carry[ch], 0.0), writes=[f"carry{ch}"])
            order = list(range(nkt - 1, -1, -1))
            zt = {}
            la = 1 if nch == 1 else 0
            zrot = Rot([(PS[0], "ps0"), (PS[1], "ps1")])
            crot = Rot([(PS[i], f"ps{i}") for i in (2, 3, 4, 5)])

            def stageA(i):
                kt = order[i]
                load_fn(kt)
                m, mn = mr.next()
                mask_fn(kt, m, mn)
                zs_ = []
                for ch in range(nch):
                    zp, zpn = zrot.next()
                    z_fn(ch, kt, zp, zpn)
                    zs_.append((zp, zpn))
                zt[i] = (m, mn, zs_)

            if la:
                stageA(0)
            for i in range(nkt):
                if la:
                    if i + 1 < nkt:
                        stageA(i + 1)
                else:
                    stageA(i)
                kt = order[i]
                m, mn, zs_ = zt.pop(i)
                st = [dict() for _ in range(nch)]
                for ch in range(nch):
                    zp, zpn = zs_[ch]
                    ee, een = f32r["e"].next()
                    S.op("act", lambda e: e.activation(out=ee[:, :n], in_=zp[:, :n], func=AF.Exp, scale=SCALE), reads=[zpn], writes=[een])
                    st[ch].update(ee=ee, een=een, zp=zp, zpn=zpn)
                for ch in range(nch):
                    d = st[ch]
                    sp, spn = f32r["sp"].next()
                    S.op("act", lambda e: e.activation(out=sp[:, :n], in_=d["ee"][:, :n], func=AF.Ln, bias=1.0, scale=1.0), reads=[d["een"]], writes=[spn])
                    d.update(sp=sp, spn=spn)
                for ch in range(nch):
                    d = st[ch]
                    zs, zsn = f32r["zs"].next()
                    S.op("act", lambda e: e.activation(out=zs[:, :n], in_=d["zp"][:, :n], func=AF.Identity, scale=SCALE), reads=[d["zpn"]], writes=[zsn])
                    d.update(zs=zs, zsn=zsn)
                for ch in range(nch):
                    d = st[ch]
                    spm, spmn = b16r["spm"].next()
                    S.op("dve", lambda e: e.tensor_tensor(out=spm[:, :n], in0=d["sp"][:, :n], in1=m[:, :n], op=ALU.mult), reads=[d["spn"], mn], writes=[spmn])
                    d.update(spm=spm, spmn=spmn)
                for ch in range(nch):
                    d = st[ch]
                    cp, cpn = crot.next()
                    S.op("pe", lambda e: e.matmul(cp[:, :n], lhsT=uinc, rhs=d["spm"][:, :n], start=True, stop=True), reads=["uinc", d["spmn"]], writes=[cpn])
                    tp, tpn = crot.next()
                    S.op("pe", lambda e: e.matmul(tp[:, :n], lhsT=ones_bf, rhs=d["spm"][:, :n], start=True, stop=True), reads=["ones_bf", d["spmn"]], writes=[tpn])
                    d.update(cp=cp, cpn=cpn, tp=tp, tpn=tpn)
                for ch in range(nch):
                    d = st[ch]
                    ar, arn = f32r["arg"].next()
                    S.op("dve", lambda e: e.tensor_tensor(out=ar[:, :n], in0=d["zs"][:, :n], in1=d["cp"][:, :n], op=ALU.subtract), reads=[d["zsn"], d["cpn"]], writes=[arn])
                    d.update(ar=ar, arn=arn)
                for ch in range(nch):
                    d = st[ch]
                    S.op("dve", lambda e: e.tensor_tensor(out=d["ar"][:, :n], in0=d["ar"][:, :n], in1=carry[ch][:, :n], op=ALU.subtract),
                         reads=[d["arn"], f"carry{ch}"], writes=[d["arn"]])
                for ch in range(nch):
                    d = st[ch]
                    S.op("dve", lambda e: e.tensor_tensor(out=carry[ch][:, :n], in0=carry[ch][:, :n], in1=d["tp"][:, :n], op=ALU.add),
                         reads=[f"carry{ch}", d["tpn"]], writes=[f"carry{ch}"])
                for ch in range(nch):
                    d = st[ch]
                    aa, aan = f32r["a"].next()
                    S.op("act", lambda e: e.activation(out=aa[:, :n], in_=d["ar"][:, :n], func=AF.Exp), reads=[d["arn"]], writes=[aan])
                    d.update(aa=aa, aan=aan)
                for ch in range(nch):
                    d = st[ch]
                    am, amn = b16r["am"].next()
                    S.op("dve" if n > 64 else "pool", lambda e: e.tensor_tensor(out=am[:, :n], in0=d["aa"][:, :n], in1=m[:, :n], op=ALU.mult), reads=[d["aan"], mn], writes=[amn])
                    d.update(am=am, amn=amn)
                for ch in range(nch):
                    d = st[ch]
                    pv_fn(ch, kt, d["am"], d["amn"], i == 0, i == nkt - 1)

        def sb_branch(job):
            n, c0, nkt = job["n"], job["c0"], job["nkt"]
            nn = max(n, 8)
            kp = kpos_tab(job)
            Lp = nkt * 128
            for h0 in (0, 2):
                S.barrier()
                Bj = Bump(nc, [list(r) for r in mark])
                kT = [Bj.alloc([128, Lp], BF16) for _ in range(2)]
                vv = [Bj.alloc([128, nkt, 128], BF16) for _ in range(2)]
                ost = Bj.rot(2, [128, nn], BF16, "ost")
                Ob = [(PS[6], "ps6"), (PS[7], "ps7")]
                for ch in range(2):
                    h = h0 + ch
                    S.dma("sp", kT[ch], job["kT"][6 + h, :, 0:Lp], writes=[f"sbk{ch}"])
                    S.dma("sp", vv[ch], job["v"][0:Lp, 256 + h * 128:256 + (h + 1) * 128].rearrange("(t p) n -> p t n", p=128), writes=[f"sbv{ch}"])

                def z_fn(ch, kt, zp, zpn):
                    S.op("pe", lambda e: e.matmul(zp[:, :n], lhsT=kT[ch][:, kt * 128:(kt + 1) * 128], rhs=Qsb[:, 8 + h0 + ch, c0:c0 + n], start=True, stop=True),
                         reads=[f"sbk{ch}", f"Q{8 + h0 + ch}"], writes=[zpn])

                def pv_fn(ch, kt, am, amn, first, last):
                    S.op("pe", lambda e: e.matmul(Ob[ch][0][:, :n], lhsT=vv[ch][:, kt, :], rhs=am[:, :n], start=first, stop=last),
                         reads=[f"sbv{ch}", amn], writes=[Ob[ch][1]])

                def mask_fn(kt, m, mn):
                    S.op("pool", lambda e: e.tensor_scalar(out=m[:, :n], in0=qpos_bc[:, c0:c0 + n], scalar1=kp[:, kt:kt + 1], scalar2=None,
                                                           op0=ALU.is_gt), reads=["qpos_bc", "kpos_seq", "kpos_smp"], writes=[mn])

                sb_steps(n, nkt, 2, lambda kt: None, z_fn, pv_fn, mask_fn, Bj)
                for ch in range(2):
                    h = h0 + ch
                    o, on = ost.next()
                    S.op("act", lambda e: e.activation(out=o[:, :n], in_=Ob[ch][0][:, :n], func=AF.Identity), reads=[Ob[ch][1]], writes=[on])
                    S.dma("sp", oT_d[8 + h, :, c0:c0 + n], o[:, :n], reads=[on])

        def sb_samples():
            S.barrier()
            Bj = Bump(nc, [list(r) for r in mark])
            n, nkt = 32, 65
            kp = tabs["kpos_smp"]
            qrow = Bj.alloc([128, 4, 4, 4], F32)
            S.op("dve", lambda e: e.tensor_copy(out=qrow, in_=qpos_bc[:, 1028:1044].rearrange("p (s q) -> p s q", s=4).unsqueeze(2).to_broadcast([128, 4, 4, 4])),
                 reads=["qpos_bc"], writes=["qrow"])
            qrow2 = qrow.rearrange("p s h q -> p (s h q)")
            ktr = Bj.rot(3, [128, 4, 4, 128], BF16, "sbkt")
            vtr = Bj.rot(3, [128, 4, 512], BF16, "sbvt")
            cur = {}
            Obs = [(PS[6], "ps6"), (PS[7], "ps7")]

            def load_fn(kt):
                kt_, ktn = ktr.next()
                vt_, vtn = vtr.next()
                for s_ in range(4):
                    S.dma("sp", kt_[:, s_], kTs_d[s_, 6:10, :, kt * 128:(kt + 1) * 128].rearrange("c p n -> p c n"), writes=[ktn + f"_{s_}"])
                S.dma("sp", vt_, vs_d[:, kt * 128:(kt + 1) * 128, 256:768].rearrange("s p n -> p s n"), writes=[vtn])
                cur[kt] = (kt_, ktn, vt_, vtn)

            def z_fn(ch, kt, zp, zpn):
                kt_, ktn, _, _ = cur[kt]
                for s in (2 * ch, 2 * ch + 1):
                    for h in range(4):
                        c = ((s - 2 * ch) * 4 + h) * 4
                        S.op("pe", lambda e: e.matmul(zp[:, c:c + 4], lhsT=kt_[:, s, h, :], rhs=Qsb[:, 8 + h, 1028 + 4 * s:1032 + 4 * s], start=True, stop=True),
                             reads=[ktn + f"_{s}", f"Q{8 + h}"], writes=[zpn])

            def pv_fn(ch, kt, am, amn, first, last):
                _, _, vt_, vtn = cur[kt]
                if ch == 1:
                    cur.pop(kt)
                Ob, On = Obs[ch]
                k = 0
                for s in (2 * ch, 2 * ch + 1):
                    for h in range(4):
                        c = ((s - 2 * ch) * 4 + h) * 4
                        S.op("pe", lambda e: e.matmul(Ob[:, c:c + 4], lhsT=vt_[:, s, h * 128:(h + 1) * 128], rhs=am[:, c:c + 4],
                                                      start=(first and k == 0), stop=last), reads=[vtn, amn], writes=[On])
                        k += 1

            def mask_fn(kt, m, mn):
                S.op("pool", lambda e: e.tensor_scalar(out=m[:, :n], in0=qrow2[:, 0:32], scalar1=kp[:, kt:kt + 1], scalar2=None, op0=ALU.is_gt),
                     reads=["qrow", "kpos_smp"], writes=[mn])

            sb_steps(n, nkt, 2, load_fn, z_fn, pv_fn, mask_fn, Bj)
            o = Bj.alloc([128, 4, 4, 4], BF16)
            o2 = o.rearrange("p s h q -> p (s h q)")
            for ch in range(2):
                S.op("act", lambda e: e.activation(out=o2[:, 32 * ch:32 * ch + 32], in_=Obs[ch][0][:, :32], func=AF.Identity), reads=[Obs[ch][1]], writes=["sbo"])
            for h in range(4):
                S.dma("sp", oT_d[8 + h, :, 1028:1044].rearrange("p (s q) -> p s q", s=4), o[:, :, h, :], reads=["sbo"])

        def gelu_tanh(t, out_bf, x, xn, n, onm):
            S.op("act", lambda e: e.activation(out=t, in_=x, func=AF.Square), reads=[xn], writes=["gl_t"])
            S.op("dve", lambda e: e.tensor_scalar(out=t, in0=t, scalar1=0.044715, scalar2=1.0, op0=ALU.mult, op1=ALU.add), reads=["gl_t"], writes=["gl_t"])
            S.op("dve", lambda e: e.tensor_tensor(out=t, in0=t, in1=x, op=ALU.mult), reads=["gl_t", xn], writes=["gl_t"])
            S.op("act", lambda e: e.activation(out=t, in_=t, func=AF.Sigmoid, scale=GELU_C), reads=["gl_t"], writes=["gl_t"])
            S.op("dve", lambda e: e.tensor_tensor(out=out_bf, in0=t, in1=x, op=ALU.mult), reads=["gl_t", xn], writes=[onm])

        def nsa_branch(job):
            S.barrier()
            Bj = Bump(nc, [list(r) for r in mark])
            n, c0, nkt = job["n"], job["c0"], job["nkt"]
            nn = max(n, 8)
            cols = (c0, n)
            ncmp, nblk, L = job["n_cmp"], job["n_blk"], job["L"]
            nct = (ncmp + 127) // 128
            ncp = nct * 128
            kp = kpos_tab(job)
            pools = std_pools(Bj, n)
            small = n <= 8
            selp = Bj.alloc([8, 2, 132], F32)
            if small:
                S.op("pool", lambda e: e.memset(selp, 0.0), writes=["selp"])
            cend_t = tabs["cend_seq"] if job["kpos"] == "seq" else tabs["cend"]
            marks = [list(r) for r in Bj.ranges]
            for g in range(2):
                S.barrier()
                Bg = Bump(nc, [list(r) for r in marks])
                heads = [4 * g + i for i in range(4)]
                kcT = Bg.alloc([128, L], BF16)
                shared = job["kpos"] == "seq"
                if shared:
                    kcmpT, vcmp = cmp_cache[g]["kT"], cmp_cache[g]["v"]
                else:
                    kcmpT = Bg.alloc([128, ncp], BF16)
                    vcmp = Bg.alloc([128, nct, 128], BF16)
                do_cmp = (not shared) or (not cmp_cache[g]["done"])
                if shared:
                    cmp_cache[g]["done"] = True
                if do_cmp:
                    S.op("pool", lambda e: e.memset(kcmpT, 0.0), writes=["kcmpT"])
                    S.op("pool", lambda e: e.memset(vcmp, 0.0), writes=["vcmp"])
                hid32 = Bg.alloc([128, ncp], F32)
                hidb = Bg.alloc([128, ncp], BF16)
                cvec = Bg.alloc([128, 1], F32)
                glt = Bg.alloc([128, ncp], F32)
                S.op("pool", lambda e: e.memset(hid32, 0.0), writes=["hid32"])
                for kv in (range(2) if do_cmp else ()):
                    S.dma("sp", kcT, job["kT"][2 * kv + g, :, 0:L], writes=["kcT"])
                    ps, pn = psrot4.next()
                    for j in range(32):
                        S.op("pe", lambda e, j=j: e.matmul(ps[:, 0:1], lhsT=w1[:, kv, j, :], rhs=pe[:, kv, j:j + 1], start=(j == 0), stop=(j == 31)),
                             reads=[f"w1_{kv}", f"pe_{kv}"], writes=[pn])
                    S.op("act", lambda e: e.activation(out=cvec, in_=ps[:, 0:1], func=AF.Identity), reads=[pn], writes=["cvec"])
                    for c1 in range(0, ncmp, 512):
                        cn = min(512, ncmp - c1)
                        ps, pn = psrot4.next()
                        for j in range(32):
                            S.op("pe", lambda e, j=j: e.matmul(ps[:, :cn], lhsT=w1[:, kv, j, :],
                                                               rhs=kcT[:, 16 * c1 + j:16 * c1 + j + 16 * (cn - 1) + 1:16],
                                                               start=(j == 0), stop=(j == 31)), reads=[f"w1_{kv}", "kcT"], writes=[pn])
                        S.op("act", lambda e: e.activation(out=hid32[:, c1:c1 + cn], in_=ps[:, :cn], func=AF.Identity, bias=cvec[:, 0:1], scale=1.0),
                             reads=[pn, "cvec"], writes=["hid32"])
                    gelu_tanh(glt, hidb, hid32, "hid32", ncp, "hidb")
                    if kv == 0:
                        for c1 in range(0, ncmp, 512):
                            cn = min(512, ncmp - c1)
                            ps, pn = psrot4.next()
                            S.op("pe", lambda e: e.matmul(ps[:, :cn], lhsT=w2[:, 0, :], rhs=hidb[:, c1:c1 + cn], start=True, stop=True),
                                 reads=["w2_0", "hidb"], writes=[pn])
                            S.op("act", lambda e: e.activation(out=kcmpT[:, c1:c1 + cn], in_=ps[:, :cn], func=AF.Identity), reads=[pn], writes=["kcmpT"])
                    else:
                        for ct in range(nct):
                            cn = min(128, ncmp - ct * 128)
                            ps, pn = psrot4.next()
                            S.op("pe", lambda e: e.matmul(ps[:cn, :128], lhsT=hidb[:, ct * 128:ct * 128 + cn], rhs=w2[:, 1, :], start=True, stop=True),
                                 reads=["w2_1", "hidb"], writes=[pn])
                            S.op("act", lambda e: e.activation(out=vcmp[:cn, ct, :], in_=ps[:cn, :128], func=AF.Identity), reads=[pn], writes=["vcmp"])
                bt_c = bias_table(Bg, job, cend_t, nct, heads, "c")
                pn32 = Bg.alloc([128, nct, 4, nn], F32)
                cmask = Bg.alloc([128, nct, nn], F32)
                for ct in range(nct):
                    S.op("dve", lambda e: e.tensor_scalar(out=cmask[:, ct, :n], in0=qpos_bc[:, c0:c0 + n], scalar1=cend_t[:, ct:ct + 1],
                                                          scalar2=None, op0=ALU.is_ge), reads=["qpos_bc", "cend", "cend_seq"], writes=["cmask"])
                pf32r = Bg.rot(2, [128, nn], F32, "pf32")
                pmb = Bg.rot(3, [128, nn], BF16, "pmb")
                for hi, h in enumerate(heads):
                    (Ob, On), (Db, Dn) = pools["acc"].next()
                    for ct in range(nct):
                        ps, pn = psrot4.next()
                        S.op("pe", lambda e: e.matmul(ps[:, :n], lhsT=kcmpT[:, ct * 128:(ct + 1) * 128], rhs=Qsb[:, h, c0:c0 + n], start=True, stop=True),
                             reads=["kcmpT", f"Q{h}"], writes=[pn])
                        pf, pfn = pf32r.next()
                        for (a0, an, ai) in job["asubs"]:
                            bt, brn = bt_c[(ai, h)]
                            S.op("act", lambda e: e.activation(out=pf[:, a0:a0 + an], in_=ps[:, a0:a0 + an], func=AF.Exp, scale=SCALE,
                                                               bias=bt[:, ct:ct + 1]), reads=[pn, brn], writes=[pfn])
                        S.op("dve", lambda e: e.tensor_tensor(out=pn32[:, ct, hi, :n], in0=pf[:, :n], in1=cmask[:, ct, :n], op=ALU.mult),
                             reads=[pfn, "cmask"], writes=[f"pn32_{hi}"])
                        pb, pbn = pmb.next()
                        S.op("pool", lambda e: e.tensor_copy(out=pb[:, :n], in_=pn32[:, ct, hi, :n]), reads=[f"pn32_{hi}"], writes=[pbn])
                        S.op("pe", lambda e: e.matmul(Ob[:, :n], lhsT=vcmp[:, ct, :], rhs=pb[:, :n], start=(ct == 0), stop=(ct == nct - 1)),
                             reads=["vcmp", pbn], writes=[On])
                        S.op("pe", lambda e: e.matmul(Db[:, :n], lhsT=ones_bf, rhs=pb[:, :n], start=(ct == 0), stop=(ct == nct - 1)),
                             reads=["ones_bf", pbn], writes=[Dn])
                    t, tn_, rd, rdn = normalize(Bg, cols, (Ob, On), (Db, Dn), pools)
                    gated_acc(cols, h, 0, t, tn_, first=True)
                    for ct in range(nct):
                        S.op("dve", lambda e: e.tensor_tensor(out=pn32[:, ct, hi, :n], in0=pn32[:, ct, hi, :n], in1=rd[:, :n], op=ALU.mult),
                             reads=[f"pn32_{hi}", rdn], writes=[f"pn32_{hi}"])
                selTsb = Bg.alloc([128, 2, nn], BF16)
                S.op("pool", lambda e: e.memset(selTsb, 0.0), writes=["selTsb"])
                for (b0, bn, bi) in job["sblks"]:
                    ps, pn = psrot4.next()
                    k = 0
                    for hi in range(4):
                        for ct in range(nct):
                            S.op("pe", lambda e, k=k: e.matmul(ps[:bn, :nblk], lhsT=pn32[:, ct, hi, b0:b0 + bn], rhs=tabs["ov"][:, ct, :nblk],
                                                               start=(k == 0), stop=(k == 4 * nct - 1)), reads=[f"pn32_{hi}", "ov"], writes=[pn])
                            k += 1
                    sc = Bg.alloc([128, nblk], F32)
                    vis = Bg.alloc([128, nblk], F32)
                    ff = Bg.alloc([128, nblk], F32)
                    f2 = Bg.alloc([128, nblk], F32)
                    m8 = Bg.alloc([128, 8], F32)
                    sc2 = Bg.alloc([128, nblk], F32)
                    qpc = tabs["qposT"][:bn, bi:bi + 1]
                    cur = tabs["curT"][:bn, bi:bi + 1]
                    bs, bj = tabs["blk"][:bn, 0, :nblk], tabs["blk"][:bn, 1, :nblk]
                    S.op("dve", lambda e: e.tensor_scalar(out=vis[:bn], in0=bs, scalar1=qpc, scalar2=None, op0=ALU.is_le), reads=["blk", "qposT"], writes=["vis"])
                    S.op("dve", lambda e: e.tensor_tensor(out=sc[:bn], in0=ps[:bn, :nblk], in1=vis[:bn], op=ALU.mult), reads=[pn, "vis"], writes=["sc"])
                    S.op("dve", lambda e: e.tensor_scalar(out=vis[:bn], in0=vis[:bn], scalar1=-1.0, scalar2=None, op0=ALU.add), reads=["vis"], writes=["vis"])
                    S.op("dve", lambda e: e.tensor_tensor(out=sc[:bn], in0=sc[:bn], in1=vis[:bn], op=ALU.add), reads=["sc", "vis"], writes=["sc"])
                    S.op("dve", lambda e: e.tensor_scalar(out=ff[:bn], in0=bj, scalar1=cur, scalar2=-1.0, op0=ALU.subtract, op1=ALU.is_ge),
                         reads=["blk", "curT"], writes=["ff"])
                    S.op("dve", lambda e: e.tensor_scalar(out=f2[:bn], in0=bj, scalar1=cur, scalar2=0.0, op0=ALU.subtract, op1=ALU.is_le),
                         reads=["blk", "curT"], writes=["f2"])
                    S.op("dve", lambda e: e.tensor_tensor(out=ff[:bn], in0=ff[:bn], in1=f2[:bn], op=ALU.mult), reads=["ff", "f2"], writes=["ff"])
                    S.op("dve", lambda e: e.memset(ff[:bn, 0:1], 1.0), reads=["ff"], writes=["ff"])
                    S.op("dve", lambda e: e.scalar_tensor_tensor(out=sc[:bn], in0=ff[:bn], scalar=2e9, in1=sc[:bn], op0=ALU.mult, op1=ALU.add),
                         reads=["ff", "sc"], writes=["sc"])
                    S.op("dve", lambda e: e.max(out=m8[:bn], in_=sc[:bn]), reads=["sc"], writes=["m8"])
                    S.op("dve", lambda e: e.match_replace(out=sc2[:bn], in_to_replace=m8[:bn], in_values=sc[:bn], imm_value=-3e9),
                         reads=["sc", "m8"], writes=["sc2"])
                    S.op("dve", lambda e: e.max(out=m8[:bn], in_=sc2[:bn]), reads=["sc2"], writes=["m8"])
                    S.op("dve", lambda e: e.tensor_scalar(out=sc2[:bn], in0=sc[:bn], scalar1=m8[:bn, 7:8], scalar2=None, op0=ALU.is_ge),
                         reads=["sc", "m8"], writes=["sc2"])
                    S.op("dve", lambda e: e.tensor_scalar(out=sc[:bn], in0=sc[:bn], scalar1=0.0, scalar2=None, op0=ALU.is_ge), reads=["sc"], writes=["sc"])
                    S.op("dve", lambda e: e.tensor_tensor(out=sc[:bn], in0=sc[:bn], in1=sc2[:bn], op=ALU.mult), reads=["sc", "sc2"], writes=["sc"])
                    if small:
                        S.op("dve", lambda e: e.tensor_copy(out=selp[:bn, g, :nblk], in_=sc[:bn]), reads=["sc", "selp"], writes=["selp"])
                        continue
                    for jt in range((nblk + 127) // 128):
                        jn = min(128, nblk - jt * 128)
                        pt_, ptn = psrot4.next()
                        S.op("pe", lambda e: e.transpose(out=pt_[:jn, :bn], in_=sc[:bn, jt * 128:jt * 128 + jn], identity=ident[:bn, :bn]),
                             reads=["sc", "ident"], writes=[ptn])
                        S.op("act", lambda e: e.activation(out=selTsb[:jn, jt, b0:b0 + bn], in_=pt_[:jn, :bn], func=AF.Identity),
                             reads=[ptn], writes=["selTsb"])
                if not small:
                    S.dma("sp", selT_d[job["ji"], g, 0:min(128, nblk), 0:n], selTsb[:min(128, nblk), 0, :n], reads=["selTsb"], writes=["selT_d"])
                if nblk > 128 and not small:
                    S.dma("sp", selT_d[job["ji"], g, 128:nblk, 0:n], selTsb[:nblk - 128, 1, :n], reads=["selTsb"], writes=["selT_d"])
                S.barrier()
                Bs = Bump(nc, [list(r) for r in marks])
                Lp = nkt * 128
                kT = Bs.alloc([128, Lp], BF16)
                vv = Bs.alloc([128, nkt, 128], BF16)
                S.dma("sp", kT, job["kT"][4 + g, :, 0:Lp], writes=["slck"])
                S.dma("sp", vv, job["v"][0:Lp, g * 128:(g + 1) * 128].rearrange("(t p) n -> p t n", p=128), writes=["slcv"])
                masks = Bs.alloc([128, nkt, nn], BF16)
                selb_r = Bs.rot(3, [128, nn], BF16, "selb")
                cm_r = Bs.rot(3, [128, nn], BF16, "cm")
                if small:
                    rhsd = Bs.alloc([8, 2, nkt, n], F32)
                    for a in range(2):
                        S.op("dve", lambda e: e.tensor_tensor(out=rhsd[:n, a], in0=selp[:n, g, a:a + 2 * nkt:2].unsqueeze(2).to_broadcast([n, nkt, n]),
                                                              in1=ident[:n, :n].unsqueeze(1).to_broadcast([n, nkt, n]), op=ALU.mult),
                             reads=["selp", "ident"], writes=["rhsd"])
                    mps, mpn = psrot4.next()
                    for a in range(2):
                        S.op("pe", lambda e: e.matmul(mps[:, :nkt * n], lhsT=halfones[:n, a, :], rhs=rhsd[:n, a].rearrange("p k q -> p (k q)"),
                                                      start=(a == 0), stop=(a == 1)), reads=["halfones", "rhsd"], writes=[mpn])
                    cmf = Bs.alloc([128, nkt, n], F32)
                    for q in range(n):
                        S.op("dve", lambda e: e.tensor_scalar(out=cmf[:, :, q], in0=kp[:, :nkt], scalar1=qpos_bc[:, c0 + q:c0 + q + 1], scalar2=None,
                                                              op0=ALU.is_le), reads=["qpos_bc", "kpos_seq", "kpos_smp"], writes=["cmf"])
                    S.op("dve", lambda e: e.tensor_tensor(out=masks[:, :, :n], in0=mps[:, :nkt * n].rearrange("p (k q) -> p k q", q=n), in1=cmf, op=ALU.mult),
                         reads=[mpn, "cmf"], writes=["masks_all"])
                for kt in (range(nkt) if not small else ()):
                    sb_, sbn = selb_r.next()
                    r0_ = min(2 * kt, nblk - 1)
                    r1_ = min(2 * kt + 1, nblk - 1)
                    S.dma("sp", sb_[0:64, :n], selT_d[job["ji"], g, r0_:r0_ + 1, 0:n].partition_broadcast(64), reads=["selT_d"], writes=[sbn + "a"])
                    S.dma("sp", sb_[64:128, :n], selT_d[job["ji"], g, r1_:r1_ + 1, 0:n].partition_broadcast(64), reads=["selT_d"], writes=[sbn + "b"])
                    cm, cmn = cm_r.next()
                    S.op("dve", lambda e: e.tensor_scalar(out=cm[:, :n], in0=qpos_bc[:, c0:c0 + n], scalar1=kp[:, kt:kt + 1], scalar2=None, op0=ALU.is_ge),
                         reads=["qpos_bc", "kpos_seq", "kpos_smp"], writes=[cmn])
                    S.op("dve", lambda e: e.tensor_tensor(out=masks[:, kt, :n], in0=cm[:, :n], in1=sb_[:, :n], op=ALU.mult),
                         reads=[cmn, sbn + "a", sbn + "b"], writes=[f"mask{kt}"])
                if small:
                    small_group(Bs, job, cols, g, 1, kT, "slck", lambda kt: vv[:, kt, :], "slcv", nkt, kp[:, :nkt], masks, "masks_all")
                bt_s = bias_table(Bs, job, kp, nkt, heads, "s") if not small else None
                pools_s = std_pools(Bs, n)
                for h in (heads if not small else ()):
                    O, Dn = softmax_head(Bs, job, cols, h, kT, "slck", lambda kt: vv[:, kt, :], "slcv", nkt,
                                         lambda ai, h=h: bt_s[(ai, h)], lambda kt: (masks[:, kt, :n], f"mask{kt}"), pools_s, head_slope_idx=h)
                    t, tn_, _, _ = normalize(Bs, cols, O, Dn, pools_s)
                    gated_acc(cols, h, 1, t, tn_, first=False)
                wpt = tabs["wpos_seq"] if job["wposk"] == "seq" else tabs["wpos_smp"]
                for (w0, wn, wt0, nwt) in job["win"]:
                    S.barrier()
                    Bw = Bump(nc, [list(r) for r in marks])
                    wcols = (c0 + w0, wn)
                    wnn = max(wn, 8)
                    kT = Bw.alloc([128, nwt * 128], BF16)
                    vv = Bw.alloc([128, nwt, 128], BF16)
                    S.dma("sp", kT, job["wkT"][g, :, wt0 * 128:(wt0 + nwt) * 128], writes=["wink"])
                    S.dma("sp", vv, job["wv"][wt0 * 128:(wt0 + nwt) * 128, g * 128:(g + 1) * 128].rearrange("(t p) n -> p t n", p=128), writes=["winv"])
                    wmask = Bw.alloc([128, nwt, wnn], BF16)
                    d1 = Bw.rot(2, [128, wnn], F32, "wd1")
                    d2 = Bw.rot(2, [128, wnn], F32, "wd2")
                    for kt in range(nwt):
                        a, an_ = d1.next()
                        b, bn_ = d2.next()
                        kpc = wpt[:, wt0 + kt:wt0 + kt + 1]
                        S.op("dve", lambda e: e.tensor_scalar(out=a[:, :wn], in0=qpos_bc[:, c0 + w0:c0 + w0 + wn], scalar1=kpc, scalar2=0.0,
                                                              op0=ALU.subtract, op1=ALU.is_ge), reads=["qpos_bc", "wpos_seq", "wpos_smp"], writes=[an_])
                        S.op("dve", lambda e: e.tensor_scalar(out=b[:, :wn], in0=qpos_bc[:, c0 + w0:c0 + w0 + wn], scalar1=kpc, scalar2=512.0,
                                                              op0=ALU.subtract, op1=ALU.is_lt), reads=["qpos_bc", "wpos_seq", "wpos_smp"], writes=[bn_])
                        S.op("dve", lambda e: e.tensor_tensor(out=wmask[:, kt, :wn], in0=a[:, :wn], in1=b[:, :wn], op=ALU.mult),
                             reads=[an_, bn_], writes=[f"wmask{kt}", "wmask_all"])
                    wjob = job
                    if small:
                        small_group(Bw, job, wcols, g, 2, kT, "wink", lambda kt: vv[:, kt, :], "winv", nwt, wpt[:, wt0:wt0 + nwt], wmask, "wmask_all")
                        continue
                    bt_w = bias_table(Bw, job, wpt[:, wt0:wt0 + nwt], nwt, heads, "w")
                    pools_w = std_pools(Bw, wn)
                    for h in heads:
                        O, Dn = softmax_head(Bw, wjob, wcols, h, kT, "wink", lambda kt: vv[:, kt, :], "winv", nwt,
                                             lambda ai, h=h: bt_w[(ai, h)], lambda kt: (wmask[:, kt, :wn], f"wmask{kt}"), pools_w, head_slope_idx=h)
                        t, tn_, _, _ = normalize(Bw, wcols, O, Dn, pools_w)
                        gated_acc(wcols, h, 2, t, tn_, first=False)
            S.barrier()
            ost = Bump(nc, [list(r) for r in marks]).rot(2, [128, nn], BF16, "nost")
            for h in range(8):
                o, on = ost.next()
                S.op("act", lambda e: e.activation(out=o[:, :n], in_=oacc[:, h, c0:c0 + n], func=AF.Identity), reads=[f"oacc{h}"], writes=[on])
                S.dma("sp", oT_d[h, :, c0:c0 + n], o[:, :n], reads=[on])

        for job in mkjobs():
            if job_names is not None and job["name"] not in job_names:
                continue
            if "mem" in P3_BRANCHES:
                mem_branch(job)
            if "sb" in P3_BRANCHES and job["kpos"] == "seq":
                sb_branch(job)
            if "nsa" in P3_BRANCHES:
                nsa_branch(job)
        if "sb" in P3_BRANCHES and (job_names is None or any(nm.startswith("S") for nm in job_names)):
            sb_samples()
        S.barrier()

    if "p1" in phases:
        phase1()
    if "p2" in phases:
        phase2()
    if "p3z" in phases:
        phase3_zero()
    if "p3" in phases:
        if job_names is None or any(n.startswith("S") for n in job_names):
            phase3_prepare()
        phase3_attention()
    if "p4" in phases:
        phase4()
    S.finish()
    return nc, S


def fm(vec, n):
    return np.ascontiguousarray(np.asarray(vec, np.float32).reshape(n, 128).T)


def host_shared(inp):
    m = {}
    m["w_in"] = inp["w_in"][0]
    b_in = inp["b_in"][0]
    bfm = np.zeros((128, NCH), np.float32)
    for ci, (col0, ncols, kind, idx) in enumerate(CHUNKS):
        bfm[:ncols, ci] = b_in[col0:col0 + ncols]
    m["b_in_fm"] = bfm
    bv = np.concatenate([b_in[1024 + 768:1024 + 1024], b_in[1024 + 1536:1024 + 2048]])
    m["b_v_bc"] = np.ascontiguousarray(np.broadcast_to(bv[None, :], (128, 768)))
    m["b_wv_bc"] = np.ascontiguousarray(np.broadcast_to(b_in[None, 3328:3584], (128, 256)))
    m["w_mem"] = inp["w_mem_kv"][0]
    bm = inp["b_mem_kv"][0]
    m["b_mem_fm"] = fm(bm, 8)
    m["b_mem_bc"] = np.ascontiguousarray(np.broadcast_to(bm[None, :], (128, 1024)))
    m["w_br"] = np.ascontiguousarray(np.concatenate([inp["w_br_nsa"][0], inp["w_br_sb"][0], inp["w_br_mem"][0]], axis=0))
    m["w_o"] = inp["w_o"][0]
    m["w_up"] = inp["w_up"][0]
    m["w_down"] = inp["w_down"][0]
    m["vec_fm"] = np.ascontiguousarray(np.stack([fm(inp[k][0], 16) for k in ("ln1_g", "ln1_b", "ln2_g", "ln2_b", "b_down")], axis=1))
    m["bup_fm"] = fm(inp["b_up"][0], 88)
    m["bconv_fm"] = fm(inp["b_conv"][0], 88)
    m["wconv_fm"] = np.ascontiguousarray(np.stack([fm(inp["w_conv"][0][k], 88) for k in range(3)], axis=1))
    p = np.arange(128, dtype=np.float32)
    m["pool"] = inp["cache_kv_pages"][0].reshape(-1, 2048)
    m["rowid"] = p.reshape(128, 1).copy()
    m["ident"] = np.eye(128, dtype=np.float32)
    m["uinc"] = (p[:, None] >= p[None, :]).astype(np.float32)
    m["kpos_seq"] = (128.0 * np.arange(32)[None, :] + p[:, None]).astype(np.float32)
    ks = (128.0 * np.arange(65)[None, :] + p[:, None]).astype(np.float32)
    ks[:, 64] = np.where(p < 4, 8192.0 + p, 1e9)
    m["kpos_smp"] = ks
    ws = (7680.0 + 128.0 * np.arange(5)[None, :] + p[:, None]).astype(np.float32)
    ws[:, 4] = np.where(p < 4, 8192.0 + p, 1e9)
    m["wpos_smp"] = ws
    nidx = 128.0 * np.arange(4)[None, :] + p[:, None]
    m["cend"] = np.where(nidx < 511, 16.0 * nidx + 31.0, 1e9).astype(np.float32)
    m["cend_seq"] = np.where(nidx[:, :2] < 255, 16.0 * nidx[:, :2] + 31.0, 1e9).astype(np.float32)
    j = np.arange(129, dtype=np.float32)
    m["blk"] = np.ascontiguousarray(np.broadcast_to(np.stack([64.0 * j, j])[None], (128, 2, 129))).astype(np.float32)
    n_ = nidx[:, :, None]
    m["ov"] = ((16.0 * n_ < 64.0 * (j[None, None, :] + 1)) & (16.0 * n_ + 32.0 > 64.0 * j[None, None, :])).astype(np.float32)
    m["w1c"] = np.ascontiguousarray(np.stack([inp["w_cmp_k1"][0].transpose(1, 0, 2), inp["w_cmp_v1"][0].transpose(1, 0, 2)]))
    m["w2c"] = np.ascontiguousarray(np.stack([inp["w_cmp_k2"][0], inp["w_cmp_v2"][0]]))
    m["pec"] = np.ascontiguousarray(np.stack([inp["pe_cmp_k"][0].T, inp["pe_cmp_v"][0].T]))
    return m


def host_layout(inp, c, shared):
    b, j = c // 4, c % 4
    ch = [j, 7 - j]
    xp = inp["x_prompt"][b]
    xq = np.zeros((D, NQ), np.float32)
    xw = np.zeros((D, 2560), np.float32)
    wpos = np.zeros((128, 20), np.float32)
    qpos = np.zeros((NQ,), np.float32)
    hvalid = np.zeros((128, 4), np.float32)
    for s in range(2):
        p0 = 512 * ch[s]
        xq[:, s * 512:(s + 1) * 512] = xp[p0:p0 + 512].T
        qpos[s * 512:(s + 1) * 512] = np.arange(p0, p0 + 512)
        for h in range(2):
            p = p0 - 2 + h
            qpos[1024 + 2 * s + h] = p
            if p >= 0:
                xq[:, 1024 + 2 * s + h] = xp[p]
                hvalid[:, 2 * s + h] = 1.0
        lo = p0 - 768
        for i in range(1280):
            pass
        a = max(lo, 0)
        xw[:, s * 1280 + (a - lo):(s + 1) * 1280] = xp[a:lo + 1280].T
        wp = lo + 128.0 * np.arange(10)[None, :] + np.arange(128)[:, None]
        wpos[:, 10 * s:10 * s + 10] = np.where(wp >= 0, wp, -1e9)
    xsmp = inp["x_sample"][4 * c:4 * c + 4].reshape(16, D)
    xq[:, 1028:1044] = xsmp.T
    qpos[1028:1044] = np.tile(8192 + np.arange(4), 4)
    m = dict(shared)
    m.update({"xq": xq, "xs": np.ascontiguousarray(xp.T), "xw": xw, "hvalid": hvalid, "wpos": wpos})
    m["qpos_bc"] = np.ascontiguousarray(np.broadcast_to(qpos[None, :], (128, NQ)))
    refs = np.zeros((14,), np.float32)
    qposT = np.zeros((128, 13), np.float32)
    for i in range(8):
        refs[i] = qpos[i * 128 + 127]
        qposT[:, i] = qpos[i * 128:(i + 1) * 128]
    refs[8], refs[9] = qpos[1025], qpos[1027]
    refs[10:14] = 8195.0
    qposT[0:4, 8] = qpos[1024:1028]
    for s in range(4):
        qposT[0:4, 9 + s] = 8192.0 + np.arange(4)
    m["refs"] = np.ascontiguousarray(np.broadcast_to(refs[None, :], (128, 14)))
    m["qposT"] = qposT
    m["curT"] = np.floor(qposT / 64.0).astype(np.float32)
    m["pt"] = np.ascontiguousarray(inp["page_table"][4 * c:4 * c + 4].reshape(1, 256).astype(np.int32))
    m["cwin"] = np.ascontiguousarray(inp["cache_win_kv"][0, 4 * c:4 * c + 4])
    m["cmem"] = np.ascontiguousarray(inp["cache_mem_kv"][0, 4 * c:4 * c + 4])
    m["memT"] = np.ascontiguousarray(inp["mem_prompt"][b].T)
    st = inp["state_ffn_conv"][0, 4 * c:4 * c + 4]
    m["state_fm"] = np.ascontiguousarray(st.reshape(4, 2, 88, 128).transpose(3, 2, 0, 1))
    return m


_CACHE = {}


def kernel(**inputs):
    inp = {k: np.asarray(v) for k, v in inputs.items()}
    if "nc" not in _CACHE:
        _CACHE["nc"] = build_program()
    nc, S = _CACHE["nc"]
    shared = host_shared(inp)
    in_maps = [host_layout(inp, c, shared) for c in range(8)]
    res = run_bass_kernel_spmd(nc, in_maps, core_ids=list(range(8)))
    return assemble(inp, res.results)


def assemble(inp, results):
    y_p = np.zeros((2, SEQ, D), np.float32)
    y_s = np.zeros((32, 4, D), np.float32)
    kv_p = np.zeros((1, 2, SEQ, 2048), np.float32)
    win_p = np.zeros((1, 2, 512, 512), np.float32)
    mem_p = np.zeros((1, 2, 256, 1024), np.float32)
    conv_p = np.zeros((1, 2, 2, 2 * D_FF), np.float32)
    kv_s = np.zeros((1, 32, 4, 2048), np.float32)
    win_s = np.zeros((1, 32, 512, 512), np.float32)
    conv_s = np.zeros((1, 32, 2, 2 * D_FF), np.float32)
    for c in range(8):
        b, j = c // 4, c % 4
        ch = [j, 7 - j]
        r = results[c]
        kvq = r["kvq"].reshape(20 * 128, NQ)
        yq = r["yq"].reshape(D, NQ)
        for s in range(2):
            p0 = 512 * ch[s]
            kv_p[0, b, p0:p0 + 512] = kvq[:2048, s * 512:(s + 1) * 512].T
            y_p[b, p0:p0 + 512] = yq[:, s * 512:(s + 1) * 512].T
            if ch[s] == 7:
                win_p[0, b] = kvq[2048:2560, s * 512:(s + 1) * 512].T
                conv_p[0, b] = r["convp_o"].transpose(2, 1, 0).reshape(2, 2 * D_FF)
        kv_s[0, 4 * c:4 * c + 4] = kvq[:2048, 1028:1044].T.reshape(4, 4, 2048)
        win_s[0, 4 * c:4 * c + 4] = r["wins_o"]
        y_s[4 * c:4 * c + 4] = yq[:, 1028:1044].T.reshape(4, 4, D)
        conv_s[0, 4 * c:4 * c + 4] = r["convs_o"].transpose(2, 3, 1, 0).reshape(4, 2, 2 * D_FF)
        if j == 0:
            mem_p[0, b] = r["memkv_o"]
    return (y_p, y_s, kv_p, win_p, mem_p, conv_p, kv_s, win_s, conv_s)
```

```python
import contextlib
import numpy as np
import concourse.bass as bass
import concourse.mybir as mybir
from concourse.bass_utils import run_bass_kernel_spmd

F32 = mybir.dt.float32
BF16 = mybir.dt.bfloat16
I32 = mybir.dt.int32
AF = mybir.ActivationFunctionType
ALU = mybir.AluOpType

D = 2048
KC = 16
SEQ = 4096
NQ = 1044
TILES = [(0, 512), (512, 512), (1024, 20)]
PROJ_W = 10776
D_FF = 5632
ALPHA = 2.0 ** 0.25
SCALE = 128 ** -0.5
NEG = -30000.0
SB_BASE = 16512

CHUNKS = []
for i in range(8):
    CHUNKS.append((i * 128, 128, "q", i))
for i in range(20):
    CHUNKS.append((1024 + i * 128, 128, "kv", i))
for i in range(4):
    CHUNKS.append((3584 + i * 128, 128, "q", 8 + i))
for i in range(4):
    CHUNKS.append((4096 + i * 128, 128, "q", 12 + i))
CHUNKS.append((4608, 24, "gn", 0))
for i in range(48):
    CHUNKS.append((4632 + i * 128, 128, "gm", i))
NCH = len(CHUNKS)

KFM_COLS = [1024 + i * 128 for i in range(6)] + [1024 + 1024 + i * 128 for i in range(4)]
VTM = [(1024 + 768, 256, 0), (1024 + 1536, 512, 256)]


class Sched:
    COMPUTE = ("pe", "act", "dve", "pool")

    def __init__(self, nc, n_dma_sems=10):
        self.nc = nc
        self.eng = {"pe": nc.tensor, "act": nc.scalar, "dve": nc.vector, "pool": nc.gpsimd, "sp": nc.sync}
        self.sem = {e: nc.alloc_semaphore("s_" + e) for e in self.COMPUTE}
        self.cnt = {e: 0 for e in self.COMPUTE}
        self.qeng = {"sp": "sp", "actq": "act", "poolq": "pool"}
        self.dsems = {q: [nc.alloc_semaphore(f"d_{q}{i}") for i in range(n_dma_sems)] for q in self.qeng}
        self.dval = {q: [0] * n_dma_sems for q in self.qeng}
        self.dnext = {q: 0 for q in self.qeng}
        self.seen = {}
        self.last_w = {}
        self.readers = {}
        self.n_waits = 0
        self.n_ops = 0

    def _wait(self, issuer, tok):
        if tok is None:
            return
        if tok[0] == "c":
            _, e, idx = tok
            if e == issuer and e == "pe":
                return
            key = (issuer, e)
            if self.seen.get(key, 0) >= idx:
                return
            self.seen[key] = idx
            self.eng[issuer].wait_ge(self.sem[e], idx)
        else:
            _, q, i, val = tok
            key = (issuer, q, i)
            if self.seen.get(key, 0) >= val:
                return
            self.seen[key] = val
            self.eng[issuer].wait_ge(self.dsems[q][i], val)
        self.n_waits += 1

    def _deps(self, issuer, reads, writes):
        for r in reads:
            self._wait(issuer, self.last_w.get(r))
        for w in writes:
            self._wait(issuer, self.last_w.get(w))
            for t in self.readers.get(w, ()):
                self._wait(issuer, t)

    def _record(self, tok, reads, writes):
        for r in reads:
            self.readers.setdefault(r, []).append(tok)
        for w in writes:
            self.last_w[w] = tok
            self.readers[w] = []

    def op(self, e, fn, reads=(), writes=()):
        self._deps(e, reads, writes)
        ins = fn(self.eng[e])
        self.cnt[e] += 1
        ins.then_inc(self.sem[e], 1)
        tok = ("c", e, self.cnt[e])
        self._record(tok, reads, writes)
        self.n_ops += 1
        return tok

    def dma(self, q, out, in_, reads=(), writes=(), **kw):
        issuer = self.qeng[q]
        i = self.dnext[q]
        self.dnext[q] = (i + 1) % len(self.dsems[q])
        if self.dval[q][i] > 0:
            self._wait(issuer, ("d", q, i, self.dval[q][i]))
        self._deps(issuer, reads, writes)
        ins = self.eng[issuer].dma_start(out=out, in_=in_, **kw)
        self.dval[q][i] += 16
        ins.then_inc(self.dsems[q][i], 16)
        tok = ("d", q, i, self.dval[q][i])
        self._record(tok, reads, writes)
        self.n_ops += 1
        return tok

    def idma(self, out, in_, idx_ap, reads=(), writes=(), element_offset=0):
        q, issuer = "poolq", "pool"
        i = self.dnext[q]
        self.dnext[q] = (i + 1) % len(self.dsems[q])
        if self.dval[q][i] > 0:
            self._wait(issuer, ("d", q, i, self.dval[q][i]))
        self._deps(issuer, reads, writes)
        ins = self.nc.gpsimd.indirect_dma_start(out=out, out_offset=None, in_=in_,
                                                in_offset=bass.IndirectOffsetOnAxis(ap=idx_ap, axis=0),
                                                element_offset=element_offset)
        self.dval[q][i] += 16
        ins.then_inc(self.dsems[q][i], 16)
        tok = ("d", q, i, self.dval[q][i])
        self._record(tok, reads, writes)
        self.n_ops += 1
        return tok

    def barrier(self):
        for issuer in ("pe", "act", "dve", "pool", "sp"):
            for e in self.COMPUTE:
                if e != issuer and self.cnt[e] > 0:
                    self._wait(issuer, ("c", e, self.cnt[e]))
            for q in self.qeng:
                for i, v in enumerate(self.dval[q]):
                    if v > 0:
                        self._wait(issuer, ("d", q, i, v))
        self.last_w = {}
        self.readers = {}

    def finish(self):
        for q, issuer in self.qeng.items():
            for i, v in enumerate(self.dval[q]):
                if v > 0:
                    self._wait(issuer, ("d", q, i, v))


class Rot:
    def __init__(self, items):
        self.items = items
        self.i = 0

    def next(self):
        it = self.items[self.i]
        self.i = (self.i + 1) % len(self.items)
        return it


class Bump:
    _n = [0]

    def __init__(self, nc, ranges):
        self.nc = nc
        self.ranges = [list(r) for r in ranges]

    def alloc(self, shape, dt):
        nbytes = int(np.prod(shape[1:])) * (2 if dt == BF16 else 4)
        nbytes = (nbytes + 63) // 64 * 64
        for r in self.ranges:
            if r[1] - r[0] >= nbytes:
                off = r[0]
                r[0] += nbytes
                Bump._n[0] += 1
                return self.nc.alloc_sbuf_tensor_at(f"bmp{Bump._n[0]}", list(shape), dt, offset=SB_BASE + off).ap()
        raise RuntimeError(f"Bump: out of SBUF for {shape} {dt} ranges={self.ranges}")

    def rot(self, n, shape, dt, tag):
        Bump._n[0] += 1
        k = Bump._n[0]
        return Rot([(self.alloc(shape, dt), f"{tag}{k}_{i}") for i in range(n)])


SB_TOP = 206 * 1024
EPS = 1e-5
GELU_C = 1.5957691216057308


def build_program(debug=False, phases=("p1", "p2", "p3", "p4"), P3_BRANCHES=("mem", "sb", "nsa"), npages=2560, job_names=None):
    nc = bass.Bass("TRN2", target_bir_lowering=False)
    S = Sched(nc)

    def din(name, shape, dt=F32):
        return nc.dram_tensor(name, list(shape), dt, kind="ExternalInput").ap()

    def dout(name, shape, dt=F32):
        return nc.dram_tensor(name, list(shape), dt, kind="ExternalOutput").ap()

    def dscr(name, shape, dt):
        return nc.dram_tensor(name, list(shape), dt, kind=("ExternalOutput" if debug else "Internal")).ap()

    xq = din("xq", [D, NQ])
    xs = din("xs", [D, SEQ])
    xw = din("xw", [D, 2560])
    memT = din("memT", [D, 256])
    w_in = din("w_in", [D, PROJ_W])
    b_in_fm = din("b_in_fm", [128, NCH])
    b_v_bc = din("b_v_bc", [128, 768])
    b_wv_bc = din("b_wv_bc", [128, 256])
    w_mem = din("w_mem", [D, 1024])
    b_mem_fm = din("b_mem_fm", [128, 8])
    b_mem_bc = din("b_mem_bc", [128, 1024])
    w_br = din("w_br", [D, D])
    w_o = din("w_o", [D, D])
    w_up = din("w_up", [D, 2 * D_FF])
    w_down = din("w_down", [D_FF, D])
    vec_fm = din("vec_fm", [128, 5, 16])
    bup_fm = din("bup_fm", [128, 88])
    bconv_fm = din("bconv_fm", [128, 88])
    wconv_fm = din("wconv_fm", [128, 3, 88])
    state_fm = din("state_fm", [128, 88, 4, 2])
    hvalid = din("hvalid", [128, 4])

    kvq = dout("kvq", [20, 128, NQ])
    memkv_o = dout("memkv_o", [256, 1024])
    yq = dout("yq", [16, 128, NQ])
    convp_o = dout("convp_o", [128, 88, 2])
    convs_o = dout("convs_o", [128, 88, 4, 2])

    qT_d = dscr("qT_d", [16, 128, NQ], BF16)
    gn_d = dscr("gn_d", [24, NQ], F32)
    gm_d = dscr("gm_d", [48, 128, NQ], F32)
    kT_d = dscr("kT_d", [10, 128, SEQ], BF16)
    v_d = dscr("v_d", [SEQ, 768], BF16)
    wkT_d = dscr("wkT_d", [2, 128, 2560], BF16)
    wv_d = dscr("wv_d", [2560, 256], BF16)
    mkT_d = dscr("mkT_d", [4, 128, 256], BF16)
    mv_d = dscr("mv_d", [256, 512], BF16)
    oT_d = dscr("oT_d", [16, 128, NQ], BF16)
    x1_d = dscr("x1_d", [16, 128, NQ], F32)

    PSA = nc.alloc_psum_tensor("psall", [128, 8, 512], F32).ap()
    PS = [PSA[:, i, :] for i in range(8)]
    psrot = Rot([(PS[i], f"ps{i}") for i in range(8)])
    psrot4 = Rot([(PS[i], f"ps{i}") for i in range(4)])

    xs_v = xs.rearrange("(kc p) n -> p kc n", p=128)
    xw_v = xw.rearrange("(kc p) n -> p kc n", p=128)
    xq_v = xq.rearrange("(kc p) n -> p kc n", p=128)
    memT_v = memT.rearrange("(kc p) n -> p kc n", p=128)
    w_in_v = w_in.rearrange("(kc p) n -> p kc n", p=128)
    w_mem_v = w_mem.rearrange("(kc p) n -> p kc n", p=128)
    w_br_v = w_br.rearrange("(kc p) n -> p kc n", p=128)
    w_o_v = w_o.rearrange("(kc p) n -> p kc n", p=128)
    w_up_v = w_up.rearrange("(kc p) n -> p kc n", p=128)
    w_down_v = w_down.rearrange("(kc p) n -> p kc n", p=128)

    def mm_fm(ps, pn, wt, wn, wcols, xt, xres, xcols, nk=KC, kc0=0):
        (c0, m), (t0, n) = wcols, xcols
        for k in range(nk):
            kc = kc0 + k
            S.op("pe", lambda e, kc=kc, k=k: e.matmul(
                ps[:m, :n], lhsT=wt[:, kc, c0:c0 + m], rhs=xt[:, kc, t0:t0 + n], start=(k == 0), stop=(k == nk - 1)),
                reads=[wn, xres(kc) if callable(xres) else xres], writes=[pn])

    def phase1():
        B = Bump(nc, [(0, SB_TOP)])
        bfm = B.alloc([128, NCH], F32)
        bvbc = B.alloc([128, 768], F32)
        bwvbc = B.alloc([128, 256], F32)
        bmfm = B.alloc([128, 8], F32)
        bmbc = B.alloc([128, 1024], F32)
        S.dma("sp", bfm, b_in_fm, writes=["bfm"])
        S.dma("sp", bvbc, b_v_bc, writes=["bvbc"])
        S.dma("sp", bwvbc, b_wv_bc, writes=["bwvbc"])
        S.dma("sp", bmfm, b_mem_fm, writes=["bmfm"])
        S.dma("sp", bmbc, b_mem_bc, writes=["bmbc"])
        wkv = B.alloc([128, KC, 2560], BF16)
        wstg = B.rot(2, [128, 2560], F32, "p1wstg")
        for kc in range(KC):
            sg, sgn = wstg.next()
            S.dma("sp", sg, w_in_v[:, kc, 1024:3584], writes=[sgn])
            if kc % 2:
                S.op("dve", lambda e: e.tensor_copy(out=wkv[:, kc, :], in_=sg), reads=[sgn], writes=[f"wkv{kc}"])
            else:
                S.op("act", lambda e: e.activation(out=wkv[:, kc, :], in_=sg, func=AF.Identity), reads=[sgn], writes=[f"wkv{kc}"])
        xbufs = B.rot(2, [128, KC, 512], BF16, "p1x")
        xstg = B.rot(2, [128, 8, 512], F32, "p1xstg")
        kst = B.rot(3, [128, 512], BF16, "p1ks")
        vst = B.rot(3, [128, 768], BF16, "p1vs")

        def kv_tile(src_v, t, fm_list, tm_list, wres):
            xb, xn = xbufs.next()
            for hf in range(2):
                sg, sgn = xstg.next()
                S.dma("sp", sg, src_v[:, 8 * hf:8 * hf + 8, t * 512:(t + 1) * 512], writes=[sgn])
                S.op("dve", lambda e: e.tensor_copy(out=xb[:, 8 * hf:8 * hf + 8, :], in_=sg), reads=[sgn], writes=[xn + "ab"[hf]])
            for (c0, bcol, bsrc, dst) in fm_list:
                ps, pn = psrot.next()
                for kc in range(KC):
                    S.op("pe", lambda e, ps=ps, kc=kc, c0=c0: e.matmul(
                        ps, lhsT=wkv[:, kc, c0:c0 + 128], rhs=xb[:, kc, :], start=(kc == 0), stop=(kc == KC - 1)),
                        reads=[xn + "ab"[kc // 8], wres(kc)], writes=[pn])
                st, sn = kst.next()
                S.op("act", lambda e, st=st, ps=ps, bcol=bcol, bsrc=bsrc: e.activation(
                    out=st, in_=ps, func=AF.Identity, bias=bsrc[:, bcol:bcol + 1], scale=1.0),
                    reads=[pn, "bfm"], writes=[sn])
                S.dma("poolq", dst, st, reads=[sn])
            for tt in range(4):
                st, sn = vst.next()
                tot = 0
                for (c0, ncols, bap, voff, bres) in tm_list:
                    ps, pn = psrot.next()
                    for kc in range(KC):
                        S.op("pe", lambda e, ps=ps, kc=kc, c0=c0, ncols=ncols, tt=tt: e.matmul(
                            ps[:, :ncols], lhsT=xb[:, kc, tt * 128:(tt + 1) * 128], rhs=wkv[:, kc, c0:c0 + ncols],
                            start=(kc == 0), stop=(kc == KC - 1)),
                            reads=[xn + "ab"[kc // 8], wres(kc)], writes=[pn])
                    S.op("dve", lambda e, st=st, ps=ps, ncols=ncols, voff=voff, bap=bap: e.tensor_tensor(
                        out=st[:, voff:voff + ncols], in0=ps[:, :ncols], in1=bap[:, voff:voff + ncols], op=ALU.add),
                        reads=[pn, bres], writes=[sn])
                    tot = max(tot, voff + ncols)
                yield tt, st, sn, tot

        for t in range(SEQ // 512):
            fm = [(col0 - 1024, 8 + (col0 - 1024) // 128, bfm, kT_d[ci, :, t * 512:(t + 1) * 512])
                  for ci, col0 in enumerate(KFM_COLS)]
            tm = [(col0 - 1024, ncols, bvbc, voff, "bvbc") for (col0, ncols, voff) in VTM]
            for tt, st, sn, tot in kv_tile(xs_v, t, fm, tm, lambda kc: f"wkv{kc}"):
                r0 = t * 512 + tt * 128
                S.dma("poolq", v_d[r0:r0 + 128, :], st[:, :tot], reads=[sn])
        for t in range(5):
            fm = [(2048 + g * 128, 8 + 16 + g, bfm, wkT_d[g, :, t * 512:(t + 1) * 512]) for g in range(2)]
            tm = [(2048 + 256, 256, bwvbc, 0, "bwvbc")]
            for tt, st, sn, tot in kv_tile(xw_v, t, fm, tm, lambda kc: f"wkv{kc}"):
                r0 = t * 512 + tt * 128
                S.dma("poolq", wv_d[r0:r0 + 128, :], st[:, :256], reads=[sn])
        S.barrier()
        B2 = Bump(nc, [(0, SB_TOP)])
        bmfm2 = B2.alloc([128, 8], F32)
        bmbc2 = B2.alloc([128, 1024], F32)
        S.dma("sp", bmfm2, b_mem_fm, writes=["bmfm"])
        S.dma("sp", bmbc2, b_mem_bc, writes=["bmbc"])
        wm = B2.alloc([128, KC, 1024], BF16)
        S.dma("poolq", wm, w_mem_v, writes=["wm"])
        mx = B2.alloc([128, KC, 256], BF16)
        S.dma("poolq", mx, memT_v, writes=["mx"])
        mst = B2.rot(2, [128, 256], BF16, "mst")
        for hh in range(4):
            ps, pn = psrot.next()
            mm_fm(ps, pn, wm, "wm", (hh * 128, 128), mx, "mx", (0, 256))
            st, sn = mst.next()
            S.op("act", lambda e, st=st, ps=ps, hh=hh: e.activation(
                out=st, in_=ps[:, :256], func=AF.Identity, bias=bmfm2[:, hh:hh + 1], scale=1.0),
                reads=[pn, "bmfm"], writes=[sn])
            S.dma("sp", mkT_d[hh], st, reads=[sn])
        m32 = B2.rot(2, [128, 1024], F32, "m32")
        m16 = B2.rot(2, [128, 512], BF16, "m16")
        for tt in range(2):
            o32, on = m32.next()
            o16, o16n = m16.next()
            for half in range(2):
                ps, pn = psrot.next()
                for kc in range(KC):
                    S.op("pe", lambda e, ps=ps, kc=kc, tt=tt, half=half: e.matmul(
                        ps, lhsT=mx[:, kc, tt * 128:(tt + 1) * 128], rhs=wm[:, kc, half * 512:(half + 1) * 512],
                        start=(kc == 0), stop=(kc == KC - 1)), reads=["mx", "wm"], writes=[pn])
                S.op("dve", lambda e, ps=ps, o32=o32, half=half: e.tensor_tensor(
                    out=o32[:, half * 512:(half + 1) * 512], in0=ps, in1=bmbc2[:, half * 512:(half + 1) * 512], op=ALU.add),
                    reads=[pn, "bmbc"], writes=[on])
            S.op("dve", lambda e, o16=o16, o32=o32: e.tensor_copy(out=o16, in_=o32[:, 512:1024]),
                 reads=[on], writes=[o16n])
            S.dma("sp", memkv_o[tt * 128:(tt + 1) * 128, :], o32, reads=[on])
            S.dma("sp", mv_d[tt * 128:(tt + 1) * 128, :], o16, reads=[o16n])
        S.barrier()

    def phase2():
        B = Bump(nc, [(0, SB_TOP)])
        bfm = B.alloc([128, NCH], F32)
        S.dma("sp", bfm, b_in_fm, writes=["bfm"])
        xqb = B.alloc([128, KC, NQ], BF16)
        xstg = B.rot(3, [128, NQ], F32, "p2xstg")
        for kc in range(KC):
            sg, sgn = xstg.next()
            S.dma("sp", sg, xq_v[:, kc, :], writes=[sgn])
            S.op("dve", lambda e: e.tensor_copy(out=xqb[:, kc, :], in_=sg), reads=[sgn], writes=[f"xq{kc}"])
        wb = B.rot(2, [128, KC, 512], BF16, "p2w")
        wstg = B.rot(2, [128, KC, 512], F32, "p2wstg")
        st32 = B.rot(3, [128, NQ], F32, "p2s32")
        st16 = B.rot(3, [128, NQ], BF16, "p2s16")
        groups = []
        i = 0
        while i < NCH:
            g = [i]
            while (len(g) < 4 and i + len(g) < NCH and CHUNKS[i + len(g)][1] == 128 and CHUNKS[g[-1]][1] == 128
                   and CHUNKS[i + len(g)][0] == CHUNKS[g[-1]][0] + 128):
                g.append(i + len(g))
            groups.append(g)
            i += len(g)
        for g in groups:
            wt, wn = wb.next()
            gcol0 = CHUNKS[g[0]][0]
            gcols = sum(CHUNKS[c][1] for c in g)
            sg, sgn = wstg.next()
            S.dma("sp", sg[:, :, :gcols], w_in_v[:, :, gcol0:gcol0 + gcols], writes=[sgn])
            S.op("dve", lambda e: e.tensor_copy(out=wt[:, :, :gcols], in_=sg[:, :, :gcols]), reads=[sgn], writes=[wn])
            for c in g:
                col0, ncols, kind, idx = CHUNKS[c]
                lc = col0 - gcol0
                pss = [psrot.next() for _ in TILES]
                for ti, (t0, tn) in enumerate(TILES):
                    ps, pn = pss[ti]
                    mm_fm(ps, pn, wt, wn, (lc, ncols), xqb, lambda kc: f"xq{kc}", (t0, tn))
                st, sn = st16.next() if kind == "q" else st32.next()
                func = AF.Sigmoid if kind in ("gn", "gm") else AF.Identity
                for ti, (t0, tn) in enumerate(TILES):
                    ps, pn = pss[ti]
                    S.op("act", lambda e, st=st, ps=ps, ncols=ncols, t0=t0, tn=tn, c=c, func=func: e.activation(
                        out=st[:ncols, t0:t0 + tn], in_=ps[:ncols, :tn], func=func, bias=bfm[:ncols, c:c + 1], scale=1.0),
                        reads=[pn, "bfm"], writes=[sn])
                if kind == "q":
                    S.dma("poolq", qT_d[idx], st, reads=[sn])
                elif kind == "kv":
                    S.dma("poolq", kvq[idx], st, reads=[sn])
                elif kind == "gn":
                    S.dma("poolq", gn_d, st[:24, :], reads=[sn])
                else:
                    S.dma("poolq", gm_d[idx], st, reads=[sn])
        S.barrier()

    def phase3_zero():
        B = Bump(nc, [(0, SB_TOP)])
        z = B.alloc([128, NQ], BF16)
        S.op("dve", lambda e: e.memset(z, 0.0), writes=["z"])
        for i in range(16):
            S.dma("sp", oT_d[i], z, reads=["z"])
        S.barrier()

    SZ16 = 16 * NQ * 2
    SZ32 = 16 * NQ * 4
    SZACT = 44 * NQ * 2

    def fixed(name, off, shape, dt):
        return nc.alloc_sbuf_tensor_at(name, list(shape), dt, offset=SB_BASE + off).ap()

    def layer_norm_fm(B, r, gcol, bcol, vec, ones32, out_bf=None, tag="ln"):
        sq = B.rot(3, [128, 512], F32, tag + "sq")
        mean_r = B.rot(2, [128, 512], F32, tag + "mean")
        rstd_r = B.rot(2, [128, 512], F32, tag + "rstd")
        tmp_r = B.rot(3, [128, 512], F32, tag + "tmp")
        for ti, (t0, tn) in enumerate(TILES):
            ps1, pn1 = psrot.next()
            ps2, pn2 = psrot.next()
            for dc in range(16):
                s, sn = sq.next()
                S.op("act", lambda e, s=s, dc=dc: e.activation(out=s[:, :tn], in_=r[:, dc, t0:t0 + tn], func=AF.Square),
                     reads=[f"r{dc}"], writes=[sn])
                S.op("pe", lambda e, dc=dc: e.matmul(ps1[:, :tn], lhsT=ones32, rhs=r[:, dc, t0:t0 + tn],
                                                      start=(dc == 0), stop=(dc == 15)),
                     reads=[f"r{dc}", "ones32"], writes=[pn1])
                S.op("pe", lambda e, dc=dc, s=s: e.matmul(ps2[:, :tn], lhsT=ones32, rhs=s[:, :tn],
                                                           start=(dc == 0), stop=(dc == 15)),
                     reads=[sn, "ones32"], writes=[pn2])
            mean, mn = mean_r.next()
            rstd, rn = rstd_r.next()
            S.op("act", lambda e: e.activation(out=mean[:, :tn], in_=ps1[:, :tn], func=AF.Identity), reads=[pn1], writes=[mn])
            S.op("act", lambda e: e.activation(out=rstd[:, :tn], in_=ps1[:, :tn], func=AF.Square), reads=[pn1], writes=[rn])
            S.op("dve", lambda e: e.tensor_tensor(out=rstd[:, :tn], in0=ps2[:, :tn], in1=rstd[:, :tn], op=ALU.subtract),
                 reads=[pn2, rn], writes=[rn])
            S.op("dve", lambda e: e.tensor_scalar(out=rstd[:, :tn], in0=rstd[:, :tn], scalar1=0.0, scalar2=EPS,
                                                  op0=ALU.max, op1=ALU.add), reads=[rn], writes=[rn])
            S.op("act", lambda e: e.activation(out=rstd[:, :tn], in_=rstd[:, :tn], func=AF.Sqrt), reads=[rn], writes=[rn])
            S.op("dve", lambda e: e.reciprocal(out=rstd[:, :tn], in_=rstd[:, :tn]), reads=[rn], writes=[rn])
            for dc in range(16):
                t, tnm = tmp_r.next()
                S.op("dve", lambda e, t=t, dc=dc: e.tensor_tensor(out=t[:, :tn], in0=r[:, dc, t0:t0 + tn], in1=mean[:, :tn],
                                                                   op=ALU.subtract), reads=[f"r{dc}", mn], writes=[tnm])
                S.op("pool", lambda e, t=t: e.tensor_tensor(out=t[:, :tn], in0=t[:, :tn], in1=rstd[:, :tn], op=ALU.mult),
                     reads=[tnm, rn], writes=[tnm])
                S.op("act", lambda e, t=t, dc=dc: e.activation(
                    out=r[:, dc, t0:t0 + tn], in_=t[:, :tn], func=AF.Identity,
                    scale=vec[:, gcol, dc:dc + 1], bias=vec[:, bcol, dc:dc + 1]),
                    reads=[tnm, "vec"], writes=[f"r{dc}"])
                if out_bf is not None:
                    S.op("pool", lambda e, dc=dc: e.tensor_copy(out=out_bf[:, dc, t0:t0 + tn], in_=r[:, dc, t0:t0 + tn]),
                         reads=[f"r{dc}"], writes=[f"xb{dc}"])

    def phase4():
        oT = fixed("p4_oT", 0, [128, 16, NQ], BF16)
        merged = fixed("p4_merged", SZ16, [128, 16, NQ], BF16)
        B = Bump(nc, [(2 * SZ16, SB_TOP)])
        for i in range(16):
            S.dma("sp", oT[:, i, :], oT_d[i], writes=[f"o{i}"])
        wr = B.rot(2, [128, KC, 512], BF16, "p4w")
        wsg = B.rot(2, [128, KC, 512], F32, "p4wsg")
        gr = B.rot(2, [128, 3, NQ], F32, "p4g")
        tr = B.rot(6, [128, 512], F32, "p4t")
        gm_v = gm_d.rearrange("(br dc) p n -> p br dc n", br=3)
        for dg in range(4):
            wt, wn = wr.next()
            sg, sgn = wsg.next()
            S.dma("sp", sg, w_br_v[:, :, dg * 512:(dg + 1) * 512], writes=[sgn])
            S.op("dve", lambda e: e.tensor_copy(out=wt, in_=sg), reads=[sgn], writes=[wn])
            for dl in range(4):
                dc = dg * 4 + dl
                gt, gn = gr.next()
                S.dma("sp", gt, gm_v[:, :, dc, :], writes=[gn])
                for ti, (t0, tn) in enumerate(TILES):
                    tmps = []
                    for br, (k0, nk) in enumerate([(0, 8), (8, 4), (12, 4)]):
                        ps, pn = psrot.next()
                        mm_fm(ps, pn, wt, wn, (dl * 128, 128), oT, lambda kc: f"o{kc}", (t0, tn), nk=nk, kc0=k0)
                        t, tnm = tr.next()
                        S.op("dve", lambda e, t=t, ps=ps, br=br: e.tensor_tensor(
                            out=t[:, :tn], in0=ps[:, :tn], in1=gt[:, br, t0:t0 + tn], op=ALU.mult),
                            reads=[pn, gn], writes=[tnm])
                        tmps.append((t, tnm))
                    (ta, na), (tb, nb), (tc, ncn) = tmps
                    S.op("pool", lambda e, ta=ta, tb=tb: e.tensor_tensor(out=ta[:, :tn], in0=ta[:, :tn], in1=tb[:, :tn], op=ALU.add),
                         reads=[na, nb], writes=[na])
                    S.op("pool", lambda e, ta=ta, tc=tc, dc=dc: e.tensor_tensor(
                        out=merged[:, dc, t0:t0 + tn], in0=ta[:, :tn], in1=tc[:, :tn], op=ALU.add),
                        reads=[na, ncn], writes=[f"m{dc}"])
        S.barrier()
        r1 = fixed("p4_r1", 2 * SZ16, [128, 16, NQ], F32)
        B = Bump(nc, [(0, SZ16), (2 * SZ16 + SZ32, SB_TOP)])
        wr = B.rot(2, [128, KC, 512], BF16, "p4wo")
        xr = B.rot(2, [128, NQ], F32, "p4x")
        wsg = B.rot(2, [128, KC, 512], F32, "p4wosg")
        for dg in range(4):
            wt, wn = wr.next()
            sg, sgn = wsg.next()
            S.dma("sp", sg, w_o_v[:, :, dg * 512:(dg + 1) * 512], writes=[sgn])
            S.op("act", lambda e: e.activation(out=wt, in_=sg, func=AF.Identity), reads=[sgn], writes=[wn])
            for dl in range(4):
                dc = dg * 4 + dl
                xt, xn = xr.next()
                S.dma("sp", xt, xq_v[:, dc, :], writes=[xn])
                for ti, (t0, tn) in enumerate(TILES):
                    ps, pn = psrot.next()
                    mm_fm(ps, pn, wt, wn, (dl * 128, 128), merged, lambda kc: f"m{kc}", (t0, tn))
                    S.op("dve", lambda e, ps=ps, xt=xt, dc=dc: e.scalar_tensor_tensor(
                        out=r1[:, dc, t0:t0 + tn], in0=xt[:, t0:t0 + tn], scalar=ALPHA, in1=ps[:, :tn],
                        op0=ALU.mult, op1=ALU.add), reads=[pn, xn], writes=[f"r{dc}"])
        S.barrier()
        x1b = fixed("p4_x1b", 0, [128, 16, NQ], BF16)
        B = Bump(nc, [(SZ16, 2 * SZ16), (2 * SZ16 + SZ32, SB_TOP)])
        vec = B.alloc([128, 5, 16], F32)
        S.dma("sp", vec, vec_fm, writes=["vec"])
        ones32 = B.alloc([128, 128], F32)
        S.op("dve", lambda e: e.memset(ones32, 1.0 / D), writes=["ones32"])
        layer_norm_fm(B, r1, 0, 1, vec, ones32, out_bf=x1b, tag="ln1")
        for dc in range(16):
            S.dma("poolq", x1_d[dc], r1[:, dc, :], reads=[f"r{dc}"])
        S.barrier()
        act = fixed("p4_act", SZ16, [128, 44, NQ], BF16)
        B = Bump(nc, [(SZ16 + SZACT, SB_TOP)])
        bup = B.alloc([128, 88], F32)
        bcv = B.alloc([128, 88], F32)
        wcv = B.alloc([128, 3, 88], F32)
        stt = B.alloc([128, 88, 4, 2], F32)
        hv = B.alloc([128, 4], F32)
        S.dma("sp", bup, bup_fm, writes=["bup"])
        S.dma("sp", bcv, bconv_fm, writes=["bcv"])
        S.dma("sp", wcv, wconv_fm, writes=["wcv"])
        S.dma("sp", stt, state_fm, writes=["stt"])
        S.dma("sp", hv, hvalid, writes=["hv"])
        convp = B.alloc([128, 88, 2], F32)
        convs = B.alloc([128, 88, 4, 2], F32)
        S.op("pool", lambda e: e.memset(act[:, :, 1024:1028], 0.0), writes=["act_halo"])
        wa_r = B.rot(2, [128, KC, 128], BF16, "wa")
        wg_r = B.rot(2, [128, KC, 128], BF16, "wg")
        wsg = B.rot(3, [128, KC, 128], F32, "wupsg")
        ub_r = B.rot(2, [128, 2, 514], F32, "ub")
        us_r = B.rot(4, [128, 4, 6], F32, "us")
        uh_r = B.rot(4, [128, 4], F32, "uh")
        cv_r = B.rot(4, [128, NQ], F32, "cv")
        t1_r = B.rot(1, [128, NQ], F32, "t1")
        t2_r = B.rot(1, [128, NQ], F32, "t2")

        def up_chunk(uc, wt, wn, lc):
            pss = [psrot.next() for _ in TILES]
            for ti, (t0, tn) in enumerate(TILES):
                ps, pn = pss[ti]
                mm_fm(ps, pn, wt, wn, (lc, 128), x1b, lambda kc: f"xb{kc}", (t0, tn))
            ub, ubn = ub_r.next()
            us, usn = us_r.next()
            uh, uhn = uh_r.next()
            bias = bup[:, uc:uc + 1]
            ps2, pn2 = pss[2]
            S.op("act", lambda e: e.activation(out=uh, in_=ps2[:, 0:4], func=AF.Identity, bias=bias, scale=1.0),
                 reads=[pn2, "bup"], writes=[uhn])
            S.op("act", lambda e: e.activation(out=us[:, :, 2:6], in_=ps2[:, 4:20].rearrange("p (s t) -> p s t", s=4),
                                               func=AF.Identity, bias=bias, scale=1.0),
                 reads=[pn2, "bup"], writes=[usn + "u"])
            S.op("pool", lambda e: e.tensor_copy(out=us[:, :, 0:2], in_=stt[:, uc, :, :]), reads=["stt"], writes=[usn + "s"])
            S.op("dve", lambda e: e.tensor_tensor(out=ub[:, :, 0:2], in0=uh.rearrange("p (s t) -> p s t", s=2),
                                                  in1=hv.rearrange("p (s t) -> p s t", s=2), op=ALU.mult),
                 reads=[uhn, "hv"], writes=[ubn + "h"])
            for s in range(2):
                ps, pn = pss[s]
                S.op("act", lambda e, s=s, ps=ps: e.activation(out=ub[:, s, 2:514], in_=ps, func=AF.Identity, bias=bias, scale=1.0),
                     reads=[pn, "bup"], writes=[ubn + f"u{s}"])
            S.op("pool", lambda e: e.tensor_copy(out=convp[:, uc, :], in_=ub[:, 1, 512:514]), reads=[ubn + "u1"], writes=["convp"])
            S.op("pool", lambda e: e.tensor_copy(out=convs[:, uc, :, :], in_=us[:, :, 4:6]), reads=[usn + "u"], writes=["convs"])
            cv, cvn = cv_r.next()
            w0, w1, w2 = (wcv[:, k, uc:uc + 1] for k in range(3))
            ub_reads = [ubn + "h", ubn + "u0", ubn + "u1"]
            cvp = cv[:, 0:1024].rearrange("p (s t) -> p s t", s=2)
            S.op("dve", lambda e: e.tensor_scalar(out=cvp, in0=ub[:, :, 0:512], scalar1=w0, scalar2=bcv[:, uc:uc + 1],
                                                  op0=ALU.mult, op1=ALU.add), reads=ub_reads + ["wcv", "bcv"], writes=[cvn])
            S.op("dve", lambda e: e.scalar_tensor_tensor(out=cvp, in0=ub[:, :, 1:513], scalar=w1, in1=cvp,
                                                         op0=ALU.mult, op1=ALU.add), reads=ub_reads + [cvn, "wcv"], writes=[cvn])
            S.op("dve", lambda e: e.scalar_tensor_tensor(out=cvp, in0=ub[:, :, 2:514], scalar=w2, in1=cvp,
                                                         op0=ALU.mult, op1=ALU.add), reads=ub_reads + [cvn, "wcv"], writes=[cvn])
            cvs = cv[:, 1028:1044].rearrange("p (s t) -> p s t", s=4)
            us_reads = [usn + "u", usn + "s"]
            S.op("dve", lambda e: e.tensor_scalar(out=cvs, in0=us[:, :, 0:4], scalar1=w0, scalar2=bcv[:, uc:uc + 1],
                                                  op0=ALU.mult, op1=ALU.add), reads=us_reads + ["wcv", "bcv"], writes=[cvn + "s"])
            S.op("dve", lambda e: e.scalar_tensor_tensor(out=cvs, in0=us[:, :, 1:5], scalar=w1, in1=cvs,
                                                         op0=ALU.mult, op1=ALU.add), reads=us_reads + [cvn + "s", "wcv"], writes=[cvn + "s"])
            S.op("dve", lambda e: e.scalar_tensor_tensor(out=cvs, in0=us[:, :, 2:6], scalar=w2, in1=cvs,
                                                         op0=ALU.mult, op1=ALU.add), reads=us_reads + [cvn + "s", "wcv"], writes=[cvn + "s"])
            return cv, cvn

        SEGS = [(0, 1024), (1028, 16)]
        for pg in range(22):
            for i in range(2):
                ia = pg * 2 + i
                wa, wan = wa_r.next()
                wg, wgn = wg_r.next()
                sg, sgn = wsg.next()
                S.dma("sp", sg, w_up_v[:, :, ia * 128:(ia + 1) * 128], writes=[sgn])
                S.op("act", lambda e: e.activation(out=wa, in_=sg, func=AF.Identity), reads=[sgn], writes=[wan])
                sg2, sgn2 = wsg.next()
                S.dma("sp", sg2, w_up_v[:, :, D_FF + ia * 128:D_FF + (ia + 1) * 128], writes=[sgn2])
                S.op("act", lambda e: e.activation(out=wg, in_=sg2, func=AF.Identity), reads=[sgn2], writes=[wgn])
                ca, can = up_chunk(ia, wa, wan, 0)
                cg, cgn = up_chunk(44 + ia, wg, wgn, 0)
                t1, t1n = t1_r.next()
                t2, t2n = t2_r.next()
                for (c0, cn) in SEGS:
                    sl = slice(c0, c0 + cn)
                    gres = cgn if c0 == 0 else cgn + "s"
                    ares = can if c0 == 0 else can + "s"
                    S.op("act", lambda e, sl=sl: e.activation(out=t1[:, sl], in_=cg[:, sl], func=AF.Square), reads=[gres], writes=[t1n])
                    S.op("dve", lambda e, sl=sl: e.tensor_scalar(out=t1[:, sl], in0=t1[:, sl], scalar1=0.044715, scalar2=1.0,
                                                                 op0=ALU.mult, op1=ALU.add), reads=[t1n], writes=[t1n])
                    S.op("pool", lambda e, sl=sl: e.tensor_tensor(out=t1[:, sl], in0=t1[:, sl], in1=cg[:, sl], op=ALU.mult),
                         reads=[t1n, gres], writes=[t1n])
                    S.op("act", lambda e, sl=sl: e.activation(out=t1[:, sl], in_=t1[:, sl], func=AF.Sigmoid, scale=GELU_C),
                         reads=[t1n], writes=[t1n])
                    S.op("pool", lambda e, sl=sl: e.tensor_tensor(out=t2[:, sl], in0=ca[:, sl], in1=cg[:, sl], op=ALU.mult),
                         reads=[ares, gres], writes=[t2n])
                    S.op("dve", lambda e, sl=sl, ia=ia: e.tensor_tensor(out=act[:, ia, sl], in0=t1[:, sl], in1=t2[:, sl], op=ALU.mult),
                         reads=[t1n, t2n], writes=[f"a{ia}"])
        S.dma("poolq", convp_o, convp, reads=["convp"])
        S.dma("poolq", convs_o, convs, reads=["convs"])
        S.barrier()
        r2 = fixed("p4_r2", SZ16 + SZACT, [128, 16, NQ], F32)
        B = Bump(nc, [(0, SZ16), (SZ16 + SZACT + SZ32, SB_TOP)])
        vec = B.alloc([128, 5, 16], F32)
        S.dma("sp", vec, vec_fm, writes=["vec"])
        ones32 = B.alloc([128, 128], F32)
        S.op("dve", lambda e: e.memset(ones32, 1.0 / D), writes=["ones32"])
        wd_r = B.rot(2, [128, 44, 128], BF16, "wd")
        wdsg = B.rot(1, [128, 22, 128], F32, "wdsg")
        x1_r = B.rot(2, [128, NQ], F32, "x1c")
        tm_r = B.rot(2, [128, 512], F32, "dtm")
        for dc in range(16):
            wt, wn = wd_r.next()
            for hf in range(2):
                sg, sgn = wdsg.next()
                S.dma("sp", sg, w_down_v[:, 22 * hf:22 * hf + 22, dc * 128:(dc + 1) * 128], writes=[sgn])
                S.op("dve", lambda e: e.tensor_copy(out=wt[:, 22 * hf:22 * hf + 22, :], in_=sg), reads=[sgn], writes=[wn + "ab"[hf]])
            xt, xn = x1_r.next()
            S.dma("sp", xt, x1_d[dc], writes=[xn])
            for ti, (t0, tn) in enumerate(TILES):
                ps, pn = psrot.next()
                for kc in range(44):
                    S.op("pe", lambda e, kc=kc, ps=ps: e.matmul(ps[:, :tn], lhsT=wt[:, kc, :], rhs=act[:, kc, t0:t0 + tn],
                                                               start=(kc == 0), stop=(kc == 43)),
                         reads=[wn + "ab"[kc // 22], f"a{kc}"] + (["act_halo"] if ti == 2 else []), writes=[pn])
                tm, tmn = tm_r.next()
                S.op("act", lambda e, ps=ps, tm=tm, dc=dc: e.activation(out=tm[:, :tn], in_=ps[:, :tn], func=AF.Identity,
                                                                         bias=vec[:, 4, dc:dc + 1], scale=1.0),
                     reads=[pn, "vec"], writes=[tmn])
                S.op("dve", lambda e, tm=tm, xt=xt, dc=dc: e.scalar_tensor_tensor(
                    out=r2[:, dc, t0:t0 + tn], in0=xt[:, t0:t0 + tn], scalar=ALPHA, in1=tm[:, :tn],
                    op0=ALU.mult, op1=ALU.add), reads=[tmn, xn], writes=[f"r{dc}"])
        S.barrier()
        B = Bump(nc, [(0, SZ16), (SZ16 + SZACT + SZ32, SB_TOP)])
        vec = B.alloc([128, 5, 16], F32)
        S.dma("sp", vec, vec_fm, writes=["vec"])
        ones32 = B.alloc([128, 128], F32)
        S.op("dve", lambda e: e.memset(ones32, 1.0 / D), writes=["ones32"])
        layer_norm_fm(B, r2, 2, 3, vec, ones32, out_bf=None, tag="ln2")
        for dc in range(16):
            S.dma("poolq", yq[dc], r2[:, dc, :], reads=[f"r{dc}"])
        S.barrier()

    SLOPES = [2.0 ** (-(h + 1)) for h in range(8)]
    KCH = [0, 128, 256, 384, 512, 640, 1024, 1152, 1280, 1408]
    KV_OF_CI = [0, 1, 2, 3, 4, 5, 8, 9, 10, 11]
    V_CHUNKS = [6, 7, 12, 13, 14, 15]
    LS = 8320

    pool_in = din("pool", [npages * 128, 2048])
    pt_in = din("pt", [1, 256], I32)
    rowid_in = din("rowid", [128, 1])
    cwin_in = din("cwin", [4, 512, 512])
    cmem_in = din("cmem", [4, 256, 1024])
    ident_in = din("ident", [128, 128])
    uinc_in = din("uinc", [128, 128])
    qpos_bc_in = din("qpos_bc", [128, NQ])
    refs_in = din("refs", [128, 14])
    qposT_in = din("qposT", [128, 13])
    curT_in = din("curT", [128, 13])
    kpos_seq_in = din("kpos_seq", [128, 32])
    kpos_smp_in = din("kpos_smp", [128, 65])
    wpos_in = din("wpos", [128, 20])
    wpos_smp_in = din("wpos_smp", [128, 5])
    cend_in = din("cend", [128, 4])
    cend_seq_in = din("cend_seq", [128, 2])
    blk_in = din("blk", [128, 2, 129])
    ov_in = din("ov", [128, 4, 129])
    w1_in = din("w1c", [2, 128, 32, 128])
    w2_in = din("w2c", [2, 128, 128])
    pe_in = din("pec", [2, 128, 32])

    wins_o = dout("wins_o", [4, 512, 512])

    kTs_d = dscr("kTs_d", [4, 10, 128, LS], BF16)
    vs_d = dscr("vs_d", [4, LS, 768], BF16)
    wkTs_d = dscr("wkTs_d", [4, 2, 128, 640], BF16)
    wvs_d = dscr("wvs_d", [4, 640, 256], BF16)
    mkTs_d = dscr("mkTs_d", [4, 4, 128, 256], BF16)
    mvs_d = dscr("mvs_d", [4, 256, 512], BF16)
    selT_d = dscr("selT_d", [7, 2, 129, 512], BF16)

    def phase3_prepare():
        B = Bump(nc, [(0, SB_TOP)])
        ident = B.alloc([128, 128], F32)
        S.dma("sp", ident, ident_in, writes=["ident"])
        pti = B.alloc([128, 256], I32)
        ptf = B.alloc([128, 256], F32)
        rid = B.alloc([128, 1], F32)
        idx = B.alloc([128, 256], I32)
        S.dma("sp", pti, pt_in.partition_broadcast(128), writes=["pti"])
        S.dma("sp", rid, rowid_in, writes=["rid"])
        S.op("dve", lambda e: e.tensor_copy(out=ptf, in_=pti), reads=["pti"], writes=["ptf"])
        S.op("dve", lambda e: e.tensor_scalar(out=ptf, in0=ptf, scalar1=128.0, scalar2=rid[:, 0:1], op0=ALU.mult, op1=ALU.add),
             reads=["ptf", "rid"], writes=["ptf"])
        S.op("dve", lambda e: e.tensor_copy(out=idx, in_=ptf), reads=["ptf"], writes=["idx"])
        zt = B.alloc([128, 10, 124], BF16)
        S.op("pool", lambda e: e.memset(zt, 0.0), writes=["zt"])
        kvn = B.alloc([128, 20, 16], F32)
        kvnb = B.alloc([128, 20, 16], BF16)
        S.dma("sp", kvn, kvq.rearrange("c p n -> p c n")[:, :, 1028:1044], writes=["kvn"])
        S.op("dve", lambda e: e.tensor_copy(out=kvnb, in_=kvn), reads=["kvn"], writes=["kvnb"])
        tok32 = B.alloc([16, 10, 128], F32)
        tokb = B.alloc([16, 10, 128], BF16)
        tl = V_CHUNKS + [16, 17, 18, 19]
        for g0 in (0, 4, 8):
            ps, pn = psrot.next()
            n = min(4, 10 - g0)
            for i in range(n):
                S.op("pe", lambda e, i=i: e.transpose(out=ps[:16, i * 128:(i + 1) * 128], in_=kvn[:, tl[g0 + i], :], identity=ident),
                     reads=["kvn", "ident"], writes=[pn])
            S.op("act", lambda e: e.activation(out=tok32[:, g0:g0 + n, :].rearrange("p a b -> p (a b)"), in_=ps[:16, :n * 128], func=AF.Identity),
                 reads=[pn], writes=["tok32"])
        S.op("dve", lambda e: e.tensor_copy(out=tokb, in_=tok32), reads=["tok32"], writes=["tokb"])
        pg = B.rot(4, [128, 2048], F32, "pg")
        kst = B.rot(2, [128, 10, 128], BF16, "kst")
        vst = B.rot(2, [128, 768], BF16, "vst")
        pgq = {}

        def issue_gather(s_, p_):
            t_, tn_ = pg.next()
            S.idma(t_, pool_in, idx[:, s_ * 64 + p_:s_ * 64 + p_ + 1], reads=["idx"], writes=[tn_])
            pgq[(s_, p_)] = (t_, tn_)

        for s in range(4):
            issue_gather(s, 0)
            issue_gather(s, 1)
            for p in range(64):
                if p + 2 < 64:
                    issue_gather(s, p + 2)
                t, tn = pgq.pop((s, p))
                ks, ksn = kst.next()
                for (g0, g1) in ((0, 4), (4, 8), (8, 10)):
                    ps, pn = psrot.next()
                    for ci in range(g0, g1):
                        S.op("pe", lambda e, ci=ci: e.transpose(out=ps[:, (ci - g0) * 128:(ci - g0 + 1) * 128],
                                                                 in_=t[:, KCH[ci]:KCH[ci] + 128], identity=ident),
                             reads=[tn, "ident"], writes=[pn])
                    eng = "act" if g0 == 0 else "dve"
                    if eng == "act":
                        S.op("act", lambda e: e.activation(out=ks[:, g0:g1, :].rearrange("p a b -> p (a b)"),
                                                           in_=ps[:, :(g1 - g0) * 128], func=AF.Identity), reads=[pn], writes=[ksn])
                    else:
                        S.op("dve", lambda e: e.tensor_copy(out=ks[:, g0:g1, :].rearrange("p a b -> p (a b)"),
                                                            in_=ps[:, :(g1 - g0) * 128]), reads=[pn], writes=[ksn])
                S.dma("sp", kTs_d[s].rearrange("c p n -> p c n")[:, :, p * 128:(p + 1) * 128], ks, reads=[ksn])
                vs, vsn = vst.next()
                S.op("dve", lambda e: e.tensor_copy(out=vs[:, 0:256], in_=t[:, 768:1024]), reads=[tn], writes=[vsn])
                S.op("act", lambda e: e.activation(out=vs[:, 256:768], in_=t[:, 1536:2048], func=AF.Identity), reads=[tn], writes=[vsn])
                S.dma("sp", vs_d[s, p * 128:(p + 1) * 128, :], vs, reads=[vsn])
            kview = kTs_d[s].rearrange("c p n -> p c n")
            for ci in range(10):
                S.dma("sp", kTs_d[s, ci, :, 8192:8196], kvnb[:, KV_OF_CI[ci], 4 * s:4 * s + 4], reads=["kvnb"])
            S.dma("sp", kview[:, :, 8196:8320], zt, reads=["zt"])
            S.dma("sp", vs_d[s, 8192:8196, :], tokb[4 * s:4 * s + 4, 0:6, :].rearrange("p a b -> p (a b)"), reads=["tokb"])
            S.dma("sp", vs_d[s, 8196:8320, :], zt[:124, 0:7, :].rearrange("p a b -> p (a b)")[:, 0:768], reads=["zt"])
            for tt in range(4):
                t, tn = pg.next()
                S.dma("sp", t[:, 0:512], cwin_in[s, tt * 128:(tt + 1) * 128, :], writes=[tn])
                ks, ksn = kst.next()
                ps, pn = psrot.next()
                for g in range(2):
                    S.op("pe", lambda e, g=g: e.transpose(out=ps[:, g * 128:(g + 1) * 128], in_=t[:, g * 128:(g + 1) * 128], identity=ident),
                         reads=[tn, "ident"], writes=[pn])
                S.op("act", lambda e: e.activation(out=ks[:, 0:2, :].rearrange("p a b -> p (a b)"), in_=ps[:, :256], func=AF.Identity),
                     reads=[pn], writes=[ksn])
                S.dma("sp", wkTs_d[s].rearrange("c p n -> p c n")[:, :, tt * 128:(tt + 1) * 128], ks[:, 0:2, :], reads=[ksn])
                vs, vsn = vst.next()
                S.op("pool", lambda e: e.tensor_copy(out=vs[:, 0:256], in_=t[:, 256:512]), reads=[tn], writes=[vsn])
                S.dma("sp", wvs_d[s, tt * 128:(tt + 1) * 128, :], vs[:, 0:256], reads=[vsn])
            for g in range(2):
                S.dma("sp", wkTs_d[s, g, :, 512:516], kvnb[:, 16 + g, 4 * s:4 * s + 4], reads=["kvnb"])
            S.dma("sp", wkTs_d[s].rearrange("c p n -> p c n")[:, :, 516:640], zt[:, 0:2, :], reads=["zt"])
            S.dma("sp", wvs_d[s, 512:516, :], tokb[4 * s:4 * s + 4, 8:10, :].rearrange("p a b -> p (a b)"), reads=["tokb"])
            S.dma("sp", wvs_d[s, 516:640, :], zt[:124, 0:3, :].rearrange("p a b -> p (a b)")[:, 0:256], reads=["zt"])
            S.dma("sp", wins_o[s, 0:508, :], cwin_in[s, 4:512, :])
            S.dma("sp", wins_o[s, 508:512, :], tok32[4 * s:4 * s + 4, 6:10, :].rearrange("p a b -> p (a b)"), reads=["tok32"])
            for tt in range(2):
                t, tn = pg.next()
                S.dma("sp", t[:, 0:1024], cmem_in[s, tt * 128:(tt + 1) * 128, :], writes=[tn])
                ks, ksn = kst.next()
                ps, pn = psrot.next()
                for h in range(4):
                    S.op("pe", lambda e, h=h: e.transpose(out=ps[:, h * 128:(h + 1) * 128], in_=t[:, h * 128:(h + 1) * 128], identity=ident),
                         reads=[tn, "ident"], writes=[pn])
                S.op("act", lambda e: e.activation(out=ks[:, 0:4, :].rearrange("p a b -> p (a b)"), in_=ps, func=AF.Identity),
                     reads=[pn], writes=[ksn])
                S.dma("sp", mkTs_d[s].rearrange("c p n -> p c n")[:, :, tt * 128:(tt + 1) * 128], ks[:, 0:4, :], reads=[ksn])
                vs, vsn = vst.next()
                S.op("pool", lambda e: e.tensor_copy(out=vs[:, 0:512], in_=t[:, 512:1024]), reads=[tn], writes=[vsn])
                S.dma("sp", mvs_d[s, tt * 128:(tt + 1) * 128, :], vs[:, 0:512], reads=[vsn])
        S.barrier()

    def mkjobs():
        jobs = []
        seq = dict(kT=kT_d, v=v_d, mkT=mkT_d, mv=mv_d, wkT=wkT_d, wv=wv_d, kpos="seq", wposk="seq", n_cmp=255, n_blk=64, L=4096)
        jobs.append(dict(seq, name="J0", ji=0, c0=0, n=512, asubs=[(i * 128, 128, i) for i in range(4)],
                         sblks=[(i * 128, 128, i) for i in range(4)], nkt=16, win=[(0, 512, 0, 10)]))
        jobs.append(dict(seq, name="J1", ji=1, c0=512, n=512, asubs=[(i * 128, 128, 4 + i) for i in range(4)],
                         sblks=[(i * 128, 128, 4 + i) for i in range(4)], nkt=32, win=[(0, 512, 10, 10)]))
        jobs.append(dict(seq, name="H", ji=2, c0=1024, n=4, asubs=[(0, 2, 8), (2, 2, 9)], sblks=[(0, 4, 8)], nkt=32,
                         win=[(0, 2, 0, 10), (2, 2, 10, 10)]))
        for s in range(4):
            jobs.append(dict(kT=kTs_d[s], v=vs_d[s], mkT=mkTs_d[s], mv=mvs_d[s], wkT=wkTs_d[s], wv=wvs_d[s], kpos="smp", wposk="smp",
                             n_cmp=511, n_blk=129, L=8192, name=f"S{s}", ji=3 + s, c0=1028 + 4 * s, n=4,
                             asubs=[(0, 4, 10 + s)], sblks=[(0, 4, 9 + s)], nkt=65, win=[(0, 4, 0, 5)]))
        return jobs

    def phase3_attention():
        B = Bump(nc, [(0, SB_TOP)])
        ident = B.alloc([128, 128], F32)
        uinc32 = B.alloc([128, 128], F32)
        uinc = B.alloc([128, 128], BF16)
        ones_bf = B.alloc([128, 128], BF16)
        ones32 = B.alloc([128, 128], F32)
        S.dma("sp", ident, ident_in, writes=["ident"])
        S.dma("sp", uinc32, uinc_in, writes=["uinc32"])
        S.op("dve", lambda e: e.tensor_copy(out=uinc, in_=uinc32), reads=["uinc32"], writes=["uinc"])
        S.op("dve", lambda e: e.memset(ones_bf, 1.0), writes=["ones_bf"])
        S.op("dve", lambda e: e.memset(ones32, 1.0), writes=["ones32"])
        tabs = {}
        for nm, ap, shp in (("qpos_bc", qpos_bc_in, [128, NQ]), ("refs", refs_in, [128, 14]), ("qposT", qposT_in, [128, 13]),
                            ("curT", curT_in, [128, 13]), ("kpos_seq", kpos_seq_in, [128, 32]), ("kpos_smp", kpos_smp_in, [128, 65]),
                            ("wpos_seq", wpos_in, [128, 20]), ("wpos_smp", wpos_smp_in, [128, 5]), ("cend", cend_in, [128, 4]), ("cend_seq", cend_seq_in, [128, 2]),
                            ("blk", blk_in, [128, 2, 129]), ("ov", ov_in, [128, 4, 129])):
            t = B.alloc(shp, F32)
            S.dma("sp", t, ap, writes=[nm])
            tabs[nm] = t
        qpos_bc = tabs["qpos_bc"]
        Qsb = B.alloc([128, 16, NQ], BF16)
        for i in range(16):
            S.dma("sp", Qsb[:, i, :], qT_d[i], writes=[f"Q{i}"])
        gn = B.alloc([24, NQ], F32)
        S.dma("sp", gn, gn_d, writes=["gn"])
        sel24 = B.alloc([24, 24, 128], F32)
        S.op("pool", lambda e: e.memset(sel24, 0.0), writes=["sel24"])
        for c in range(24):
            pass
        S.op("dve", lambda e: e.tensor_tensor(out=sel24, in0=sel24, in1=ident[:24, 0:24].unsqueeze(2).to_broadcast([24, 24, 128]), op=ALU.add),
             reads=["sel24", "ident"], writes=["sel24"])
        oacc = B.alloc([128, 8, NQ], F32)
        halfones = B.alloc([8, 2, 128], F32)
        S.op("pool", lambda e: e.memset(halfones, 0.0), writes=["halfones"])
        S.op("pool", lambda e: e.memset(halfones[:, 0, 0:64], 1.0), reads=["halfones"], writes=["halfones"])
        S.op("pool", lambda e: e.memset(halfones[:, 1, 64:128], 1.0), reads=["halfones"], writes=["halfones"])
        w1 = B.alloc([128, 2, 32, 128], BF16)
        w2 = B.alloc([128, 2, 128], BF16)
        pe = B.alloc([128, 2, 32], BF16)
        for i in range(2):
            S.dma("poolq", w1[:, i], w1_in[i], writes=[f"w1_{i}"])
            S.dma("poolq", w2[:, i], w2_in[i], writes=[f"w2_{i}"])
            S.dma("poolq", pe[:, i], pe_in[i], writes=[f"pe_{i}"])
        cmp_cache = {}
        for g_ in range(2):
            cmp_cache[g_] = dict(kT=B.alloc([128, 256], BF16), v=B.alloc([128, 2, 128], BF16), done=False)
        mark = [list(r) for r in B.ranges]

        def kpos_tab(job):
            return tabs["kpos_seq"] if job["kpos"] == "seq" else tabs["kpos_smp"]

        def bias_table(Bj, job, kp, nkt, heads, tag):
            out = {}
            for (a0, an, ai) in job["asubs"]:
                for h in heads:
                    t = Bj.alloc([128, nkt], F32)
                    rn = f"bt_{tag}_{ai}_{h}"
                    S.op("dve", lambda e, t=t, ai=ai, h=h: e.tensor_scalar(out=t, in0=kp[:, :nkt], scalar1=tabs["refs"][:, ai:ai + 1],
                                                                         scalar2=SLOPES[h], op0=ALU.subtract, op1=ALU.mult),
                         reads=["refs", "kpos_seq", "kpos_smp", "wpos_seq", "wpos_smp", "cend", "cend_seq"], writes=[rn])
                    S.op("dve", lambda e, t=t: e.tensor_scalar(out=t, in0=t, scalar1=0.0, scalar2=None, op0=ALU.min),
                         reads=[rn], writes=[rn])
                    out[(ai, h)] = (t, rn)
            return out

        def softmax_head(Bj, job, cols, qchunk, kT_sb, kres, v_fn, vres, nkt, bias_fn, mask_fn, pools, head_slope_idx=None):
            c0, n = cols
            (Ob, On), (Db, Dn) = pools["acc"].next()
            LA = 3
            stl = {}

            def emit_S(k_):
                ps_, pn_ = psrot4.next()
                S.op("pe", lambda e: e.matmul(ps_[:, :n], lhsT=kT_sb[:, k_ * 128:(k_ + 1) * 128], rhs=Qsb[:, qchunk, c0:c0 + n],
                                              start=True, stop=True), reads=[kres, f"Q{qchunk}"], writes=[pn_])
                stl[k_] = (ps_, pn_)

            for k_ in range(min(LA, nkt)):
                emit_S(k_)
            for kt in range(nkt):
                if kt + LA < nkt:
                    emit_S(kt + LA)
                ps, pn = stl.pop(kt)
                pf, pfn = pools["pf"].next()
                asl = job["asubs"]
                if bias_fn is not None and len(asl) == 4 and head_slope_idx is not None:
                    gs = 1 if head_slope_idx == 0 else (2 if head_slope_idx == 1 else 4)
                    asl = [(asl[i][0], sum(x[1] for x in asl[i:i + gs]), asl[i + gs - 1][2]) for i in range(0, 4, gs)]
                for (a0, an, ai) in asl:
                    lo, hi = max(a0, c0 - job["c0"]), min(a0 + an, c0 - job["c0"] + n)
                    if lo >= hi:
                        continue
                    l0 = lo - (c0 - job["c0"])
                    if bias_fn is None:
                        S.op("act", lambda e: e.activation(out=pf[:, l0:l0 + hi - lo], in_=ps[:, l0:l0 + hi - lo], func=AF.Exp, scale=SCALE),
                             reads=[pn], writes=[pfn])
                    else:
                        bt, brn = bias_fn(ai)
                        S.op("act", lambda e: e.activation(out=pf[:, l0:l0 + hi - lo], in_=ps[:, l0:l0 + hi - lo], func=AF.Exp, scale=SCALE,
                                                           bias=bt[:, kt:kt + 1]), reads=[pn, brn], writes=[pfn])
                if mask_fn is not None:
                    mk, mkn = mask_fn(kt)
                    pm, pmn = pools["pm"].next()
                    S.op("pool" if (kt % 4 == 3) else "dve", lambda e: e.tensor_tensor(out=pm[:, :n], in0=pf[:, :n], in1=mk, op=ALU.mult),
                         reads=[pfn, mkn], writes=[pmn])
                else:
                    pm, pmn = pf, pfn
                vt = v_fn(kt)
                S.op("pe", lambda e: e.matmul(Ob[:, :n], lhsT=vt, rhs=pm[:, :n], start=(kt == 0), stop=(kt == nkt - 1)),
                     reads=[vres, pmn], writes=[On])
                S.op("pe", lambda e: e.matmul(Db[:, :n], lhsT=ones_bf, rhs=pm[:, :n], start=(kt == 0), stop=(kt == nkt - 1)),
                     reads=["ones_bf", pmn], writes=[Dn])
            return (Ob, On), (Db, Dn)

        def normalize(Bj, cols, O, Dn, pools):
            (Ob, On), (Db, Dnn) = O, Dn
            c0, n = cols
            rd, rdn = pools["rd"].next()
            S.op("dve", lambda e: e.tensor_scalar(out=rd[:, :n], in0=Db[:, :n], scalar1=1e-30, scalar2=None, op0=ALU.max),
                 reads=[Dnn], writes=[rdn])
            S.op("dve", lambda e: e.reciprocal(out=rd[:, :n], in_=rd[:, :n]), reads=[rdn], writes=[rdn])
            t, tn_ = pools["nt"].next()
            S.op("dve", lambda e: e.tensor_tensor(out=t[:, :n], in0=Ob[:, :n], in1=rd[:, :n], op=ALU.mult), reads=[On, rdn], writes=[tn_])
            return t, tn_, rd, rdn

        def gated_acc(cols, h, br, t, tn_, first, toff=0):
            c0, n = cols
            t = t[:, toff:]
            ps, pn = psrot4.next()
            c = br * 8 + h
            S.op("pe", lambda e: e.matmul(ps[:, :n], lhsT=sel24[:, c, :], rhs=gn[:, c0:c0 + n], start=True, stop=True),
                 reads=["sel24", "gn"], writes=[pn])
            if first:
                S.op("dve", lambda e: e.tensor_tensor(out=oacc[:, h, c0:c0 + n], in0=t[:, :n], in1=ps[:, :n], op=ALU.mult),
                     reads=[tn_, pn], writes=[f"oacc{h}"])
            else:
                S.op("dve", lambda e: e.tensor_tensor(out=t[:, :n], in0=t[:, :n], in1=ps[:, :n], op=ALU.mult), reads=[tn_, pn], writes=[tn_])
                S.op("pool", lambda e: e.tensor_tensor(out=oacc[:, h, c0:c0 + n], in0=oacc[:, h, c0:c0 + n], in1=t[:, :n], op=ALU.add),
                     reads=[tn_, f"oacc{h}"], writes=[f"oacc{h}"])

        def small_group(Bs, job, cols, g, br, kT_sb, kres, v_fn, vres, nkt, kp_ap, mask_all, mres):
            c0, n = cols
            W = 4 * n
            pools_ = std_pools(Bs, W)
            BT = Bs.alloc([128, nkt, 4, n], F32)
            for hi in range(4):
                for q in range(n):
                    S.op("dve", lambda e: e.tensor_scalar(out=BT[:, :, hi, q], in0=kp_ap, scalar1=qpos_bc[:, c0 + q:c0 + q + 1], scalar2=SLOPES[4 * g + hi] / SCALE,
                                                          op0=ALU.subtract, op1=ALU.mult),
                         reads=["qpos_bc", "kpos_seq", "kpos_smp", "wpos_seq", "wpos_smp"], writes=["BT"])
            BT2 = BT.rearrange("p k h q -> p (k h q)")
            S.op("dve", lambda e: e.tensor_scalar(out=BT2, in0=BT2, scalar1=0.0, scalar2=None, op0=ALU.min), reads=["BT"], writes=["BT"])
            (Ob, On), (Db, Dn) = pools_["acc"].next()
            argr = Bs.rot(3, [128, W], F32, "sarg")
            pr = Bs.rot(3, [128, W], BF16, "spp")
            pmr = Bs.rot(3, [128, W], BF16, "spm")
            LA = 2
            stl = {}

            def emit_S(k_):
                ps_, pn_ = psrot4.next()
                for hi in range(4):
                    S.op("pe", lambda e: e.matmul(ps_[:, hi * n:(hi + 1) * n], lhsT=kT_sb[:, k_ * 128:(k_ + 1) * 128], rhs=Qsb[:, 4 * g + hi, c0:c0 + n],
                                                  start=True, stop=True), reads=[kres, f"Q{4 * g + hi}"], writes=[pn_])
                stl[k_] = (ps_, pn_)

            for k_ in range(min(LA, nkt)):
                emit_S(k_)
            for kt in range(nkt):
                if kt + LA < nkt:
                    emit_S(kt + LA)
                ps, pn = stl.pop(kt)
                ar, arn = argr.next()
                S.op("dve", lambda e: e.tensor_tensor(out=ar, in0=ps[:, :W], in1=BT[:, kt].rearrange("p h q -> p (h q)"), op=ALU.add),
                     reads=[pn, "BT"], writes=[arn])
                p_, ppn = pr.next()
                S.op("act", lambda e: e.activation(out=p_, in_=ar, func=AF.Exp, scale=SCALE), reads=[arn], writes=[ppn])
                if mask_all is not None:
                    pm, pmn = pmr.next()
                    S.op("dve", lambda e: e.tensor_tensor(out=pm.rearrange("p (h q) -> p h q", h=4), in0=p_.rearrange("p (h q) -> p h q", h=4),
                                                           in1=mask_all[:, kt, :n].unsqueeze(1).to_broadcast([128, 4, n]), op=ALU.mult),
                         reads=[ppn, mres], writes=[pmn])
                else:
                    pm, pmn = p_, ppn
                vt = v_fn(kt)
                S.op("pe", lambda e: e.matmul(Ob[:, :W], lhsT=vt, rhs=pm, start=(kt == 0), stop=(kt == nkt - 1)), reads=[vres, pmn], writes=[On])
                S.op("pe", lambda e: e.matmul(Db[:, :W], lhsT=ones_bf, rhs=pm, start=(kt == 0), stop=(kt == nkt - 1)), reads=["ones_bf", pmn], writes=[Dn])
            t, tn_, _, _ = normalize(Bs, (c0, W), (Ob, On), (Db, Dn), pools_)
            for hi in range(4):
                gated_acc(cols, 4 * g + hi, br, t, tn_, first=False, toff=hi * n)

        def std_pools(Bj, n):
            nn = max(n, 8)
            return dict(
                acc=Rot([((PSA[:, 4, :], "ps4"), (PSA[:, 5, :], "ps5")), ((PSA[:, 6, :], "ps6"), (PSA[:, 7, :], "ps7"))]),
                pf=Bj.rot(3, [128, nn], BF16, "pf"), pm=Bj.rot(3, [128, nn], BF16, "pm"),
                rd=Bj.rot(2, [128, nn], F32, "rd"), nt=Bj.rot(3, [128, nn], F32, "nt"))

        def mem_branch(job):
            S.barrier()
            Bj = Bump(nc, [list(r) for r in mark])
            n = job["n"]
            cols = (job["c0"], n)
            pools = std_pools(Bj, n)
            kT = Bj.alloc([128, 4, 256], BF16)
            vv = Bj.alloc([128, 2, 512], BF16)
            S.dma("sp", kT, job["mkT"].rearrange("c p n -> p c n"), writes=["mkT"])
            S.dma("sp", vv, job["mv"].rearrange("(t p) n -> p t n", p=128), writes=["mv"])
            ost = Bj.rot(2, [128, max(n, 8)], BF16, "ost")
            for h in range(4):
                O, Dn = softmax_head(Bj, job, cols, 12 + h, kT[:, h, :], "mkT", lambda kt: vv[:, kt, h * 128:(h + 1) * 128], "mv", 2,
                                     None, None, pools)
                t, tn_, _, _ = normalize(Bj, cols, O, Dn, pools)
                o, on = ost.next()
                S.op("act", lambda e: e.activation(out=o[:, :n], in_=t[:, :n], func=AF.Identity), reads=[tn_], writes=[on])
                S.dma("sp", oT_d[12 + h, :, cols[0]:cols[0] + n], o[:, :n], reads=[on])

        psrot6 = Rot([(PS[i], f"ps{i}") for i in range(6)])

        def sb_steps(n, nkt, nch, load_fn, z_fn, pv_fn, mask_fn, Bj):
            nn = max(n, 8)
            f32r = {k: Bj.rot(2 * nch, [128, nn], F32, k) for k in ("e", "sp", "zs", "arg", "a")}
            b16r = {k: Bj.rot(2 * nch, [128, nn], BF16, k) for k in ("spm", "am")}
            mr = Bj.rot(3, [128, nn], BF16, "m")
            carry = [Bj.alloc([128, nn], F32) for _ in range(nch)]
            for ch in range(nch):
                S.op("pool", lambda e, ch=ch: e.memset(carry[ch], 0.0), writes=[f"carry{ch}"])
            order = list(range(nkt - 1, -1, -1))
            zt = {}
            la = 1 if nch == 1 else 0
            zrot = Rot([(PS[0], "ps0"), (PS[1], "ps1")])
            crot = Rot([(PS[i], f"ps{i}") for i in (2, 3, 4, 5)])

            def stageA(i):
                kt = order[i]
                load_fn(kt)
                m, mn = mr.next()
                mask_fn(kt, m, mn)
                zs_ = []
                for ch in range(nch):
                    zp, zpn = zrot.next()
                    z_fn(ch, kt, zp, zpn)
                    zs_.append((zp, zpn))
                zt[i] = (m, mn, zs_)

            if la:
                stageA(0)
            for i in range(nkt):
                if la:
                    if i + 1 < nkt:
                        stageA(i + 1)
                else:
                    stageA(i)
                kt = order[i]
                m, mn, zs_ = zt.pop(i)
                st = [dict() for _ in range(nch)]
                for ch in range(nch):
                    zp, zpn = zs_[ch]
                    ee, een = f32r["e"].next()
                    S.op("act", lambda e: e.activation(out=ee[:, :n], in_=zp[:, :n], func=AF.Exp, scale=SCALE), reads=[zpn], writes=[een])
                    st[ch].update(ee=ee, een=een, zp=zp, zpn=zpn)
                for ch in range(nch):
                    d = st[ch]
                    sp, spn = f32r["sp"].next()
                    S.op("act", lambda e: e.activation(out=sp[:, :n], in_=d["ee"][:, :n], func=AF.Ln, bias=1.0, scale=1.0), reads=[d["een"]], writes=[spn])
                    d.update(sp=sp, spn=spn)
                for ch in range(nch):
                    d = st[ch]
                    zs, zsn = f32r["zs"].next()
                    S.op("act", lambda e: e.activation(out=zs[:, :n], in_=d["zp"][:, :n], func=AF.Identity, scale=SCALE), reads=[d["zpn"]], writes=[zsn])
                    d.update(zs=zs, zsn=zsn)
                for ch in range(nch):
                    d = st[ch]
                    spm, spmn = b16r["spm"].next()
                    S.op("dve", lambda e: e.tensor_tensor(out=spm[:, :n], in0=d["sp"][:, :n], in1=m[:, :n], op=ALU.mult), reads=[d["spn"], mn], writes=[spmn])
                    d.update(spm=spm, spmn=spmn)
                for ch in range(nch):
                    d = st[ch]
                    cp, cpn = crot.next()
                    S.op("pe", lambda e: e.matmul(cp[:, :n], lhsT=uinc, rhs=d["spm"][:, :n], start=True, stop=True), reads=["uinc", d["spmn"]], writes=[cpn])
                    tp, tpn = crot.next()
                    S.op("pe", lambda e: e.matmul(tp[:, :n], lhsT=ones_bf, rhs=d["spm"][:, :n], start=True, stop=True), reads=["ones_bf", d["spmn"]], writes=[tpn])
                    d.update(cp=cp, cpn=cpn, tp=tp, tpn=tpn)
                for ch in range(nch):
                    d = st[ch]
                    ar, arn = f32r["arg"].next()
                    S.op("dve", lambda e: e.tensor_tensor(out=ar[:, :n], in0=d["zs"][:, :n], in1=d["cp"][:, :n], op=ALU.subtract), reads=[d["zsn"], d["cpn"]], writes=[arn])
                    d.update(ar=ar, arn=arn)
                for ch in range(nch):
                    d = st[ch]
                    S.op("dve", lambda e: e.tensor_tensor(out=d["ar"][:, :n], in0=d["ar"][:, :n], in1=carry[ch][:, :n], op=ALU.subtract),
                         reads=[d["arn"], f"carry{ch}"], writes=[d["arn"]])
                for ch in range(nch):
                    d = st[ch]
                    S.op("dve", lambda e: e.tensor_tensor(out=carry[ch][:, :n], in0=carry[ch][:, :n], in1=d["tp"][:, :n], op=ALU.add),
                         reads=[f"carry{ch}", d["tpn"]], writes=[f"carry{ch}"])
                for ch in range(nch):
                    d = st[ch]
                    aa, aan = f32r["a"].next()
                    S.op("act", lambda e: e.activation(out=aa[:, :n], in_=d["ar"][:, :n], func=AF.Exp), reads=[d["arn"]], writes=[aan])
                    d.update(aa=aa, aan=aan)
                for ch in range(nch):
                    d = st[ch]
                    am, amn = b16r["am"].next()
                    S.op("dve" if n > 64 else "pool", lambda e: e.tensor_tensor(out=am[:, :n], in0=d["aa"][:, :n], in1=m[:, :n], op=ALU.mult), reads=[d["aan"], mn], writes=[amn])
                    d.update(am=am, amn=amn)
                for ch in range(nch):
                    d = st[ch]
                    pv_fn(ch, kt, d["am"], d["amn"], i == 0, i == nkt - 1)

        def sb_branch(job):
            n, c0, nkt = job["n"], job["c0"], job["nkt"]
            nn = max(n, 8)
            kp = kpos_tab(job)
            Lp = nkt * 128
            for h0 in (0, 2):
                S.barrier()
                Bj = Bump(nc, [list(r) for r in mark])
                kT = [Bj.alloc([128, Lp], BF16) for _ in range(2)]
                vv = [Bj.alloc([128, nkt, 128], BF16) for _ in range(2)]
                ost = Bj.rot(2, [128, nn], BF16, "ost")
                Ob = [(PS[6], "ps6"), (PS[7], "ps7")]
                for ch in range(2):
                    h = h0 + ch
                    S.dma("sp", kT[ch], job["kT"][6 + h, :, 0:Lp], writes=[f"sbk{ch}"])
                    S.dma("sp", vv[ch], job["v"][0:Lp, 256 + h * 128:256 + (h + 1) * 128].rearrange("(t p) n -> p t n", p=128), writes=[f"sbv{ch}"])

                def z_fn(ch, kt, zp, zpn):
                    S.op("pe", lambda e: e.matmul(zp[:, :n], lhsT=kT[ch][:, kt * 128:(kt + 1) * 128], rhs=Qsb[:, 8 + h0 + ch, c0:c0 + n], start=True, stop=True),
                         reads=[f"sbk{ch}", f"Q{8 + h0 + ch}"], writes=[zpn])

                def pv_fn(ch, kt, am, amn, first, last):
                    S.op("pe", lambda e: e.matmul(Ob[ch][0][:, :n], lhsT=vv[ch][:, kt, :], rhs=am[:, :n], start=first, stop=last),
                         reads=[f"sbv{ch}", amn], writes=[Ob[ch][1]])

                def mask_fn(kt, m, mn):
                    S.op("pool", lambda e: e.tensor_scalar(out=m[:, :n], in0=qpos_bc[:, c0:c0 + n], scalar1=kp[:, kt:kt + 1], scalar2=None,
                                                           op0=ALU.is_gt), reads=["qpos_bc", "kpos_seq", "kpos_smp"], writes=[mn])

                sb_steps(n, nkt, 2, lambda kt: None, z_fn, pv_fn, mask_fn, Bj)
                for ch in range(2):
                    h = h0 + ch
                    o, on = ost.next()
                    S.op("act", lambda e: e.activation(out=o[:, :n], in_=Ob[ch][0][:, :n], func=AF.Identity), reads=[Ob[ch][1]], writes=[on])
                    S.dma("sp", oT_d[8 + h, :, c0:c0 + n], o[:, :n], reads=[on])

        def sb_samples():
            S.barrier()
            Bj = Bump(nc, [list(r) for r in mark])
            n, nkt = 32, 65
            kp = tabs["kpos_smp"]
            qrow = Bj.alloc([128, 4, 4, 4], F32)
            S.op("dve", lambda e: e.tensor_copy(out=qrow, in_=qpos_bc[:, 1028:1044].rearrange("p (s q) -> p s q", s=4).unsqueeze(2).to_broadcast([128, 4, 4, 4])),
                 reads=["qpos_bc"], writes=["qrow"])
            qrow2 = qrow.rearrange("p s h q -> p (s h q)")
            ktr = Bj.rot(3, [128, 4, 4, 128], BF16, "sbkt")
            vtr = Bj.rot(3, [128, 4, 512], BF16, "sbvt")
            cur = {}
            Obs = [(PS[6], "ps6"), (PS[7], "ps7")]

            def load_fn(kt):
                kt_, ktn = ktr.next()
                vt_, vtn = vtr.next()
                for s_ in range(4):
                    S.dma("sp", kt_[:, s_], kTs_d[s_, 6:10, :, kt * 128:(kt + 1) * 128].rearrange("c p n -> p c n"), writes=[ktn + f"_{s_}"])
                S.dma("sp", vt_, vs_d[:, kt * 128:(kt + 1) * 128, 256:768].rearrange("s p n -> p s n"), writes=[vtn])
                cur[kt] = (kt_, ktn, vt_, vtn)

            def z_fn(ch, kt, zp, zpn):
                kt_, ktn, _, _ = cur[kt]
                for s in (2 * ch, 2 * ch + 1):
                    for h in range(4):
                        c = ((s - 2 * ch) * 4 + h) * 4
                        S.op("pe", lambda e: e.matmul(zp[:, c:c + 4], lhsT=kt_[:, s, h, :], rhs=Qsb[:, 8 + h, 1028 + 4 * s:1032 + 4 * s], start=True, stop=True),
                             reads=[ktn + f"_{s}", f"Q{8 + h}"], writes=[zpn])

            def pv_fn(ch, kt, am, amn, first, last):
                _, _, vt_, vtn = cur[kt]
                if ch == 1:
                    cur.pop(kt)
                Ob, On = Obs[ch]
                k = 0
                for s in (2 * ch, 2 * ch + 1):
                    for h in range(4):
                        c = ((s - 2 * ch) * 4 + h) * 4
                        S.op("pe", lambda e: e.matmul(Ob[:, c:c + 4], lhsT=vt_[:, s, h * 128:(h + 1) * 128], rhs=am[:, c:c + 4],
                                                      start=(first and k == 0), stop=last), reads=[vtn, amn], writes=[On])
                        k += 1

            def mask_fn(kt, m, mn):
                S.op("pool", lambda e: e.tensor_scalar(out=m[:, :n], in0=qrow2[:, 0:32], scalar1=kp[:, kt:kt + 1], scalar2=None, op0=ALU.is_gt),
                     reads=["qrow", "kpos_smp"], writes=[mn])

            sb_steps(n, nkt, 2, load_fn, z_fn, pv_fn, mask_fn, Bj)
            o = Bj.alloc([128, 4, 4, 4], BF16)
            o2 = o.rearrange("p s h q -> p (s h q)")
            for ch in range(2):
                S.op("act", lambda e: e.activation(out=o2[:, 32 * ch:32 * ch + 32], in_=Obs[ch][0][:, :32], func=AF.Identity), reads=[Obs[ch][1]], writes=["sbo"])
            for h in range(4):
                S.dma("sp", oT_d[8 + h, :, 1028:1044].rearrange("p (s q) -> p s q", s=4), o[:, :, h, :], reads=["sbo"])

        def gelu_tanh(t, out_bf, x, xn, n, onm):
            S.op("act", lambda e: e.activation(out=t, in_=x, func=AF.Square), reads=[xn], writes=["gl_t"])
            S.op("dve", lambda e: e.tensor_scalar(out=t, in0=t, scalar1=0.044715, scalar2=1.0, op0=ALU.mult, op1=ALU.add), reads=["gl_t"], writes=["gl_t"])
            S.op("dve", lambda e: e.tensor_tensor(out=t, in0=t, in1=x, op=ALU.mult), reads=["gl_t", xn], writes=["gl_t"])
            S.op("act", lambda e: e.activation(out=t, in_=t, func=AF.Sigmoid, scale=GELU_C), reads=["gl_t"], writes=["gl_t"])
            S.op("dve", lambda e: e.tensor_tensor(out=out_bf, in0=t, in1=x, op=ALU.mult), reads=["gl_t", xn], writes=[onm])

        def nsa_branch(job):
            S.barrier()
            Bj = Bump(nc, [list(r) for r in mark])
            n, c0, nkt = job["n"], job["c0"], job["nkt"]
            nn = max(n, 8)
            cols = (c0, n)
            ncmp, nblk, L = job["n_cmp"], job["n_blk"], job["L"]
            nct = (ncmp + 127) // 128
            ncp = nct * 128
            kp = kpos_tab(job)
            pools = std_pools(Bj, n)
            small = n <= 8
            selp = Bj.alloc([8, 2, 132], F32)
            if small:
                S.op("pool", lambda e: e.memset(selp, 0.0), writes=["selp"])
            cend_t = tabs["cend_seq"] if job["kpos"] == "seq" else tabs["cend"]
            marks = [list(r) for r in Bj.ranges]
            for g in range(2):
                S.barrier()
                Bg = Bump(nc, [list(r) for r in marks])
                heads = [4 * g + i for i in range(4)]
                kcT = Bg.alloc([128, L], BF16)
                shared = job["kpos"] == "seq"
                if shared:
                    kcmpT, vcmp = cmp_cache[g]["kT"], cmp_cache[g]["v"]
                else:
                    kcmpT = Bg.alloc([128, ncp], BF16)
                    vcmp = Bg.alloc([128, nct, 128], BF16)
                do_cmp = (not shared) or (not cmp_cache[g]["done"])
                if shared:
                    cmp_cache[g]["done"] = True
                if do_cmp:
                    S.op("pool", lambda e: e.memset(kcmpT, 0.0), writes=["kcmpT"])
                    S.op("pool", lambda e: e.memset(vcmp, 0.0), writes=["vcmp"])
                hid32 = Bg.alloc([128, ncp], F32)
                hidb = Bg.alloc([128, ncp], BF16)
                cvec = Bg.alloc([128, 1], F32)
                glt = Bg.alloc([128, ncp], F32)
                S.op("pool", lambda e: e.memset(hid32, 0.0), writes=["hid32"])
                for kv in (range(2) if do_cmp else ()):
                    S.dma("sp", kcT, job["kT"][2 * kv + g, :, 0:L], writes=["kcT"])
                    ps, pn = psrot4.next()
                    for j in range(32):
                        S.op("pe", lambda e, j=j: e.matmul(ps[:, 0:1], lhsT=w1[:, kv, j, :], rhs=pe[:, kv, j:j + 1], start=(j == 0), stop=(j == 31)),
                             reads=[f"w1_{kv}", f"pe_{kv}"], writes=[pn])
                    S.op("act", lambda e: e.activation(out=cvec, in_=ps[:, 0:1], func=AF.Identity), reads=[pn], writes=["cvec"])
                    for c1 in range(0, ncmp, 512):
                        cn = min(512, ncmp - c1)
                        ps, pn = psrot4.next()
                        for j in range(32):
                            S.op("pe", lambda e, j=j: e.matmul(ps[:, :cn], lhsT=w1[:, kv, j, :],
                                                               rhs=kcT[:, 16 * c1 + j:16 * c1 + j + 16 * (cn - 1) + 1:16],
                                                               start=(j == 0), stop=(j == 31)), reads=[f"w1_{kv}", "kcT"], writes=[pn])
                        S.op("act", lambda e: e.activation(out=hid32[:, c1:c1 + cn], in_=ps[:, :cn], func=AF.Identity, bias=cvec[:, 0:1], scale=1.0),
                             reads=[pn, "cvec"], writes=["hid32"])
                    gelu_tanh(glt, hidb, hid32, "hid32", ncp, "hidb")
                    if kv == 0:
                        for c1 in range(0, ncmp, 512):
                            cn = min(512, ncmp - c1)
                            ps, pn = psrot4.next()
                            S.op("pe", lambda e: e.matmul(ps[:, :cn], lhsT=w2[:, 0, :], rhs=hidb[:, c1:c1 + cn], start=True, stop=True),
                                 reads=["w2_0", "hidb"], writes=[pn])
                            S.op("act", lambda e: e.activation(out=kcmpT[:, c1:c1 + cn], in_=ps[:, :cn], func=AF.Identity), reads=[pn], writes=["kcmpT"])
                    else:
                        for ct in range(nct):
                            cn = min(128, ncmp - ct * 128)
                            ps, pn = psrot4.next()
                            S.op("pe", lambda e: e.matmul(ps[:cn, :128], lhsT=hidb[:, ct * 128:ct * 128 + cn], rhs=w2[:, 1, :], start=True, stop=True),
                                 reads=["w2_1", "hidb"], writes=[pn])
                            S.op("act", lambda e: e.activation(out=vcmp[:cn, ct, :], in_=ps[:cn, :128], func=AF.Identity), reads=[pn], writes=["vcmp"])
                bt_c = bias_table(Bg, job, cend_t, nct, heads, "c")
                pn32 = Bg.alloc([128, nct, 4, nn], F32)
                cmask = Bg.alloc([128, nct, nn], F32)
                for ct in range(nct):
                    S.op("dve", lambda e: e.tensor_scalar(out=cmask[:, ct, :n], in0=qpos_bc[:, c0:c0 + n], scalar1=cend_t[:, ct:ct + 1],
                                                          scalar2=None, op0=ALU.is_ge), reads=["qpos_bc", "cend", "cend_seq"], writes=["cmask"])
                pf32r = Bg.rot(2, [128, nn], F32, "pf32")
                pmb = Bg.rot(3, [128, nn], BF16, "pmb")
                for hi, h in enumerate(heads):
                    (Ob, On), (Db, Dn) = pools["acc"].next()
                    for ct in range(nct):
                        ps, pn = psrot4.next()
                        S.op("pe", lambda e: e.matmul(ps[:, :n], lhsT=kcmpT[:, ct * 128:(ct + 1) * 128], rhs=Qsb[:, h, c0:c0 + n], start=True, stop=True),
                             reads=["kcmpT", f"Q{h}"], writes=[pn])
                        pf, pfn = pf32r.next()
                        for (a0, an, ai) in job["asubs"]:
                            bt, brn = bt_c[(ai, h)]
                            S.op("act", lambda e: e.activation(out=pf[:, a0:a0 + an], in_=ps[:, a0:a0 + an], func=AF.Exp, scale=SCALE,
                                                               bias=bt[:, ct:ct + 1]), reads=[pn, brn], writes=[pfn])
                        S.op("dve", lambda e: e.tensor_tensor(out=pn32[:, ct, hi, :n], in0=pf[:, :n], in1=cmask[:, ct, :n], op=ALU.mult),
                             reads=[pfn, "cmask"], writes=[f"pn32_{hi}"])
                        pb, pbn = pmb.next()
                        S.op("pool", lambda e: e.tensor_copy(out=pb[:, :n], in_=pn32[:, ct, hi, :n]), reads=[f"pn32_{hi}"], writes=[pbn])
                        S.op("pe", lambda e: e.matmul(Ob[:, :n], lhsT=vcmp[:, ct, :], rhs=pb[:, :n], start=(ct == 0), stop=(ct == nct - 1)),
                             reads=["vcmp", pbn], writes=[On])
                        S.op("pe", lambda e: e.matmul(Db[:, :n], lhsT=ones_bf, rhs=pb[:, :n], start=(ct == 0), stop=(ct == nct - 1)),
                             reads=["ones_bf", pbn], writes=[Dn])
                    t, tn_, rd, rdn = normalize(Bg, cols, (Ob, On), (Db, Dn), pools)
                    gated_acc(cols, h, 0, t, tn_, first=True)
                    for ct in range(nct):
                        S.op("dve", lambda e: e.tensor_tensor(out=pn32[:, ct, hi, :n], in0=pn32[:, ct, hi, :n], in1=rd[:, :n], op=ALU.mult),
                             reads=[f"pn32_{hi}", rdn], writes=[f"pn32_{hi}"])
                selTsb = Bg.alloc([128, 2, nn], BF16)
                S.op("pool", lambda e: e.memset(selTsb, 0.0), writes=["selTsb"])
                for (b0, bn, bi) in job["sblks"]:
                    ps, pn = psrot4.next()
                    k = 0
                    for hi in range(4):
                        for ct in range(nct):
                            S.op("pe", lambda e, k=k: e.matmul(ps[:bn, :nblk], lhsT=pn32[:, ct, hi, b0:b0 + bn], rhs=tabs["ov"][:, ct, :nblk],
                                                               start=(k == 0), stop=(k == 4 * nct - 1)), reads=[f"pn32_{hi}", "ov"], writes=[pn])
                            k += 1
                    sc = Bg.alloc([128, nblk], F32)
                    vis = Bg.alloc([128, nblk], F32)
                    ff = Bg.alloc([128, nblk], F32)
                    f2 = Bg.alloc([128, nblk], F32)
                    m8 = Bg.alloc([128, 8], F32)
                    sc2 = Bg.alloc([128, nblk], F32)
                    qpc = tabs["qposT"][:bn, bi:bi + 1]
                    cur = tabs["curT"][:bn, bi:bi + 1]
                    bs, bj = tabs["blk"][:bn, 0, :nblk], tabs["blk"][:bn, 1, :nblk]
                    S.op("dve", lambda e: e.tensor_scalar(out=vis[:bn], in0=bs, scalar1=qpc, scalar2=None, op0=ALU.is_le), reads=["blk", "qposT"], writes=["vis"])
                    S.op("dve", lambda e: e.tensor_tensor(out=sc[:bn], in0=ps[:bn, :nblk], in1=vis[:bn], op=ALU.mult), reads=[pn, "vis"], writes=["sc"])
                    S.op("dve", lambda e: e.tensor_scalar(out=vis[:bn], in0=vis[:bn], scalar1=-1.0, scalar2=None, op0=ALU.add), reads=["vis"], writes=["vis"])
                    S.op("dve", lambda e: e.tensor_tensor(out=sc[:bn], in0=sc[:bn], in1=vis[:bn], op=ALU.add), reads=["sc", "vis"], writes=["sc"])
                    S.op("dve", lambda e: e.tensor_scalar(out=ff[:bn], in0=bj, scalar1=cur, scalar2=-1.0, op0=ALU.subtract, op1=ALU.is_ge),
                         reads=["blk", "curT"], writes=["ff"])
                    S.op("dve", lambda e: e.tensor_scalar(out=f2[:bn], in0=bj, scalar1=cur, scalar2=0.0, op0=ALU.subtract, op1=ALU.is_le),
                         reads=["blk", "curT"], writes=["f2"])
                    S.op("dve", lambda e: e.tensor_tensor(out=ff[:bn], in0=ff[:bn], in1=f2[:bn], op=ALU.mult), reads=["ff", "f2"], writes=["ff"])
                    S.op("dve", lambda e: e.memset(ff[:bn, 0:1], 1.0), reads=["ff"], writes=["ff"])
                    S.op("dve", lambda e: e.scalar_tensor_tensor(out=sc[:bn], in0=ff[:bn], scalar=2e9, in1=sc[:bn], op0=ALU.mult, op1=ALU.add),
                         reads=["ff", "sc"], writes=["sc"])
                    S.op("dve", lambda e: e.max(out=m8[:bn], in_=sc[:bn]), reads=["sc"], writes=["m8"])
                    S.op("dve", lambda e: e.match_replace(out=sc2[:bn], in_to_replace=m8[:bn], in_values=sc[:bn], imm_value=-3e9),
                         reads=["sc", "m8"], writes=["sc2"])
                    S.op("dve", lambda e: e.max(out=m8[:bn], in_=sc2[:bn]), reads=["sc2"], writes=["m8"])
                    S.op("dve", lambda e: e.tensor_scalar(out=sc2[:bn], in0=sc[:bn], scalar1=m8[:bn, 7:8], scalar2=None, op0=ALU.is_ge),
                         reads=["sc", "m8"], writes=["sc2"])
                    S.op("dve", lambda e: e.tensor_scalar(out=sc[:bn], in0=sc[:bn], scalar1=0.0, scalar2=None, op0=ALU.is_ge), reads=["sc"], writes=["sc"])
                    S.op("dve", lambda e: e.tensor_tensor(out=sc[:bn], in0=sc[:bn], in1=sc2[:bn], op=ALU.mult), reads=["sc", "sc2"], writes=["sc"])
                    if small:
                        S.op("dve", lambda e: e.tensor_copy(out=selp[:bn, g, :nblk], in_=sc[:bn]), reads=["sc", "selp"], writes=["selp"])
                        continue
                    for jt in range((nblk + 127) // 128):
                        jn = min(128, nblk - jt * 128)
                        pt_, ptn = psrot4.next()
                        S.op("pe", lambda e: e.transpose(out=pt_[:jn, :bn], in_=sc[:bn, jt * 128:jt * 128 + jn], identity=ident[:bn, :bn]),
                             reads=["sc", "ident"], writes=[ptn])
                        S.op("act", lambda e: e.activation(out=selTsb[:jn, jt, b0:b0 + bn], in_=pt_[:jn, :bn], func=AF.Identity),
                             reads=[ptn], writes=["selTsb"])
                if not small:
                    S.dma("sp", selT_d[job["ji"], g, 0:min(128, nblk), 0:n], selTsb[:min(128, nblk), 0, :n], reads=["selTsb"], writes=["selT_d"])
                if nblk > 128 and not small:
                    S.dma("sp", selT_d[job["ji"], g, 128:nblk, 0:n], selTsb[:nblk - 128, 1, :n], reads=["selTsb"], writes=["selT_d"])
                S.barrier()
                Bs = Bump(nc, [list(r) for r in marks])
                Lp = nkt * 128
                kT = Bs.alloc([128, Lp], BF16)
                vv = Bs.alloc([128, nkt, 128], BF16)
                S.dma("sp", kT, job["kT"][4 + g, :, 0:Lp], writes=["slck"])
                S.dma("sp", vv, job["v"][0:Lp, g * 128:(g + 1) * 128].rearrange("(t p) n -> p t n", p=128), writes=["slcv"])
                masks = Bs.alloc([128, nkt, nn], BF16)
                selb_r = Bs.rot(3, [128, nn], BF16, "selb")
                cm_r = Bs.rot(3, [128, nn], BF16, "cm")
                if small:
                    rhsd = Bs.alloc([8, 2, nkt, n], F32)
                    for a in range(2):
                        S.op("dve", lambda e: e.tensor_tensor(out=rhsd[:n, a], in0=selp[:n, g, a:a + 2 * nkt:2].unsqueeze(2).to_broadcast([n, nkt, n]),
                                                              in1=ident[:n, :n].unsqueeze(1).to_broadcast([n, nkt, n]), op=ALU.mult),
                             reads=["selp", "ident"], writes=["rhsd"])
                    mps, mpn = psrot4.next()
                    for a in range(2):
                        S.op("pe", lambda e: e.matmul(mps[:, :nkt * n], lhsT=halfones[:n, a, :], rhs=rhsd[:n, a].rearrange("p k q -> p (k q)"),
                                                      start=(a == 0), stop=(a == 1)), reads=["halfones", "rhsd"], writes=[mpn])
                    cmf = Bs.alloc([128, nkt, n], F32)
                    for q in range(n):
                        S.op("dve", lambda e: e.tensor_scalar(out=cmf[:, :, q], in0=kp[:, :nkt], scalar1=qpos_bc[:, c0 + q:c0 + q + 1], scalar2=None,
                                                              op0=ALU.is_le), reads=["qpos_bc", "kpos_seq", "kpos_smp"], writes=["cmf"])
                    S.op("dve", lambda e: e.tensor_tensor(out=masks[:, :, :n], in0=mps[:, :nkt * n].rearrange("p (k q) -> p k q", q=n), in1=cmf, op=ALU.mult),
                         reads=[mpn, "cmf"], writes=["masks_all"])
                for kt in (range(nkt) if not small else ()):
                    sb_, sbn = selb_r.next()
                    r0_ = min(2 * kt, nblk - 1)
                    r1_ = min(2 * kt + 1, nblk - 1)
                    S.dma("sp", sb_[0:64, :n], selT_d[job["ji"], g, r0_:r0_ + 1, 0:n].partition_broadcast(64), reads=["selT_d"], writes=[sbn + "a"])
                    S.dma("sp", sb_[64:128, :n], selT_d[job["ji"], g, r1_:r1_ + 1, 0:n].partition_broadcast(64), reads=["selT_d"], writes=[sbn + "b"])
                    cm, cmn = cm_r.next()
                    S.op("dve", lambda e: e.tensor_scalar(out=cm[:, :n], in0=qpos_bc[:, c0:c0 + n], scalar1=kp[:, kt:kt + 1], scalar2=None, op0=ALU.is_ge),
                         reads=["qpos_bc", "kpos_seq", "kpos_smp"], writes=[cmn])
                    S.op("dve", lambda e: e.tensor_tensor(out=masks[:, kt, :n], in0=cm[:, :n], in1=sb_[:, :n], op=ALU.mult),
                         reads=[cmn, sbn + "a", sbn + "b"], writes=[f"mask{kt}"])
                if small:
                    small_group(Bs, job, cols, g, 1, kT, "slck", lambda kt: vv[:, kt, :], "slcv", nkt, kp[:, :nkt], masks, "masks_all")
                bt_s = bias_table(Bs, job, kp, nkt, heads, "s") if not small else None
                pools_s = std_pools(Bs, n)
                for h in (heads if not small else ()):
                    O, Dn = softmax_head(Bs, job, cols, h, kT, "slck", lambda kt: vv[:, kt, :], "slcv", nkt,
                                         lambda ai, h=h: bt_s[(ai, h)], lambda kt: (masks[:, kt, :n], f"mask{kt}"), pools_s, head_slope_idx=h)
                    t, tn_, _, _ = normalize(Bs, cols, O, Dn, pools_s)
                    gated_acc(cols, h, 1, t, tn_, first=False)
                wpt = tabs["wpos_seq"] if job["wposk"] == "seq" else tabs["wpos_smp"]
                for (w0, wn, wt0, nwt) in job["win"]:
                    if small:
                        Bw = Bs
                    else:
                        S.barrier()
                        Bw = Bump(nc, [list(r) for r in marks])
                    wcols = (c0 + w0, wn)
                    wnn = max(wn, 8)
                    kT = Bw.alloc([128, nwt * 128], BF16)
                    vv = Bw.alloc([128, nwt, 128], BF16)
                    S.dma("sp", kT, job["wkT"][g, :, wt0 * 128:(wt0 + nwt) * 128], writes=["wink"])
                    S.dma("sp", vv, job["wv"][wt0 * 128:(wt0 + nwt) * 128, g * 128:(g + 1) * 128].rearrange("(t p) n -> p t n", p=128), writes=["winv"])
                    wmask = Bw.alloc([128, nwt, wnn], BF16)
                    d1 = Bw.rot(2, [128, wnn], F32, "wd1")
                    d2 = Bw.rot(2, [128, wnn], F32, "wd2")
                    for kt in range(nwt):
                        a, an_ = d1.next()
                        b, bn_ = d2.next()
                        kpc = wpt[:, wt0 + kt:wt0 + kt + 1]
                        S.op("dve", lambda e: e.tensor_scalar(out=a[:, :wn], in0=qpos_bc[:, c0 + w0:c0 + w0 + wn], scalar1=kpc, scalar2=0.0,
                                                              op0=ALU.subtract, op1=ALU.is_ge), reads=["qpos_bc", "wpos_seq", "wpos_smp"], writes=[an_])
                        S.op("dve", lambda e: e.tensor_scalar(out=b[:, :wn], in0=qpos_bc[:, c0 + w0:c0 + w0 + wn], scalar1=kpc, scalar2=512.0,
                                                              op0=ALU.subtract, op1=ALU.is_lt), reads=["qpos_bc", "wpos_seq", "wpos_smp"], writes=[bn_])
                        S.op("dve", lambda e: e.tensor_tensor(out=wmask[:, kt, :wn], in0=a[:, :wn], in1=b[:, :wn], op=ALU.mult),
                             reads=[an_, bn_], writes=[f"wmask{kt}", "wmask_all"])
                    wjob = job
                    if small:
                        small_group(Bw, job, wcols, g, 2, kT, "wink", lambda kt: vv[:, kt, :], "winv", nwt, wpt[:, wt0:wt0 + nwt], wmask, "wmask_all")
                        continue
                    bt_w = bias_table(Bw, job, wpt[:, wt0:wt0 + nwt], nwt, heads, "w")
                    pools_w = std_pools(Bw, wn)
                    for h in heads:
                        O, Dn = softmax_head(Bw, wjob, wcols, h, kT, "wink", lambda kt: vv[:, kt, :], "winv", nwt,
                                             lambda ai, h=h: bt_w[(ai, h)], lambda kt: (wmask[:, kt, :wn], f"wmask{kt}"), pools_w, head_slope_idx=h)
                        t, tn_, _, _ = normalize(Bw, wcols, O, Dn, pools_w)
                        gated_acc(wcols, h, 2, t, tn_, first=False)
            if small:
                ost = Bs.rot(2, [128, nn], BF16, "nost")
            else:
                S.barrier()
                ost = Bump(nc, [list(r) for r in marks]).rot(2, [128, nn], BF16, "nost")
            for h in range(8):
                o, on = ost.next()
                S.op("act", lambda e: e.activation(out=o[:, :n], in_=oacc[:, h, c0:c0 + n], func=AF.Identity), reads=[f"oacc{h}"], writes=[on])
                S.dma("sp", oT_d[h, :, c0:c0 + n], o[:, :n], reads=[on])

        for job in mkjobs():
            if job_names is not None and job["name"] not in job_names:
                continue
            if "mem" in P3_BRANCHES:
                mem_branch(job)
            if "sb" in P3_BRANCHES and job["kpos"] == "seq":
                sb_branch(job)
            if "nsa" in P3_BRANCHES:
                nsa_branch(job)
        if "sb" in P3_BRANCHES and (job_names is None or any(nm.startswith("S") for nm in job_names)):
            sb_samples()
        S.barrier()

    if "p1" in phases:
        phase1()
    if "p2" in phases:
        phase2()
    if "p3z" in phases:
        phase3_zero()
    if "p3" in phases:
        if job_names is None or any(n.startswith("S") for n in job_names):
            phase3_prepare()
        phase3_attention()
    if "p4" in phases:
        phase4()
    S.finish()
    return nc, S


def fm(vec, n):
    return np.ascontiguousarray(np.asarray(vec, np.float32).reshape(n, 128).T)


def host_shared(inp):
    m = {}
    m["w_in"] = inp["w_in"][0]
    b_in = inp["b_in"][0]
    bfm = np.zeros((128, NCH), np.float32)
    for ci, (col0, ncols, kind, idx) in enumerate(CHUNKS):
        bfm[:ncols, ci] = b_in[col0:col0 + ncols]
    m["b_in_fm"] = bfm
    bv = np.concatenate([b_in[1024 + 768:1024 + 1024], b_in[1024 + 1536:1024 + 2048]])
    m["b_v_bc"] = np.ascontiguousarray(np.broadcast_to(bv[None, :], (128, 768)))
    m["b_wv_bc"] = np.ascontiguousarray(np.broadcast_to(b_in[None, 3328:3584], (128, 256)))
    m["w_mem"] = inp["w_mem_kv"][0]
    bm = inp["b_mem_kv"][0]
    m["b_mem_fm"] = fm(bm, 8)
    m["b_mem_bc"] = np.ascontiguousarray(np.broadcast_to(bm[None, :], (128, 1024)))
    m["w_br"] = np.ascontiguousarray(np.concatenate([inp["w_br_nsa"][0], inp["w_br_sb"][0], inp["w_br_mem"][0]], axis=0))
    m["w_o"] = inp["w_o"][0]
    m["w_up"] = inp["w_up"][0]
    m["w_down"] = inp["w_down"][0]
    m["vec_fm"] = np.ascontiguousarray(np.stack([fm(inp[k][0], 16) for k in ("ln1_g", "ln1_b", "ln2_g", "ln2_b", "b_down")], axis=1))
    m["bup_fm"] = fm(inp["b_up"][0], 88)
    m["bconv_fm"] = fm(inp["b_conv"][0], 88)
    m["wconv_fm"] = np.ascontiguousarray(np.stack([fm(inp["w_conv"][0][k], 88) for k in range(3)], axis=1))
    p = np.arange(128, dtype=np.float32)
    m["pool"] = inp["cache_kv_pages"][0].reshape(-1, 2048)
    m["rowid"] = p.reshape(128, 1).copy()
    m["ident"] = np.eye(128, dtype=np.float32)
    m["uinc"] = (p[:, None] >= p[None, :]).astype(np.float32)
    m["kpos_seq"] = (128.0 * np.arange(32)[None, :] + p[:, None]).astype(np.float32)
    ks = (128.0 * np.arange(65)[None, :] + p[:, None]).astype(np.float32)
    ks[:, 64] = np.where(p < 4, 8192.0 + p, 1e9)
    m["kpos_smp"] = ks
    ws = (7680.0 + 128.0 * np.arange(5)[None, :] + p[:, None]).astype(np.float32)
    ws[:, 4] = np.where(p < 4, 8192.0 + p, 1e9)
    m["wpos_smp"] = ws
    nidx = 128.0 * np.arange(4)[None, :] + p[:, None]
    m["cend"] = np.where(nidx < 511, 16.0 * nidx + 31.0, 1e9).astype(np.float32)
    m["cend_seq"] = np.where(nidx[:, :2] < 255, 16.0 * nidx[:, :2] + 31.0, 1e9).astype(np.float32)
    j = np.arange(129, dtype=np.float32)
    m["blk"] = np.ascontiguousarray(np.broadcast_to(np.stack([64.0 * j, j])[None], (128, 2, 129))).astype(np.float32)
    n_ = nidx[:, :, None]
    m["ov"] = ((16.0 * n_ < 64.0 * (j[None, None, :] + 1)) & (16.0 * n_ + 32.0 > 64.0 * j[None, None, :])).astype(np.float32)
    m["w1c"] = np.ascontiguousarray(np.stack([inp["w_cmp_k1"][0].transpose(1, 0, 2), inp["w_cmp_v1"][0].transpose(1, 0, 2)]))
    m["w2c"] = np.ascontiguousarray(np.stack([inp["w_cmp_k2"][0], inp["w_cmp_v2"][0]]))
    m["pec"] = np.ascontiguousarray(np.stack([inp["pe_cmp_k"][0].T, inp["pe_cmp_v"][0].T]))
    return m


def host_layout(inp, c, shared):
    b, j = c // 4, c % 4
    ch = [j, 7 - j]
    xp = inp["x_prompt"][b]
    xq = np.zeros((D, NQ), np.float32)
    xw = np.zeros((D, 2560), np.float32)
    wpos = np.zeros((128, 20), np.float32)
    qpos = np.zeros((NQ,), np.float32)
    hvalid = np.zeros((128, 4), np.float32)
    for s in range(2):
        p0 = 512 * ch[s]
        xq[:, s * 512:(s + 1) * 512] = xp[p0:p0 + 512].T
        qpos[s * 512:(s + 1) * 512] = np.arange(p0, p0 + 512)
        for h in range(2):
            p = p0 - 2 + h
            qpos[1024 + 2 * s + h] = p
            if p >= 0:
                xq[:, 1024 + 2 * s + h] = xp[p]
                hvalid[:, 2 * s + h] = 1.0
        lo = p0 - 768
        for i in range(1280):
            pass
        a = max(lo, 0)
        xw[:, s * 1280 + (a - lo):(s + 1) * 1280] = xp[a:lo + 1280].T
        wp = lo + 128.0 * np.arange(10)[None, :] + np.arange(128)[:, None]
        wpos[:, 10 * s:10 * s + 10] = np.where(wp >= 0, wp, -1e9)
    xsmp = inp["x_sample"][4 * c:4 * c + 4].reshape(16, D)
    xq[:, 1028:1044] = xsmp.T
    qpos[1028:1044] = np.tile(8192 + np.arange(4), 4)
    m = dict(shared)
    m.update({"xq": xq, "xs": np.ascontiguousarray(xp.T), "xw": xw, "hvalid": hvalid, "wpos": wpos})
    m["qpos_bc"] = np.ascontiguousarray(np.broadcast_to(qpos[None, :], (128, NQ)))
    refs = np.zeros((14,), np.float32)
    qposT = np.zeros((128, 13), np.float32)
    for i in range(8):
        refs[i] = qpos[i * 128 + 127]
        qposT[:, i] = qpos[i * 128:(i + 1) * 128]
    refs[8], refs[9] = qpos[1025], qpos[1027]
    refs[10:14] = 8195.0
    qposT[0:4, 8] = qpos[1024:1028]
    for s in range(4):
        qposT[0:4, 9 + s] = 8192.0 + np.arange(4)
    m["refs"] = np.ascontiguousarray(np.broadcast_to(refs[None, :], (128, 14)))
    m["qposT"] = qposT
    m["curT"] = np.floor(qposT / 64.0).astype(np.float32)
    m["pt"] = np.ascontiguousarray(inp["page_table"][4 * c:4 * c + 4].reshape(1, 256).astype(np.int32))
    m["cwin"] = np.ascontiguousarray(inp["cache_win_kv"][0, 4 * c:4 * c + 4])
    m["cmem"] = np.ascontiguousarray(inp["cache_mem_kv"][0, 4 * c:4 * c + 4])
    m["memT"] = np.ascontiguousarray(inp["mem_prompt"][b].T)
    st = inp["state_ffn_conv"][0, 4 * c:4 * c + 4]
    m["state_fm"] = np.ascontiguousarray(st.reshape(4, 2, 88, 128).transpose(3, 2, 0, 1))
    return m


_CACHE = {}


def kernel(**inputs):
    inp = {k: np.asarray(v) for k, v in inputs.items()}
    if "nc" not in _CACHE:
        _CACHE["nc"] = build_program()
    nc, S = _CACHE["nc"]
    shared = host_shared(inp)
    in_maps = [host_layout(inp, c, shared) for c in range(8)]
    res = run_bass_kernel_spmd(nc, in_maps, core_ids=list(range(8)))
    return assemble(inp, res.results)


def assemble(inp, results):
    y_p = np.zeros((2, SEQ, D), np.float32)
    y_s = np.zeros((32, 4, D), np.float32)
    kv_p = np.zeros((1, 2, SEQ, 2048), np.float32)
    win_p = np.zeros((1, 2, 512, 512), np.float32)
    mem_p = np.zeros((1, 2, 256, 1024), np.float32)
    conv_p = np.zeros((1, 2, 2, 2 * D_FF), np.float32)
    kv_s = np.zeros((1, 32, 4, 2048), np.float32)
    win_s = np.zeros((1, 32, 512, 512), np.float32)
    conv_s = np.zeros((1, 32, 2, 2 * D_FF), np.float32)
    for c in range(8):
        b, j = c // 4, c % 4
        ch = [j, 7 - j]
        r = results[c]
        kvq = r["kvq"].reshape(20 * 128, NQ)
        yq = r["yq"].reshape(D, NQ)
        for s in range(2):
            p0 = 512 * ch[s]
            kv_p[0, b, p0:p0 + 512] = kvq[:2048, s * 512:(s + 1) * 512].T
            y_p[b, p0:p0 + 512] = yq[:, s * 512:(s + 1) * 512].T
            if ch[s] == 7:
                win_p[0, b] = kvq[2048:2560, s * 512:(s + 1) * 512].T
                conv_p[0, b] = r["convp_o"].transpose(2, 1, 0).reshape(2, 2 * D_FF)
        kv_s[0, 4 * c:4 * c + 4] = kvq[:2048, 1028:1044].T.reshape(4, 4, 2048)
        win_s[0, 4 * c:4 * c + 4] = r["wins_o"]
        y_s[4 * c:4 * c + 4] = yq[:, 1028:1044].T.reshape(4, 4, D)
        conv_s[0, 4 * c:4 * c + 4] = r["convs_o"].transpose(2, 3, 1, 0).reshape(4, 2, 2 * D_FF)
        if j == 0:
            mem_p[0, b] = r["memkv_o"]
    return (y_p, y_s, kv_p, win_p, mem_p, conv_p, kv_s, win_s, conv_s)
```
